# Optimizing a Trainium2 kernel written in Bass

```python
import jax, jax.numpy as jnp
from jax import lax
import numpy as np

D_MODEL = 1024
BATCH = 4
SEQ = 8192
DEPTH = 2

D_MIX = D_MODEL
SC_WIDTH = D_MIX // 4
SC_GROUPS = 4
SC_KERNEL = 3
GDN_WIDTH = D_MIX // 2
GDN_HEADS = 4
GDN_HEAD_DIM = GDN_WIDTH // GDN_HEADS
GDN_CONV = 4
GDN_CHUNK = 64
SB_WIDTH = D_MIX - SC_WIDTH - GDN_WIDTH
SB_HEADS = 4
SB_HEAD_DIM = SB_WIDTH // SB_HEADS
SB_BLOCK = 128
D_FF = 256 * ((8 * D_MODEL // 3 + 255) // 256)
FFN_CONV = 3
NORM_EPS = 1e-6
SPLIT_SIZES = (SC_WIDTH, SC_WIDTH, SC_WIDTH,
               GDN_WIDTH, GDN_WIDTH, GDN_WIDTH, GDN_WIDTH, GDN_HEADS, GDN_HEADS,
               SB_WIDTH, SB_WIDTH, SB_WIDTH)
D_IN_PROJ = 3 * SC_WIDTH + 4 * GDN_WIDTH + 2 * GDN_HEADS + 3 * SB_WIDTH

kernel_name = 'hymba_style_conv_gdn_stickbreaking_hybrid'


def rmsnorm(x, w):
    xf = x.astype(jnp.float32)
    y = xf * lax.rsqrt(jnp.mean(xf * xf, axis=-1, keepdims=True) + NORM_EPS) * w.astype(jnp.float32)
    return y.astype(x.dtype)


def l2norm(x):
    xf = x.astype(jnp.float32)
    return xf * lax.rsqrt(jnp.sum(xf * xf, axis=-1, keepdims=True) + NORM_EPS)


def causal_dwconv(x, w):
    K, C = w.shape
    return lax.conv_general_dilated(
        x, w[:, None, :].astype(x.dtype), window_strides=(1,), padding=[(K - 1, 0)],
        dimension_numbers=('NWC', 'WIO', 'NWC'), feature_group_count=C)


def split_columns(proj):
    points = [int(p) for p in np.cumsum(np.array(SPLIT_SIZES))[:-1]]
    return jnp.split(proj, points, axis=-1)


def gated_delta_rule_chunked(q, k, v, g, beta):
    f32 = jnp.float32
    Bsz, L, H, Dk = q.shape
    Dv = v.shape[-1]
    N = L // GDN_CHUNK

    def to_chunks(t):
        t = t.reshape((Bsz, N, GDN_CHUNK, H) + t.shape[3:])
        return jnp.moveaxis(t, 3, 1)

    q = to_chunks(q.astype(f32)) * (Dk ** -0.5)
    k = to_chunks(k.astype(f32))
    v = to_chunks(v.astype(f32))
    beta = to_chunks(beta.astype(f32))
    g = jnp.cumsum(to_chunks(g.astype(f32)), axis=-1)
    idx = jnp.arange(GDN_CHUNK)
    causal = idx[:, None] >= idx[None, :]
    strict = idx[:, None] > idx[None, :]
    decay = jnp.exp(jnp.where(causal, g[..., :, None] - g[..., None, :], -jnp.inf))
    k_beta = k * beta[..., None]
    lower = jnp.where(strict, jnp.einsum('bhncd,bhnsd->bhncs', k_beta, k) * decay, 0.0)
    lhs = lower + jnp.eye(GDN_CHUNK, dtype=f32)
    rhs = jnp.concatenate([v * beta[..., None], k_beta * jnp.exp(g)[..., None]], axis=-1)
    sol = lax.linalg.triangular_solve(lhs, rhs, left_side=True, lower=True, unit_diagonal=True)
    u, w = sol[..., :Dv], sol[..., Dv:]
    attn_intra = jnp.where(causal, jnp.einsum('bhncd,bhnsd->bhncs', q, k) * decay, 0.0)

    def step(S, inp):
        q_i, k_i, u_i, w_i, g_i, a_i = inp
        v_new = u_i - jnp.einsum('bhcd,bhde->bhce', w_i, S)
        o = jnp.einsum('bhcd,bhde->bhce', q_i * jnp.exp(g_i)[..., None], S) + \
            jnp.einsum('bhcs,bhse->bhce', a_i, v_new)
        g_last = g_i[..., -1]
        S = S * jnp.exp(g_last)[..., None, None] + \
            jnp.einsum('bhcd,bhce->bhde', k_i * jnp.exp(g_last[..., None] - g_i)[..., None], v_new)
        return S, o

    xs = tuple(jnp.moveaxis(t, 2, 0) for t in (q, k, u, w, g, attn_intra))
    S0 = jnp.zeros((Bsz, H, Dk, Dv), f32)
    _, outs = lax.scan(step, S0, xs)
    return outs.transpose(1, 0, 3, 2, 4).reshape(Bsz, L, H, Dv)


def stick_breaking_attention(q, k, v):
    L, D = q.shape[1], q.shape[-1]
    scale = D ** -0.5
    q_off = jnp.arange(SB_BLOCK)
    outs = []
    for blk in range(L // SB_BLOCK):
        start, end = blk * SB_BLOCK, (blk + 1) * SB_BLOCK
        z = jnp.einsum('bqhd,bkhd->bhqk', q[:, start:end], k[:, :end]).astype(jnp.float32) * scale
        strict = jnp.arange(end)[None, :] < (start + q_off)[:, None]
        log_beta = jax.nn.log_sigmoid(z)
        log_one_minus = jnp.where(strict, jax.nn.log_sigmoid(-z), 0.0)
        tail = lax.cumsum(log_one_minus, axis=3, reverse=True) - log_one_minus
        A = jnp.where(strict, jnp.exp(log_beta + tail), 0.0)
        outs.append(jnp.einsum('bhqk,bkhd->bqhd', A.astype(v.dtype), v[:, :end]))
    return jnp.concatenate(outs, axis=1)


def setup_inputs(seed: int = 0) -> dict:
    key = jax.random.key(seed)
    ks = jax.random.split(key, 16)
    f32 = jnp.float32
    nrm = lambda k, shape: jax.random.normal(k, shape, f32)
    dt = jnp.exp(jax.random.uniform(ks[6], (DEPTH, GDN_HEADS), f32, np.log(1e-3), np.log(1e-1)))
    return {
        'x': nrm(ks[0], (BATCH, SEQ, D_MODEL)),
        'w_norm_mix': 1.0 + 0.02 * nrm(ks[1], (DEPTH, D_MODEL)),
        'w_mix_in': nrm(ks[2], (DEPTH, D_MODEL, D_IN_PROJ)) * D_MODEL ** -0.5,
        'w_sconv': nrm(ks[3], (DEPTH, SC_KERNEL, SC_WIDTH)) * SC_KERNEL ** -0.5,
        'w_gdn_conv': nrm(ks[4], (DEPTH, GDN_CONV, 3 * GDN_WIDTH)) * GDN_CONV ** -0.5,
        'gdn_a_log': jnp.log(jax.random.uniform(ks[5], (DEPTH, GDN_HEADS), f32, 1.0, 16.0)),
        'gdn_dt_bias': dt + jnp.log(-jnp.expm1(-dt)),
        'w_gdn_norm': 1.0 + 0.02 * nrm(ks[7], (DEPTH, GDN_HEAD_DIM)),
        'w_mix_out': nrm(ks[8], (DEPTH, D_MIX, D_MODEL)) * D_MIX ** -0.5,
        'w_norm_ffn': 1.0 + 0.02 * nrm(ks[9], (DEPTH, D_MODEL)),
        'w_ffn_up': nrm(ks[10], (DEPTH, D_MODEL, 2 * D_FF)) * D_MODEL ** -0.5,
        'w_ffn_conv': nrm(ks[11], (DEPTH, FFN_CONV, 2 * D_FF)) * FFN_CONV ** -0.5,
        'w_ffn_down': nrm(ks[12], (DEPTH, D_FF, D_MODEL)) * D_FF ** -0.5,
        'w_norm_final': 1.0 + 0.02 * nrm(ks[13], (D_MODEL,)),
    }


def reference(x, w_norm_mix, w_mix_in, w_sconv, w_gdn_conv, gdn_a_log, gdn_dt_bias, w_gdn_norm,
              w_mix_out, w_norm_ffn, w_ffn_up, w_ffn_conv, w_ffn_down, w_norm_final):
    Bsz, L, _ = x.shape
    f32 = jnp.float32
    for l in range(DEPTH):
        h = rmsnorm(x, w_norm_mix[l])
        proj = h @ w_mix_in[l]
        (sc_b, sc_c, sc_h, gq, gk, gv, gz, ga, gb, sq, sk, sv) = split_columns(proj)

        y_sc = sc_b * causal_dwconv(sc_c * sc_h, w_sconv[l])

        qkv = jax.nn.silu(causal_dwconv(jnp.concatenate([gq, gk, gv], axis=-1), w_gdn_conv[l]))
        gq, gk, gv = jnp.split(qkv, 3, axis=-1)
        heads = lambda t: t.reshape(Bsz, L, GDN_HEADS, GDN_HEAD_DIM)
        beta = jax.nn.sigmoid(gb.astype(f32))
        g = -jnp.exp(gdn_a_log[l].astype(f32)) * jax.nn.softplus(ga.astype(f32) + gdn_dt_bias[l].astype(f32))
        o = gated_delta_rule_chunked(l2norm(heads(gq)), l2norm(heads(gk)), heads(gv), g, beta)
        o = rmsnorm(o, w_gdn_norm[l]) * jax.nn.silu(heads(gz).astype(f32))
        y_gdn = o.reshape(Bsz, L, GDN_WIDTH).astype(x.dtype)

        sb_heads = lambda t: t.reshape(Bsz, L, SB_HEADS, SB_HEAD_DIM)
        y_sb = stick_breaking_attention(sb_heads(sq), sb_heads(sk), sb_heads(sv)).reshape(Bsz, L, SB_WIDTH)

        x = x + jnp.concatenate([y_sc, y_gdn, y_sb], axis=-1) @ w_mix_out[l]

        h = rmsnorm(x, w_norm_ffn[l])
        u = causal_dwconv(h @ w_ffn_up[l], w_ffn_conv[l])
        gate, val = jnp.split(u, 2, axis=-1)
        x = x + (jax.nn.silu(gate) * val) @ w_ffn_down[l]
    return rmsnorm(x, w_norm_final)
```

```python
import numpy as np
from contextlib import ExitStack
import concourse.bass as bass
import concourse.mybir as mybir
from concourse.bass_utils import run_bass_kernel_spmd

F32 = mybir.dt.float32
BF16 = mybir.dt.bfloat16
F32R = mybir.dt.float32r
ALU = mybir.AluOpType
AF = mybir.ActivationFunctionType

D = 1024
DEPTH = 2
DFF = 2816
DIN = 3592
EPS = 1e-6
NEG = -60000.0
NPRM = 206
C_ID, C_UT, C_NTRI, C_MI, C_MS, C_MSL, C_SBM = 0, 128, 256, 384, 512, 640, 768
NCST = 768 + 2048

EPOCH = 20000


class Buf:
    __slots__ = ("t", "w", "r")

    def __init__(self, t):
        self.t = t
        self.w = None
        self.r = {}

    def __getitem__(self, k):
        return self.t[k]


class _Rec:
    def __getattr__(self, name):
        def f(*a, **kw):
            self.call = (name, a, kw)
            return self
        return f


class Sched:
    def __init__(self, nc, es):
        self.nc = nc
        self.es = es
        self.names = ["pe", "act", "dve", "pool", "sp"]
        self.prog = {e: [] for e in self.names}
        self.sem = {}
        self.val = {}
        self.cur = {}
        self.n = {e: 0 for e in self.names}
        self.seen = {e: {} for e in self.names}
        for e in self.names:
            self._new_epoch(e, 0)
        self.dq = {}
        self.dqi = {}
        for q in ("sp", "pool", "act"):
            keys = []
            for i in range(8):
                k = ("d", q, i)
                self.sem[k] = es.enter_context(nc.semaphore(f"d_{q}_{i}"))
                self.val[k] = 0
                keys.append(k)
            self.dq[q] = keys
            self.dqi[q] = 0
        self.nbuf = 0

    def _new_epoch(self, e, ep):
        k = ("e", e, ep)
        self.sem[k] = self.es.enter_context(self.nc.semaphore(f"s_{e}_{ep}"))
        self.val[k] = 0
        self.cur[e] = k

    def sb(self, shape, dt, name=None):
        self.nbuf += 1
        t = self.es.enter_context(self.nc.sbuf_tensor(f"{name or 'b'}_{self.nbuf}", list(shape), dt))
        return Buf(t)

    def _deps(self, reads, writes):
        deps = {}

        def add(ev):
            if ev is None:
                return
            k, v = ev
            if deps.get(k, 0) < v:
                deps[k] = v

        for b in reads:
            add(b.w)
        for b in writes:
            add(b.w)
            for k, v in b.r.items():
                add((k, v))
        return deps

    def _waits(self, e, deps):
        ws = []
        seen = self.seen[e]
        for k, v in deps.items():
            if seen.get(k, 0) < v:
                seen[k] = v
                ws.append((self.sem[k], v))
        return ws

    def _mark(self, ev, reads, writes):
        k, v = ev
        for b in reads:
            if b.r.get(k, 0) < v:
                b.r[k] = v
        for b in writes:
            b.w = ev
            b.r = {}

    def op(self, e, fn, reads=(), writes=()):
        ws = self._waits(e, self._deps(reads, writes))
        if self.val[self.cur[e]] >= EPOCH:
            self._new_epoch(e, self.cur[e][2] + 1)
        k = self.cur[e]
        self.val[k] += 1
        v = self.val[k]
        sem = self.sem[k]
        if e == "pe":
            self.seen[e][k] = v

        rec = _Rec()
        fn(rec)
        name, a, kw = rec.call

        def thunk(eng, ws=ws, name=name, a=a, kw=kw, sem=sem):
            for s, val in ws:
                eng.wait_ge(s, val)
            getattr(eng, name)(*a, **kw).then_inc(sem, 1)

        self.prog[e].append(thunk)
        self._mark((k, v), reads, writes)

    def dma(self, q, out, in_, reads=(), writes=()):
        deps = self._deps(reads, writes)
        i = self.dqi[q]
        self.dqi[q] = (i + 1) % len(self.dq[q])
        k = self.dq[q][i]
        if deps.get(k, 0) < self.val[k]:
            deps[k] = self.val[k]
        ws = self._waits(q, deps)
        self.val[k] += 16
        v = self.val[k]
        sem = self.sem[k]

        def thunk(eng, ws=ws, sem=sem, out=out, in_=in_):
            for s, val in ws:
                eng.wait_ge(s, val)
            eng.dma_start(out=out, in_=in_).then_inc(sem, 16)

        self.prog[q].append(thunk)
        self._mark((k, v), reads, writes)

    def barrier(self):
        allv = {k: v for k, v in self.val.items() if v > 0}
        for e in self.names:
            ws = self._waits(e, allv)
            if ws:
                def thunk(eng, ws=ws):
                    for s, val in ws:
                        eng.wait_ge(s, val)
                self.prog[e].append(thunk)

    def emit(self):
        nc = self.nc
        self.barrier()
        with nc.Block() as block:
            @block.tensor
            def _(eng):
                for th in self.prog["pe"]:
                    th(eng)

            @block.scalar
            def _(eng):
                for th in self.prog["act"]:
                    th(eng)

            @block.vector
            def _(eng):
                for th in self.prog["dve"]:
                    th(eng)

            @block.gpsimd
            def _(eng):
                for th in self.prog["pool"]:
                    th(eng)

            @block.sync
            def _(eng):
                for th in self.prog["sp"]:
                    th(eng)


def skew_emit(units):
    n = len(units)
    ns = max(len(u) for u in units) if units else 0
    for it in range(n + ns - 1):
        for st in range(ns):
            u = it - st
            if 0 <= u < n and st < len(units[u]) and units[u][st] is not None:
                units[u][st]()


def build(L, depth=DEPTH, phases="ABGC", dbg=(), ext_in=()):
    NT = L // 512
    nc = bass.Bass("TRN2", target_bir_lowering=False)
    es = ExitStack()
    dr = {}

    def dram(name, shape, dt, kind="Internal"):
        if name in dbg:
            kind = "ExternalOutput"
        if name in ext_in:
            kind = "ExternalInput"
        dr[name] = nc.dram_tensor(name, list(shape), dt, kind=kind)
        return dr[name]

    x_in = dram("x", [L, D], F32, "ExternalInput")
    w_in = dram("w_in", [depth, D, DIN], F32, "ExternalInput")
    w_out = dram("w_out", [depth, D, D], F32, "ExternalInput")
    w_up = dram("w_up", [depth, D, 2 * DFF], F32, "ExternalInput")
    w_dn = dram("w_dn", [depth, DFF, D], F32, "ExternalInput")
    prm = dram("prm", [depth, 128, NPRM], F32, "ExternalInput")
    wfin = dram("wfin", [1, D], F32, "ExternalInput")
    cst = dram("cst", [128, NCST], F32, "ExternalInput")
    out = dram("out", [L, D], F32, "ExternalOutput")
    xres = dram("xres", [L, D], F32)
    YT = dram("YT", [D, L], BF16)
    gq = dram("gq", [512, L], BF16)
    gk = dram("gk", [512, L], BF16)
    gz = dram("gz", [512, L], BF16)
    gktm = dram("gktm", [L, 512], BF16)
    gvtm = dram("gvtm", [L, 512], BF16)
    gab = dram("gab", [L, 12], F32)
    sq = dram("sq", [256, L], BF16)
    sk = dram("sk", [256, L], BF16)
    svtm = dram("svtm", [L, 256], BF16)
    if "dbg" in dbg:
        dram("dbg", [16, 128, 512], F32)

    with es:
        S = Sched(nc, es)
        cf = S.sb([128, C_SBM], F32, "cf")
        S.dma("sp", cf[:, :], cst[:, 0:C_SBM], writes=[cf])
        idb = S.sb([128, 128], BF16, "idb")
        S.op("dve", lambda e: e.tensor_copy(out=idb[:, :], in_=cf[:, C_ID:C_ID + 128]), [cf], [idb])
        onesb = S.sb([128, 128], BF16, "onesb")
        S.op("dve", lambda e: e.memset(onesb[:, :], 1.0), [], [onesb])
        onesf = S.sb([128, 128], F32, "onesf")
        S.op("dve", lambda e: e.memset(onesf[:, :], 1.0), [], [onesf])
        psum = [Buf(es.enter_context(nc.psum_tensor(f"ps{i}", [128, 512], F32))) for i in range(8)]
        prmall = S.sb([128, depth, NPRM], F32, "prm")
        for l in range(depth):
            S.dma("sp", prmall[:, l, :], prm[l, :, :], writes=[prmall])

        pctr = [0]

        def nextps(lo=0, hi=8):
            i = lo + pctr[0] % (hi - lo)
            pctr[0] += 1
            return psum[i]

        def rmsnorm_T(es2, xt, nb, wcol, l, hT, W, banks=(4, 8)):
            junk, ss, rstd, xn = W["junk"], W["ss"], W["rstd"], W["xn"]
            for b in range(nb):
                S.op("act", lambda e, b=b: e.activation(out=junk[:, :], in_=xt[:, b, :], func=AF.Square,
                                                         accum_out=ss[:, b:b + 1]), [xt], [junk, ss])
            S.op("act", lambda e: e.activation(out=rstd[:, 0:nb], in_=ss[:, 0:nb], func=AF.Ln,
                                               scale=1.0 / D, bias=W["eps"][:, 0:1]), [ss, W["eps"]], [rstd])
            S.op("act", lambda e: e.activation(out=rstd[:, 0:nb], in_=rstd[:, 0:nb], func=AF.Exp, scale=-0.5),
                 [rstd], [rstd])
            for b in range(nb):
                if b % 2 == 0:
                    S.op("dve", lambda e, b=b: e.tensor_scalar(out=xn[:, b, :], in0=xt[:, b, :], scalar1=rstd[:, b:b + 1],
                                                               scalar2=None, op0=ALU.mult), [xt, rstd], [xn])
                else:
                    S.op("act", lambda e, b=b: e.activation(out=xn[:, b, :], in_=xt[:, b, :], func=AF.Copy,
                                                            scale=rstd[:, b:b + 1]), [xt, rstd], [xn])
            n = nb * 128
            cpb = 1024 // n
            for c0 in range(0, 8, cpb):
                ps = nextps(*banks)
                pv = ps.t[:, :].bitcast(BF16)
                for ci in range(cpb):
                    c = c0 + ci
                    for b in range(nb):
                        S.op("pe", lambda e, c=c, b=b, ci=ci, pv=pv: e.transpose(
                            out=pv[:, ci * n + b * 128: ci * n + (b + 1) * 128],
                            in_=xn[:, b, c * 128:(c + 1) * 128], identity=idb[:, :]), [xn, idb], [ps])
                for ci in range(cpb):
                    c = c0 + ci
                    S.op("act", lambda e, c=c, ci=ci, pv=pv: e.activation(
                        out=hT[:, c, 0:n], in_=pv[:, ci * n:(ci + 1) * n], func=AF.Copy,
                        scale=prmall[:, l, wcol + c:wcol + c + 1]), [ps, prmall], [hT])

        def load_w_bf16(dst, src_ap, nchunk):
            for c in range(nchunk):
                S.dma("pool", dst[:, c, :], src_ap[c * 128:(c + 1) * 128, :], writes=[dst])

        for l in range(depth):
            xsrc = x_in if l == 0 else xres
            if "A" in phases:
                with ExitStack() as pa:
                    es_save = S.es
                    S.es = pa
                    phase_A(S, nc, L, l, xsrc, w_in, prmall, cf, idb, onesb, psum, nextps, rmsnorm_T,
                            load_w_bf16, dr)
                    S.barrier()
                    S.es = es_save
            if "B" in phases:
                with ExitStack() as pa:
                    es_save = S.es
                    S.es = pa
                    phase_SB(S, nc, L, l, cf, psum, nextps, dr)
                    S.barrier()
                    S.es = es_save
            if "G" in phases:
                with ExitStack() as pa:
                    es_save = S.es
                    S.es = pa
                    phase_GDN(S, nc, L, l, prmall, cf, idb, onesb, onesf, psum, nextps, dr)
                    S.barrier()
                    S.es = es_save
            if "C" in phases:
                with ExitStack() as pa:
                    es_save = S.es
                    S.es = pa
                    phase_C(S, nc, L, l, depth, xsrc, w_out, w_up, w_dn, wfin, prmall, idb, psum, nextps,
                            rmsnorm_T, load_w_bf16, dr)
                    S.barrier()
                    S.es = es_save
        S.emit()
    return nc


def phase_C(S, nc, L, l, depth, xsrc, w_out, w_up, w_dn, wfin, prmall, idb, psum, nextps, rmsnorm_T,
            load_w_bf16, dr):
    TT = 256
    nb = 2
    xres, YT, out = dr["xres"], dr["YT"], dr["out"]
    last = (l == depth - 1)
    Wo = S.sb([128, 8, D], BF16, "Wo")
    Wu = S.sb([128, 8, 2 * DFF], BF16, "Wu")
    Wd = S.sb([128, 22, D], BF16, "Wd")
    load_w_bf16(Wo, w_out[l], 8)
    load_w_bf16(Wu, w_up[l], 8)
    load_w_bf16(Wd, w_dn[l], 22)
    xt = S.sb([128, nb, D], F32, "xt")
    yT = S.sb([128, 8, TT], BF16, "yT")
    hT = S.sb([128, 8, TT], BF16, "hT")
    aT = S.sb([128, 22, TT], BF16, "aT")
    Wk = dict(junk=S.sb([128, D], BF16, "junk"), ss=S.sb([128, 4], F32, "ss"), rstd=S.sb([128, 4], F32, "rstd"),
              xn=S.sb([128, nb, D], BF16, "xn"), eps=S.sb([128, 1], F32, "eps"))
    S.op("dve", lambda e: e.memset(Wk["eps"][:, :], EPS), [], [Wk["eps"]])
    halo = S.sb([128, 44, 2], F32, "halo")
    S.op("pool", lambda e: e.memset(halo[:, :, :], 0.0), [], [halo])
    ub = [S.sb([128, TT + 2], F32, f"ub{i}") for i in range(4)]
    cv = [S.sb([128, TT], F32, f"cv{i}") for i in range(4)]
    tmp = [S.sb([128, TT], F32, f"tmp{i}") for i in range(4)]
    one = S.sb([128, 1], F32, "one")
    S.op("dve", lambda e: e.memset(one[:, :], 1.0), [], [one])
    wfb = None
    if last:
        wfb = S.sb([128, D], F32, "wfb")
        S.dma("sp", wfb[:, :], wfin[0:1, :].partition_broadcast(128), writes=[wfb])
    FC = 71
    k = 0
    def outproj_early(t):
        t0 = t * TT
        S.dma("sp", yT[:, :, :], YT[:, t0:t0 + TT].rearrange("(c p) n -> p c n", p=128), writes=[yT])
        for b in range(nb):
            for h in range(2):
                ps = psum[4 + b * 2 + h]
                for c in range(8):
                    S.op("pe", lambda e, ps=ps, c=c, b=b, h=h: e.matmul(
                        ps[:, :], lhsT=yT[:, c, b * 128:(b + 1) * 128], rhs=Wo[:, c, h * 512:(h + 1) * 512],
                        start=(c == 0), stop=(c == 7)), [yT, Wo], [ps])

    outproj_early(0)
    for t in range(L // TT):
        t0 = t * TT
        S.dma("sp", xt[:, :, :], xsrc[t0:t0 + TT, :].rearrange("(b p) d -> p b d", p=128), writes=[xt])
        for b in range(nb):
            for h in range(2):
                ps = psum[4 + b * 2 + h]
                S.op("dve", lambda e, ps=ps, b=b, h=h: e.tensor_tensor(
                    out=xt[:, b, h * 512:(h + 1) * 512], in0=ps[:, :], in1=xt[:, b, h * 512:(h + 1) * 512],
                    op=ALU.add), [ps, xt], [xt])
        rmsnorm_T(None, xt, nb, 8, l, hT, Wk, banks=(0, 4))
        units = []
        for j in range(22):
            ug, uv = ub[(2 * j) % 4], ub[(2 * j + 1) % 4]
            cg, cvv = cv[(2 * j) % 4], cv[(2 * j + 1) % 4]
            t1, t2 = tmp[(2 * j) % 4], tmp[(2 * j + 1) % 4]

            def s0(j=j, ug=ug, uv=uv):
                for m, u in ((j, ug), (j + 22, uv)):
                    ps = nextps(0, 4)
                    for c in range(8):
                        S.op("pe", lambda e, ps=ps, c=c, m=m: e.matmul(
                            ps[:, 0:TT], lhsT=Wu[:, c, m * 128:(m + 1) * 128], rhs=hT[:, c, :],
                            start=(c == 0), stop=(c == 7)), [Wu, hT], [ps])
                    S.op("act", lambda e, ps=ps, u=u: e.activation(out=u[:, 2:TT + 2], in_=ps[:, 0:TT], func=AF.Copy),
                         [ps], [u])
                    S.op("pool", lambda e, u=u, m=m: e.tensor_copy(out=u[:, 0:2], in_=halo[:, m, :]), [halo], [u])

            def s1(j=j, ug=ug, uv=uv, cg=cg, cvv=cvv):
                for m, u, c_ in ((j, ug, cg), (j + 22, uv, cvv)):
                    pc = FC + m * 3
                    S.op("act", lambda e, u=u, c_=c_, pc=pc: e.activation(
                        out=c_[:, :], in_=u[:, 0:TT], func=AF.Copy, scale=prmall[:, l, pc:pc + 1]),
                        [u, prmall], [c_])
                    S.op("dve", lambda e, u=u, c_=c_, pc=pc: e.scalar_tensor_tensor(
                        out=c_[:, :], in0=u[:, 1:TT + 1], scalar=prmall[:, l, pc + 1:pc + 2], in1=c_[:, :],
                        op0=ALU.mult, op1=ALU.add), [u, prmall, c_], [c_])
                    S.op("dve", lambda e, u=u, c_=c_, pc=pc: e.scalar_tensor_tensor(
                        out=c_[:, :], in0=u[:, 2:TT + 2], scalar=prmall[:, l, pc + 2:pc + 3], in1=c_[:, :],
                        op0=ALU.mult, op1=ALU.add), [u, prmall, c_], [c_])
                    S.op("pool", lambda e, u=u, m=m: e.tensor_copy(out=halo[:, m, :], in_=u[:, TT:TT + 2]), [u], [halo])

            def s2(cg=cg, cvv=cvv, t1=t1, t2=t2):
                S.op("act", lambda e: e.activation(out=t1[:, :], in_=cg[:, :], func=AF.Exp, scale=-1.0), [cg], [t1])
                S.op("act", lambda e: e.activation(out=t1[:, :], in_=t1[:, :], func=AF.Ln, bias=one[:, 0:1]), [t1, one], [t1])
                S.op("act", lambda e: e.activation(out=t1[:, :], in_=t1[:, :], func=AF.Exp, scale=-1.0), [t1], [t1])
                S.op("pool", lambda e: e.tensor_tensor(out=t2[:, :], in0=cg[:, :], in1=cvv[:, :], op=ALU.mult),
                     [cg, cvv], [t2])

            def s3(j=j, t1=t1, t2=t2):
                S.op("pool", lambda e: e.tensor_tensor(out=aT[:, j, :], in0=t1[:, :], in1=t2[:, :], op=ALU.mult),
                     [t1, t2], [aT])

            units.append([s0, s1, s2, s3])
            if j == 8 and t + 1 < L // TT:
                units.append([lambda t=t: outproj_early(t + 1)])
        skew_emit(units)
        for b in range(nb):
            for h in range(2):
                ps = nextps(0, 4)
                for j in range(22):
                    S.op("pe", lambda e, ps=ps, j=j, b=b, h=h: e.matmul(
                        ps[:, :], lhsT=aT[:, j, b * 128:(b + 1) * 128], rhs=Wd[:, j, h * 512:(h + 1) * 512],
                        start=(j == 0), stop=(j == 21)), [aT, Wd], [ps])
                S.op("dve", lambda e, ps=ps, b=b, h=h: e.tensor_tensor(
                    out=xt[:, b, h * 512:(h + 1) * 512], in0=ps[:, :], in1=xt[:, b, h * 512:(h + 1) * 512],
                    op=ALU.add), [ps, xt], [xt])
        if not last:
            S.dma("sp", xres[t0:t0 + TT, :].rearrange("(b p) d -> p b d", p=128), xt[:, :, :], reads=[xt])
        else:
            junk, ss, rstd = Wk["junk"], Wk["ss"], Wk["rstd"]
            for b in range(nb):
                S.op("act", lambda e, b=b: e.activation(out=junk[:, :], in_=xt[:, b, :], func=AF.Square,
                                                         accum_out=ss[:, b:b + 1]), [xt], [junk, ss])
            S.op("act", lambda e: e.activation(out=rstd[:, 0:nb], in_=ss[:, 0:nb], func=AF.Ln,
                                               scale=1.0 / D, bias=Wk["eps"][:, 0:1]), [ss, Wk["eps"]], [rstd])
            S.op("act", lambda e: e.activation(out=rstd[:, 0:nb], in_=rstd[:, 0:nb], func=AF.Exp, scale=-0.5),
                 [rstd], [rstd])
            for b in range(nb):
                S.op("dve", lambda e, b=b: e.scalar_tensor_tensor(
                    out=xt[:, b, :], in0=xt[:, b, :], scalar=rstd[:, b:b + 1], in1=wfb[:, :],
                    op0=ALU.mult, op1=ALU.mult), [xt, rstd, wfb], [xt])
            S.dma("sp", out[t0:t0 + TT, :].rearrange("(b p) d -> p b d", p=128), xt[:, :, :], reads=[xt])


def phase_A(S, nc, L, l, xsrc, w_in, prmall, cf, idb, onesb, psum, nextps, rmsnorm_T, load_w_bf16, dr):
    TT = 512
    nb = 4
    YT, gq, gk, gz, gktm, gvtm, gab, sq, sk, svtm = (dr[n] for n in
                                                       ("YT", "gq", "gk", "gz", "gktm", "gvtm", "gab", "sq", "sk", "svtm"))
    Win = S.sb([128, 8, DIN], BF16, "Win")
    load_w_bf16(Win, w_in[l], 8)
    xt = S.sb([128, nb, D], F32, "xt")
    hT = S.sb([128, 8, TT], BF16, "hT")
    xn = S.sb([128, nb, D], BF16, "xn")
    ss, rstd = S.sb([128, 4], F32, "ss"), S.sb([128, 4], F32, "rstd")
    eps = S.sb([128, 1], F32, "eps")
    S.op("dve", lambda e: e.memset(eps[:, :], EPS), [], [eps])
    one = S.sb([128, 1], F32, "one")
    S.op("dve", lambda e: e.memset(one[:, :], 1.0), [], [one])
    lnq = S.sb([128, 1], F32, "lnq")
    S.op("dve", lambda e: e.memset(lnq[:, :], -0.5 * float(np.log(128.0))), [], [lnq])
    na = S.sb([128, 1], F32, "na")
    S.op("act", lambda e: e.activation(out=na[:, :], in_=prmall[:, l, 205:206], func=AF.Exp), [prmall], [na])
    S.op("dve", lambda e: e.tensor_scalar(out=na[:, :], in0=na[:, :], scalar1=-1.0, scalar2=None, op0=ALU.mult),
         [na], [na])
    rbuf = [S.sb([128, TT + 3], F32, f"rbuf{m}") for m in range(12)]
    pbuf = [S.sb([128, TT + 2], F32, f"pbuf{i}") for i in range(2)]
    for b_ in rbuf:
        S.op("pool", lambda e, b_=b_: e.memset(b_[:, 0:3], 0.0), [], [b_])
    for b_ in pbuf:
        S.op("pool", lambda e, b_=b_: e.memset(b_[:, 0:2], 0.0), [], [b_])

    def fam(n, dt, name, shape=(128, TT)):
        return [S.sb(list(shape), dt, f"{name}{i}") for i in range(n)]

    cvb, e1, sfl, rs = fam(4, F32, "cvb"), fam(3, F32, "e1"), fam(5, F32, "sfl"), fam(3, F32, "rs")
    evA, evB, evC = fam(5, F32, "evA"), fam(2, F32, "evB"), fam(2, F32, "evC")
    sqb, obf = fam(3, BF16, "sqb"), fam(9, BF16, "obf")
    tm = fam(3, BF16, "tm", (128, 4, 128))
    ab = fam(4, F32, "ab")
    gabt = S.sb([128, 4, 12], F32, "gabt")
    idf = cf
    nt = L // TT

    def xload(t):
        S.dma("sp", xt[:, :, :], xsrc[t * TT:(t + 1) * TT, :].rearrange("(b p) d -> p b d", p=128), writes=[xt])

    def mm_tile(c0, ncols):
        ps = nextps(0, 4)
        for c in range(8):
            S.op("pe", lambda e, ps=ps, c=c: e.matmul(ps[0:ncols, 0:TT], lhsT=Win[:, c, c0:c0 + ncols], rhs=hT[:, c, :],
                                                      start=(c == 0), stop=(c == 7)), [Win, hT], [ps])
        return ps

    def evac(ps, dst_ap, dst):
        S.op("act", lambda e: e.activation(out=dst_ap, in_=ps[:, 0:TT], func=AF.Copy), [ps], [dst])

    def sigmoid(src, e_):
        S.op("act", lambda e: e.activation(out=e_[:, :], in_=src[:, 0:TT], func=AF.Exp, scale=-1.0), [src], [e_])
        S.op("act", lambda e: e.activation(out=e_[:, :], in_=e_[:, :], func=AF.Ln, bias=one[:, 0:1]), [e_, one], [e_])
        S.op("act", lambda e: e.activation(out=e_[:, :], in_=e_[:, :], func=AF.Exp, scale=-1.0), [e_], [e_])

    def conv(src, cv_, pc, K):
        S.op("dve", lambda e: e.tensor_scalar(out=cv_[:, :], in0=src[:, 0:TT], scalar1=prmall[:, l, pc:pc + 1],
                                              scalar2=None, op0=ALU.mult), [src, prmall], [cv_])
        for kk in range(1, K):
            S.op("dve", lambda e, kk=kk: e.scalar_tensor_tensor(
                out=cv_[:, :], in0=src[:, kk:TT + kk], scalar=prmall[:, l, pc + kk:pc + kk + 1], in1=cv_[:, :],
                op0=ALU.mult, op1=ALU.add), [src, prmall, cv_], [cv_])
        S.op("pool", lambda e: e.tensor_copy(out=src[:, 0:K - 1], in_=src[:, TT:TT + K - 1]), [src], [src])

    def tr_pe(o, bank):
        pv = bank.t[:, :].bitcast(BF16)
        for b in range(4):
            S.op("pe", lambda e, b=b: e.transpose(out=pv[:, b * 128:(b + 1) * 128], in_=o[:, b * 128:(b + 1) * 128],
                                                  identity=idb[:, :]), [o, idb], [bank])

    def tr_ev(bank, t_, dst_ap):
        pv = bank.t[:, :].bitcast(BF16)
        S.op("act", lambda e: e.activation(out=t_[:, :, :], in_=pv[:, 0:512].rearrange("p (b f) -> p b f", b=4),
                                           func=AF.Copy), [bank], [t_])
        S.dma("sp", dst_ap, t_[:, :, :], reads=[t_])

    units = []

    def add(stages):
        units.append(stages + [None] * (10 - len(stages)))

    for t in range(nt):
        t0 = t * TT
        def prologue(t=t):
            for b in range(nb):
                S.op("act", lambda e, b=b: e.activation(out=xn[:, b, :], in_=xt[:, b, :], func=AF.Square,
                                                         accum_out=ss[:, b:b + 1]), [xt], [xn, ss])
            S.op("act", lambda e: e.activation(out=rstd[:, 0:nb], in_=ss[:, 0:nb], func=AF.Ln, scale=1.0 / D,
                                               bias=eps[:, 0:1]), [ss, eps], [rstd])
            S.op("act", lambda e: e.activation(out=rstd[:, 0:nb], in_=rstd[:, 0:nb], func=AF.Exp, scale=-0.5), [rstd], [rstd])
            for b in range(nb):
                if b % 2 == 0:
                    S.op("dve", lambda e, b=b: e.tensor_scalar(out=xn[:, b, :], in0=xt[:, b, :], scalar1=rstd[:, b:b + 1],
                                                               scalar2=None, op0=ALU.mult), [xt, rstd], [xn])
                else:
                    S.op("act", lambda e, b=b: e.activation(out=xn[:, b, :], in_=xt[:, b, :], func=AF.Copy,
                                                            scale=rstd[:, b:b + 1]), [xt, rstd], [xn])
            if t + 1 < nt:
                xload(t + 1)
            for c0 in range(0, 8, 2):
                ps = nextps(0, 4)
                pv = ps.t[:, :].bitcast(BF16)
                for ci in range(2):
                    c = c0 + ci
                    for b in range(nb):
                        S.op("pe", lambda e, c=c, b=b, ci=ci: e.transpose(
                            out=pv[:, ci * 512 + b * 128: ci * 512 + (b + 1) * 128],
                            in_=xn[:, b, c * 128:(c + 1) * 128], identity=idb[:, :]), [xn, idb], [ps])
                for ci in range(2):
                    c = c0 + ci
                    S.op("act", lambda e, c=c, ci=ci: e.activation(
                        out=hT[:, c, :], in_=pv[:, ci * 512:(ci + 1) * 512], func=AF.Copy,
                        scale=prmall[:, l, c:c + 1]), [ps, prmall], [hT])
        add([prologue])
        for i in range(2):
            u = len(units)
            a_, b_, c_, cv_, o = evA[u % 5], evB[i], evC[i], cvb[u % 4], obf[u % 9]
            pb = pbuf[i]

            def s0(i=i, a_=a_, b_=b_, c_=c_):
                evac(mm_tile(256 + i * 128, 128), a_[:, :], a_)
                evac(mm_tile(512 + i * 128, 128), b_[:, :], b_)
                evac(mm_tile(i * 128, 128), c_[:, :], c_)

            def s1(i=i, a_=a_, b_=b_, cv_=cv_, pb=pb):
                S.op("dve", lambda e: e.tensor_tensor(out=pb[:, 2:TT + 2], in0=a_[:, :], in1=b_[:, :], op=ALU.mult),
                     [a_, b_], [pb])
                conv(pb, cv_, 16 + i * 3, 3)

            def s2(i=i, c_=c_, cv_=cv_, o=o, t0=t0):
                S.op("dve", lambda e: e.tensor_tensor(out=o[:, :], in0=c_[:, :], in1=cv_[:, :], op=ALU.mult), [c_, cv_], [o])
                S.dma("sp", YT[i * 128:(i + 1) * 128, t0:t0 + TT], o[:, :], reads=[o])
            add([s0, s1, s2])
        for m in range(12):
            u = len(units)
            h = m % 4
            rb, cv_, e_, s_, q_, r_, o, t_ = rbuf[m], cvb[u % 4], e1[u % 3], sfl[u % 5], sqb[u % 3], rs[u % 3], obf[u % 9], tm[u % 3]
            pss, ptr = psum[4 + u % 2], psum[6 + u % 2]
            qk = m < 8

            def s0(m=m, rb=rb):
                evac(mm_tile(768 + m * 128, 128), rb[:, 3:TT + 3], rb)

            def s1(m=m, rb=rb, cv_=cv_):
                conv(rb, cv_, 22 + m * 4, 4)

            def s2(cv_=cv_, e_=e_):
                sigmoid(cv_, e_)

            def s3(cv_=cv_, e_=e_, dst=(s_ if qk else o)):
                S.op("dve", lambda e: e.tensor_tensor(out=dst[:, :], in0=cv_[:, :], in1=e_[:, :], op=ALU.mult),
                     [cv_, e_], [dst])
            st = [s0, s1, s2, s3]
            if qk:
                def s4(s_=s_, q_=q_):
                    S.op("act", lambda e: e.activation(out=q_[:, :], in_=s_[:, :], func=AF.Square), [s_], [q_])

                def s5(q_=q_, pss=pss):
                    S.op("pe", lambda e: e.matmul(pss[:, :], lhsT=onesb[:, :], rhs=q_[:, :], start=True, stop=True),
                         [onesb, q_], [pss])

                def s6(m=m, pss=pss, r_=r_):
                    S.op("act", lambda e: e.activation(out=r_[:, :], in_=pss[:, :], func=AF.Ln, bias=eps[:, 0:1]),
                         [pss, eps], [r_])
                    if m < 4:
                        S.op("act", lambda e: e.activation(out=r_[:, :], in_=r_[:, :], func=AF.Exp, scale=-0.5,
                                                           bias=lnq[:, 0:1]), [r_, lnq], [r_])
                    else:
                        S.op("act", lambda e: e.activation(out=r_[:, :], in_=r_[:, :], func=AF.Exp, scale=-0.5), [r_], [r_])

                def s7(m=m, h=h, s_=s_, r_=r_, o=o, t0=t0):
                    S.op("pool", lambda e: e.tensor_tensor(out=o[:, :], in0=s_[:, :], in1=r_[:, :], op=ALU.mult), [s_, r_], [o])
                    dst = gq if m < 4 else gk
                    S.dma("sp", dst[h * 128:(h + 1) * 128, t0:t0 + TT], o[:, :], reads=[o])
                st += [s4, s5, s6, s7]
            else:
                st += [None, None, None, None]
            if m >= 4:
                dstT = gktm if m < 8 else gvtm

                def s8(o=o, ptr=ptr):
                    tr_pe(o, ptr)

                def s9(h=h, ptr=ptr, t_=t_, dstT=dstT, t0=t0):
                    tr_ev(ptr, t_, dstT[t0:t0 + TT, h * 128:(h + 1) * 128].rearrange("(b p) f -> p b f", p=128))
                st += [s8, s9]
            add(st)
        for h in range(4):
            u = len(units)
            a_, e_, o = evA[u % 5], e1[u % 3], obf[u % 9]

            def s0(h=h, a_=a_):
                evac(mm_tile(2304 + h * 128, 128), a_[:, :], a_)

            def s2(a_=a_, e_=e_):
                sigmoid(a_, e_)

            def s3(h=h, a_=a_, e_=e_, o=o, t0=t0):
                S.op("dve", lambda e: e.tensor_tensor(out=o[:, :], in0=a_[:, :], in1=e_[:, :], op=ALU.mult), [a_, e_], [o])
                S.dma("sp", gz[h * 128:(h + 1) * 128, t0:t0 + TT], o[:, :], reads=[o])
            add([s0, None, s2, s3])
        u = len(units)
        e8, ln8, rc8, g8 = ab
        ptr = psum[6 + u % 2]

        def s0():
            ps = mm_tile(2816, 8)
            S.op("act", lambda e: e.activation(out=e8[0:8, :], in_=ps[0:8, :], func=AF.Exp,
                                               scale=prmall[0:8, l, 203:204], bias=prmall[0:8, l, 204:205]),
                 [ps, prmall], [e8])

        def s1():
            S.op("dve", lambda e: e.tensor_scalar(out=e8[0:8, :], in0=e8[0:8, :], scalar1=1.0, scalar2=None, op0=ALU.add),
                 [e8], [e8])

        def s2():
            S.op("act", lambda e: e.activation(out=ln8[0:8, :], in_=e8[0:8, :], func=AF.Ln), [e8], [ln8])

        def s3():
            S.op("dve", lambda e: e.reciprocal(out=rc8[0:8, :], in_=e8[0:8, :]), [e8], [rc8])
            S.op("dve", lambda e: e.tensor_scalar(out=g8[0:8, :], in0=ln8[0:8, :], scalar1=na[0:8, 0:1], scalar2=None,
                                                  op0=ALU.mult), [ln8, na], [g8])

        def s8(ptr=ptr):
            for b in range(4):
                S.op("pe", lambda e, b=b: e.transpose(out=ptr[:, b * 16:b * 16 + 8], in_=g8[0:8, b * 128:(b + 1) * 128],
                                                      identity=idf[0:8, C_ID:C_ID + 8]), [g8, idf], [ptr])
                S.op("pe", lambda e, b=b: e.transpose(out=ptr[:, b * 16 + 8:b * 16 + 16], in_=rc8[0:8, b * 128:(b + 1) * 128],
                                                      identity=idf[0:8, C_ID:C_ID + 8]), [rc8, idf], [ptr])

        def s9(ptr=ptr, t0=t0):
            pA = ptr.t[:, 0:64].rearrange("p (b f) -> p b f", b=4)
            S.op("dve", lambda e: e.tensor_copy(out=gabt[:, :, 0:4], in_=pA[:, :, 0:4]), [ptr], [gabt])
            S.op("dve", lambda e: e.tensor_copy(out=gabt[:, :, 4:8], in_=pA[:, :, 12:16]), [ptr], [gabt])
            S.op("dve", lambda e: e.tensor_copy(out=gabt[:, :, 8:12], in_=pA[:, :, 4:8]), [ptr], [gabt])
            S.dma("sp", gab[t0:t0 + TT, :].rearrange("(b p) f -> p b f", p=128), gabt[:, :, :], reads=[gabt])
        add([s0, s1, s2, s3, None, None, None, None, s8, s9])
        for i in range(2):
            for dst, c0 in ((sq, 2824), (sk, 3080)):
                u = len(units)
                o = obf[u % 9]

                def s0(i=i, dst=dst, c0=c0, o=o, t0=t0):
                    evac(mm_tile(c0 + i * 128, 128), o[:, :], o)
                    S.dma("sp", dst[i * 128:(i + 1) * 128, t0:t0 + TT], o[:, :], reads=[o])
                add([s0])
            u = len(units)
            o, t_, ptr = obf[u % 9], tm[u % 3], psum[6 + u % 2]

            def s0(i=i, o=o):
                evac(mm_tile(3336 + i * 128, 128), o[:, :], o)

            def s8(o=o, ptr=ptr):
                tr_pe(o, ptr)

            def s9(i=i, ptr=ptr, t_=t_, t0=t0):
                tr_ev(ptr, t_, svtm[t0:t0 + TT, i * 128:(i + 1) * 128].rearrange("(b p) f -> p b f", p=128))
            add([s0, None, None, None, None, None, None, None, s8, s9])
    xload(0)
    skew_emit(units)


def phase_SB(S, nc, L, l, cf, psum, nextps, dr):
    YT, sq, sk, svtm, cst = dr["YT"], dr["sq"], dr["sk"], dr["svtm"], dr["cst"]
    NB = L // 128
    NT = L // 512
    M01 = S.sb([128, 4, 512], BF16, "M01")
    S.dma("pool", M01[:, :, :], cst[:, C_SBM:C_SBM + 2048].rearrange("p (k q) -> p k q", k=4), writes=[M01])
    ntri = S.sb([128, 128], BF16, "ntri")
    S.op("dve", lambda e: e.tensor_copy(out=ntri[:, :], in_=cf[:, C_NTRI:C_NTRI + 128]), [cf], [ntri])
    nones = S.sb([128, 128], BF16, "nones")
    S.op("dve", lambda e: e.memset(nones[:, :], -1.0), [], [nones])
    one = S.sb([128, 1], F32, "one")
    S.op("dve", lambda e: e.memset(one[:, :], 1.0), [], [one])
    KT = [S.sb([128, L], BF16, f"KT{i}") for i in range(2)]
    QT = [S.sb([128, L], BF16, f"QTz{h}") for h in range(4)]
    Vp = [S.sb([128, NB, 128], BF16, f"Vp{i}") for i in range(2)]
    for i in range(2):
        S.dma("sp", KT[i][:, :], sk[i * 128:(i + 1) * 128, :], writes=[KT[i]])
        for hh in range(2):
            qz = QT[2 * i + hh]
            oth = (1 - hh) * 64
            S.op("pool" if hh else "dve", lambda e, qz=qz, oth=oth: e.memset(qz[oth:oth + 64, :], 0.0), [], [qz])
            S.dma("sp", qz[hh * 64:hh * 64 + 64, :], sq[i * 128 + hh * 64:i * 128 + hh * 64 + 64, :], writes=[qz])
        for n0 in range(0, NB, 8):
            S.dma("sp", Vp[i][:, n0:n0 + 8, :],
                  svtm[n0 * 128:(n0 + 8) * 128, i * 128:(i + 1) * 128].rearrange("(n p) f -> p n f", p=128), writes=[Vp[i]])
    ebuf = [S.sb([128, 512], F32, f"eb{i}") for i in range(3)]
    spb = [S.sb([128, 512], BF16, f"spb{i}") for i in range(4)]
    lbuf = [S.sb([128, 512], F32, f"lb{i}") for i in range(3)]
    abuf = [S.sb([128, 512], BF16, f"ab{i}") for i in range(4)]
    Sacc = [S.sb([128, 512], BF16, f"Sacc{i}") for i in range(3)]
    osb = [S.sb([128, 512], BF16, f"osb{i}") for i in range(2)]
    units = []
    on = 0
    for h in range(4):
        for T in range(NT):
            nkb = 4 * T + 4
            for idx, kb in enumerate(range(nkb - 1, -1, -1)):
                u = len(units)
                units.append(dict(u=u, h=h, i=h // 2, r0=(h % 2) * 64, T=T, idx=idx, kb=kb, diag=kb >= 4 * T, kbi=kb - 4 * T,
                                  on=on, e_=ebuf[u % 3], sp_=spb[u % 4], lb_=lbuf[u % 3], a_=abuf[u % 4],
                                  zps=psum[u % 3], tps=psum[3 + u % 3], ops_=psum[6 + on % 2], o_=osb[on % 2]))
            on += 1
    st = dict(sa=0)

    def S1(U):
        i, r0, T, kb, zps, e_, sp_ = U["i"], U["r0"], U["T"], U["kb"], U["zps"], U["e_"], U["sp_"]
        qz = QT[U["h"]]
        S.op("pe", lambda e: e.matmul(zps[:, :], lhsT=KT[i][:, kb * 128:(kb + 1) * 128],
                                      rhs=qz[:, T * 512:(T + 1) * 512], start=True, stop=True),
             [KT[i], qz], [zps])
        S.op("act", lambda e: e.activation(out=e_[:, :], in_=zps[:, :], func=AF.Exp, scale=0.125), [zps], [e_])
        S.op("act", lambda e: e.activation(out=sp_[:, :], in_=e_[:, :], func=AF.Ln, bias=one[:, 0:1]), [e_, one], [sp_])
        if U["diag"]:
            S.op("pool", lambda e: e.tensor_tensor(out=sp_[:, :], in0=sp_[:, :], in1=M01[:, U["kbi"], :], op=ALU.mult),
                 [sp_, M01], [sp_])

    def S2(U):
        idx, kb, zps, tps, sp_, lb_ = U["idx"], U["kb"], U["zps"], U["tps"], U["sp_"], U["lb_"]
        S.op("pe", lambda e: e.matmul(tps[:, :], lhsT=ntri[:, :], rhs=sp_[:, :], start=True, stop=(idx == 0)),
             [ntri, sp_], [tps])
        if idx == 0:
            st["sa"] = 0
        else:
            sc = Sacc[st["sa"] % 3]
            S.op("pe", lambda e: e.matmul(tps[:, :], lhsT=nones[:, :], rhs=sc[:, :], start=False, stop=True),
                 [nones, sc], [tps])
        if kb > 0:
            if idx == 0:
                sn = Sacc[st["sa"] % 3]
                S.op("pool", lambda e: e.tensor_copy(out=sn[:, :], in_=sp_[:, :]), [sp_], [sn])
            else:
                so, sn = Sacc[st["sa"] % 3], Sacc[(st["sa"] + 1) % 3]
                S.op("pool", lambda e: e.tensor_tensor(out=sn[:, :], in0=so[:, :], in1=sp_[:, :], op=ALU.add),
                     [so, sp_], [sn])
                st["sa"] += 1
        S.op("dve", lambda e: e.scalar_tensor_tensor(out=lb_[:, :], in0=zps[:, :], scalar=0.125, in1=sp_[:, :],
                                                     op0=ALU.mult, op1=ALU.subtract), [zps, sp_], [lb_])

    def S3(U):
        i, r0, h, T, idx, kb = U["i"], U["r0"], U["h"], U["T"], U["idx"], U["kb"]
        tps, lb_, a_, ops_, o_ = U["tps"], U["lb_"], U["a_"], U["ops_"], U["o_"]
        S.op("dve", lambda e: e.tensor_tensor(out=lb_[:, :], in0=tps[:, :], in1=lb_[:, :], op=ALU.add), [tps, lb_], [lb_])
        S.op("act", lambda e: e.activation(out=a_[:, :], in_=lb_[:, :], func=AF.Exp), [lb_], [a_])
        if U["diag"]:
            S.op("pool", lambda e: e.tensor_tensor(out=a_[:, :], in0=a_[:, :], in1=M01[:, U["kbi"], :], op=ALU.mult),
                 [a_, M01], [a_])

    def S4(U):
        i, r0, h, T, idx, kb = U["i"], U["r0"], U["h"], U["T"], U["idx"], U["kb"]
        a_, ops_, o_ = U["a_"], U["ops_"], U["o_"]
        S.op("pe", lambda e: e.matmul(ops_[:, :], lhsT=Vp[i][:, kb, :], rhs=a_[:, :], start=(idx == 0),
                                      stop=(kb == 0)), [Vp[i], a_], [ops_])
        if kb == 0:
            S.op("act", lambda e: e.activation(out=o_[r0:r0 + 64, :], in_=ops_[r0:r0 + 64, :], func=AF.Copy), [ops_], [o_])
            S.dma("sp", YT[768 + h * 64:768 + (h + 1) * 64, T * 512:(T + 1) * 512], o_[r0:r0 + 64, :], reads=[o_])

    skew_emit([[lambda U=U: S1(U), lambda U=U: S2(U), lambda U=U: S3(U), lambda U=U: S4(U)] for U in units])


def phase_GDN(S, nc, L, l, prmall, cf, idb, onesb, onesf, psum, nextps, dr):
    YT, gq, gk, gz, gktm, gvtm, gab = (dr[n] for n in ("YT", "gq", "gk", "gz", "gktm", "gvtm", "gab"))
    NG = L // 512
    NCH = L // 128
    UT = cf[:, C_UT:C_UT + 128]
    idf = cf[:, C_ID:C_ID + 128]
    MI, MS, MSL = cf[:, C_MI:C_MI + 128], cf[:, C_MS:C_MS + 128], cf[:, C_MSL:C_MSL + 128]
    idb4 = S.sb([128, 4, 128], F32, "idb4")
    for h in range(4):
        S.op("dve", lambda e, h=h: e.tensor_copy(out=idb4[:, h, :], in_=idf), [cf], [idb4])
    eps = S.sb([128, 1], F32, "eps")
    S.op("dve", lambda e: e.memset(eps[:, :], EPS), [], [eps])

    def B4(name, dt=BF16):
        return S.sb([128, 4, 128], dt, name)

    qTg = [S.sb([128, 4, 512], BF16, f"qTg{i}") for i in range(2)]
    kTg = [S.sb([128, 4, 512], BF16, f"kTg{i}") for i in range(2)]
    gzg = [S.sb([128, 4, 512], BF16, f"gzg{i}") for i in range(2)]
    ktmg = [S.sb([128, 4, 512], BF16, f"ktmg{i}") for i in range(2)]
    vtmg = [S.sb([128, 4, 512], BF16, f"vtmg{i}") for i in range(2)]
    gabg = [S.sb([128, 4, 12], F32, f"gabg{i}") for i in range(2)]
    ygo = [S.sb([128, 4, 512], BF16, f"ygo{i}") for i in range(2)]
    sets = []
    for si in range(2):
        W = {}
        for n in ("Gc", "nGc", "R2", "eG", "bEG", "dGl", "eGlmG", "eGl"):
            W[n] = S.sb([128, 4], F32, f"{n}{si}")
        for n in ("diag1", "diag2", "arg1", "arg2", "arg3", "E1", "E2", "E3", "EG", "u_sb", "rn", "y1"):
            W[n] = B4(f"{n}{si}", F32)
        for n in ("AT", "An", "P0", "P1", "Pt0", "Pt1", "M0", "M1", "Mt0", "Mt1", "vb", "kge", "IpA", "Rr", "Xr"):
            W[n] = B4(f"{n}{si}", F32R)
        for n in ("attnT", "kdec", "wT", "qg", "vnew", "sqo"):
            W[n] = B4(f"{n}{si}")
        sets.append(W)
    S32, Sbf = B4("S32", F32), B4("Sbf")
    S.op("pool", lambda e: e.memset(S32[:, :, :], 0.0), [], [S32])
    S.op("pool", lambda e: e.memset(Sbf[:, :, :], 0.0), [], [Sbf])
    wgn = prmall[:, l, 70:71]
    bctr = [0, 0]

    def f2(b):
        return b[:, :, :].rearrange("p h c -> p (h c)")

    def f2r(b):
        return b[:, :, :].bitcast(F32).rearrange("p h c -> p (h c)")

    def load_group(g):
        t0 = g * 512
        i = g % 2
        S.dma("sp", qTg[i][:, :, :], gq[:, t0:t0 + 512].rearrange("(h p) n -> p h n", p=128), writes=[qTg[i]])
        S.dma("sp", kTg[i][:, :, :], gk[:, t0:t0 + 512].rearrange("(h p) n -> p h n", p=128), writes=[kTg[i]])
        S.dma("sp", gzg[i][:, :, :], gz[:, t0:t0 + 512].rearrange("(h p) n -> p h n", p=128), writes=[gzg[i]])
        S.dma("sp", ktmg[i][:, :, :], gktm[t0:t0 + 512, :].rearrange("(n p) f -> p n f", p=128), writes=[ktmg[i]])
        S.dma("sp", vtmg[i][:, :, :], gvtm[t0:t0 + 512, :].rearrange("(n p) f -> p n f", p=128), writes=[vtmg[i]])
        S.dma("sp", gabg[i][:, :, :], gab[t0:t0 + 512, :].rearrange("(n p) f -> p n f", p=128), writes=[gabg[i]])

    def mm4(ps, lhs_fn, rhs_fn, reads):
        for h in range(4):
            S.op("pe", lambda e, h=h: e.matmul(ps[:, h * 128:(h + 1) * 128], lhsT=lhs_fn(h), rhs=rhs_fn(h), start=True,
                                               stop=True), reads, [ps])

    def chunk(n):
        g, ci = n // 4, n % 4
        gi, si = g % 2, n % 2
        W = sets[si]

        def nps():
            i = si * 4 + bctr[si] % 4
            bctr[si] += 1
            return psum[i]

        qT_, kT_, gz_, ktm_, vtm_, gab_, yo_ = qTg[gi], kTg[gi], gzg[gi], ktmg[gi], vtmg[gi], gabg[gi], ygo[gi]
        cs = slice(ci * 128, (ci + 1) * 128)
        graw, beta, lnb = gab_[:, ci, 0:4], gab_[:, ci, 4:8], gab_[:, ci, 8:12]
        Gc, nGc, R2, eG, bEG, dGl, eGlmG, eGl = (W[k] for k in ("Gc", "nGc", "R2", "eG", "bEG", "dGl", "eGlmG", "eGl"))
        diag1, diag2, arg1, arg2, arg3 = (W[k] for k in ("diag1", "diag2", "arg1", "arg2", "arg3"))
        E1, E2, E3, EG, u_sb, rn, y1 = (W[k] for k in ("E1", "E2", "E3", "EG", "u_sb", "rn", "y1"))
        AT, An, vb, kge, IpA, Rr, Xr = (W[k] for k in ("AT", "An", "vb", "kge", "IpA", "Rr", "Xr"))
        attnT, kdec, wT, qg, vnew, sqo = (W[k] for k in ("attnT", "kdec", "wT", "qg", "vnew", "sqo"))
        Pb, Ptb, Mb, Mtb = [W["P0"], W["P1"]], [W["Pt0"], W["Pt1"]], [W["M0"], W["M1"]], [W["Mt0"], W["Mt1"]]
        psG = nps()
        S.op("pe", lambda e: e.matmul(psG[:, 0:4], lhsT=UT, rhs=graw, start=True, stop=True), [cf, gab_], [psG])
        S.op("pe", lambda e: e.matmul(psG[:, 8:12], lhsT=onesf[:, :], rhs=graw, start=True, stop=True), [onesf, gab_], [psG])
        S.op("dve", lambda e: e.tensor_copy(out=Gc[:, :], in_=psG[:, 0:4]), [psG], [Gc])
        S.op("dve", lambda e: e.tensor_scalar(out=nGc[:, :], in0=psG[:, 0:4], scalar1=-1.0, scalar2=None, op0=ALU.mult),
             [psG], [nGc])
        S.op("dve", lambda e: e.tensor_tensor(out=R2[:, :], in0=psG[:, 0:4], in1=lnb, op=ALU.add), [psG, gab_], [R2])
        S.op("act", lambda e: e.activation(out=eG[:, :], in_=psG[:, 0:4], func=AF.Exp), [psG], [eG])
        S.op("dve", lambda e: e.tensor_tensor(out=bEG[:, :], in0=eG[:, :], in1=beta, op=ALU.mult), [eG, gab_], [bEG])
        S.op("dve", lambda e: e.tensor_tensor(out=dGl[:, :], in0=psG[:, 8:12], in1=Gc[:, :], op=ALU.subtract),
             [psG, Gc], [dGl])
        S.op("act", lambda e: e.activation(out=eGlmG[:, :], in_=dGl[:, :], func=AF.Exp), [dGl], [eGlmG])
        S.op("act", lambda e: e.activation(out=eGl[:, :], in_=psG[:, 8:12], func=AF.Exp), [psG], [eGl])
        for h in range(4):
            S.op("pool", lambda e, h=h: e.tensor_scalar(out=diag1[:, h, :], in0=idf, scalar1=Gc[:, h:h + 1], scalar2=1.0,
                                                        op0=ALU.mult, op1=ALU.mult), [cf, Gc], [diag1])
            S.op("pool", lambda e, h=h: e.tensor_scalar(out=diag2[:, h, :], in0=idf, scalar1=R2[:, h:h + 1], scalar2=1.0,
                                                        op0=ALU.mult, op1=ALU.mult), [cf, R2], [diag2])
        for h in range(4):
            hs = slice(h * 128, (h + 1) * 128)
            S.op("dve", lambda e, h=h, hs=hs: e.tensor_scalar(out=vb[:, h, :], in0=vtm_[:, ci, hs], scalar1=beta[:, h:h + 1],
                                                              scalar2=None, op0=ALU.mult), [vtm_, gab_], [vb])
            S.op("dve", lambda e, h=h, hs=hs: e.tensor_scalar(out=kge[:, h, :], in0=ktm_[:, ci, hs], scalar1=bEG[:, h:h + 1],
                                                              scalar2=None, op0=ALU.mult), [ktm_, bEG], [kge])
            S.op("pool", lambda e, h=h, hs=hs: e.tensor_scalar(
                out=kdec[:, h, :], in0=ktm_[:, ci, hs], scalar1=eGlmG[:, h:h + 1], scalar2=1.0, op0=ALU.mult,
                op1=ALU.mult), [ktm_, eGlmG], [kdec])
        yield
        psR1 = nps()
        S.op("pe", lambda e: e.matmul(psR1[:, :], lhsT=onesf[:, :], rhs=f2(diag1), start=True, stop=True),
             [onesf, diag1], [psR1])
        psR2 = nps()
        S.op("pe", lambda e: e.matmul(psR2[:, :], lhsT=onesf[:, :], rhs=f2(diag2), start=True, stop=True),
             [onesf, diag2], [psR2])
        for h in range(4):
            S.op("dve", lambda e, h=h: e.scalar_tensor_tensor(
                out=arg1[:, h, :], in0=psR1[:, h * 128:(h + 1) * 128], scalar=nGc[:, h:h + 1], in1=MI, op0=ALU.add,
                op1=ALU.add), [psR1, nGc, cf], [arg1])
            S.op("dve", lambda e, h=h: e.scalar_tensor_tensor(
                out=arg2[:, h, :], in0=psR2[:, h * 128:(h + 1) * 128], scalar=nGc[:, h:h + 1], in1=MS, op0=ALU.add,
                op1=ALU.add), [psR2, nGc, cf], [arg2])
            S.op("dve", lambda e, h=h: e.scalar_tensor_tensor(
                out=arg3[:, h, :], in0=psR1[:, h * 128:(h + 1) * 128], scalar=-1.0, in1=MSL, op0=ALU.mult,
                op1=ALU.add), [psR1, cf], [arg3])
        S.op("act", lambda e: e.activation(out=f2(E1), in_=f2(arg1), func=AF.Exp), [arg1], [E1])
        S.op("act", lambda e: e.activation(out=f2(E2), in_=f2(arg2), func=AF.Exp), [arg2], [E2])
        for h in range(4):
            S.op("act", lambda e, h=h: e.activation(out=E3[:, h, :], in_=arg3[:, h, :], func=AF.Exp, bias=R2[:, h:h + 1]),
                 [arg3, R2], [E3])
        S.op("act", lambda e: e.activation(out=f2(EG), in_=psR1[:, :], func=AF.Exp), [psR1], [EG])
        S.op("pool", lambda e: e.tensor_tensor(out=qg[:, :, :], in0=qT_[:, :, cs], in1=EG[:, :, :], op=ALU.mult),
             [qT_, EG], [qg])
        yield
        psKK = nps()
        mm4(psKK, lambda h: kT_[:, h, cs], lambda h: kT_[:, h, cs], [kT_])
        psQK = nps()
        mm4(psQK, lambda h: kT_[:, h, cs], lambda h: qT_[:, h, cs], [kT_, qT_])
        S.op("dve", lambda e: e.tensor_tensor(out=f2(AT), in0=psKK[:, :], in1=f2(E2), op=ALU.mult), [psKK, E2], [AT])
        S.op("dve", lambda e: e.tensor_tensor(out=f2(An), in0=psKK[:, :], in1=f2(E3), op=ALU.mult), [psKK, E3], [An])
        S.op("dve", lambda e: e.tensor_tensor(out=f2(attnT), in0=psQK[:, :], in1=f2(E1), op=ALU.mult), [psQK, E1], [attnT])
        P, Pt = AT, An
        M, Mt = Mb[0], Mtb[0]
        S.op("dve", lambda e: e.tensor_tensor(out=f2(M), in0=f2(idb4), in1=f2r(AT), op=ALU.subtract), [idb4, AT], [M])
        S.op("dve", lambda e: e.tensor_tensor(out=f2(Mt), in0=f2(idb4), in1=f2r(An), op=ALU.subtract), [idb4, An], [Mt])
        S.op("pool", lambda e: e.tensor_tensor(out=f2(IpA), in0=f2(idb4), in1=f2r(An), op=ALU.add), [idb4, An], [IpA])
        yield
        for k in range(1, 7):
            lastk = (k == 6)
            Pn, Ptn = Pb[k % 2], Ptb[k % 2]
            Mn, Mtn = Mb[k % 2], Mtb[k % 2]
            psP = nps()
            mm4(psP, lambda h: Pt[:, h, :], lambda h: P[:, h, :], [P, Pt])
            S.op("act", lambda e: e.activation(out=f2(Pn), in_=psP[:, :], func=AF.Copy), [psP], [Pn])
            if not lastk:
                psPt = nps()
                mm4(psPt, lambda h: P[:, h, :], lambda h: Pt[:, h, :], [P, Pt])
                S.op("act", lambda e: e.activation(out=f2(Ptn), in_=psPt[:, :], func=AF.Copy), [psPt], [Ptn])
            yield
            psM = nps()
            mm4(psM, lambda h: Mt[:, h, :], lambda h: Pn[:, h, :], [Mt, Pn])
            S.op("dve", lambda e: e.tensor_tensor(out=f2(Mn), in0=psM[:, :], in1=f2r(M), op=ALU.add), [psM, M], [Mn])
            psMt = nps()
            mm4(psMt, lambda h: Pn[:, h, :], lambda h: Mt[:, h, :], [Pn, Mt])
            S.op("dve", lambda e: e.tensor_tensor(out=f2(Mtn), in0=psMt[:, :], in1=f2r(Mt), op=ALU.add),
                 [psMt, Mt], [Mtn])
            P, Pt, M, Mt = Pn, Ptn, Mn, Mtn
            yield
        psZ = nps()
        mm4(psZ, lambda h: IpA[:, h, :], lambda h: M[:, h, :], [IpA, M])
        S.op("dve", lambda e: e.tensor_tensor(out=f2(Rr), in0=f2(idb4), in1=psZ[:, :], op=ALU.subtract), [idb4, psZ], [Rr])
        yield
        psC = nps()
        mm4(psC, lambda h: Mt[:, h, :], lambda h: Rr[:, h, :], [Mt, Rr])
        S.op("dve", lambda e: e.tensor_tensor(out=f2(Xr), in0=psC[:, :], in1=f2r(M), op=ALU.add), [psC, M], [Xr])
        yield
        TT = Xr
        psU = nps()
        mm4(psU, lambda h: TT[:, h, :], lambda h: vb[:, h, :], [TT, vb])
        S.op("act", lambda e: e.activation(out=f2(u_sb), in_=psU[:, :], func=AF.Copy), [psU], [u_sb])
        psW = nps()
        mm4(psW, lambda h: kge[:, h, :], lambda h: TT[:, h, :], [TT, kge])
        S.op("act", lambda e: e.activation(out=f2(wT), in_=psW[:, :], func=AF.Copy), [psW], [wT])
        yield
        psWS = nps()
        mm4(psWS, lambda h: wT[:, h, :], lambda h: Sbf[:, h, :], [wT, Sbf])
        S.op("dve", lambda e: e.tensor_tensor(out=f2(vnew), in0=f2(u_sb), in1=psWS[:, :], op=ALU.subtract),
             [u_sb, psWS], [vnew])
        yield
        psO = nps()
        for h in range(4):
            S.op("pe", lambda e, h=h: e.matmul(psO[:, h * 128:(h + 1) * 128], lhsT=Sbf[:, h, :], rhs=qg[:, h, :],
                                               start=True, stop=False), [Sbf, qg], [psO])
            S.op("pe", lambda e, h=h: e.matmul(psO[:, h * 128:(h + 1) * 128], lhsT=vnew[:, h, :], rhs=attnT[:, h, :],
                                               start=False, stop=True), [vnew, attnT], [psO])
        psS = nps()
        mm4(psS, lambda h: kdec[:, h, :], lambda h: vnew[:, h, :], [kdec, vnew])
        for h in range(4):
            S.op("dve", lambda e, h=h: e.scalar_tensor_tensor(
                out=S32[:, h, :], in0=S32[:, h, :], scalar=eGl[:, h:h + 1], in1=psS[:, h * 128:(h + 1) * 128],
                op0=ALU.mult, op1=ALU.add), [S32, eGl, psS], [S32])
        S.op("act", lambda e: e.activation(out=f2(Sbf), in_=f2(S32), func=AF.Copy), [S32], [Sbf])
        S.op("act", lambda e: e.activation(out=f2(sqo), in_=psO[:, :], func=AF.Square), [psO], [sqo])
        yield
        psN = nps()
        S.op("pe", lambda e: e.matmul(psN[:, :], lhsT=onesb[:, :], rhs=f2(sqo), start=True, stop=True), [onesb, sqo], [psN])
        S.op("act", lambda e: e.activation(out=f2(rn), in_=psN[:, :], func=AF.Ln, scale=1.0 / 128.0, bias=eps[:, 0:1]),
             [psN, eps], [rn])
        S.op("act", lambda e: e.activation(out=f2(rn), in_=f2(rn), func=AF.Exp, scale=-0.5), [rn], [rn])
        S.op("dve", lambda e: e.scalar_tensor_tensor(out=f2(y1), in0=psO[:, :], scalar=wgn, in1=f2(rn), op0=ALU.mult,
                                                     op1=ALU.mult), [psO, prmall, rn], [y1])
        S.op("pool", lambda e: e.tensor_tensor(out=yo_[:, :, cs], in0=y1[:, :, :], in1=gz_[:, :, cs], op=ALU.mult),
             [y1, gz_], [yo_])
        if ci == 3:
            t0 = g * 512
            S.dma("sp", YT[256:768, t0:t0 + 512].rearrange("(h p) n -> p h n", p=128), yo_[:, :, :], reads=[yo_])
            if g + 2 < NG:
                load_group(g + 2)
        yield

    load_group(0)
    if NG > 1:
        load_group(1)
    NGRP = 21
    A, a_cnt = chunk(0), 0
    for _ in range(NGRP // 2):
        next(A)
        a_cnt += 1
    for n in range(1, NCH + 1):
        B, b_cnt = (chunk(n) if n < NCH else None), 0
        while a_cnt < NGRP:
            if B is not None and b_cnt < NGRP:
                next(B)
                b_cnt += 1
            next(A)
            a_cnt += 1
        A, a_cnt = B, b_cnt


def host_consts():
    c = np.zeros((128, NCST), np.float32)
    i = np.arange(128)
    c[:, C_ID:C_ID + 128] = np.eye(128, dtype=np.float32)
    c[:, C_UT:C_UT + 128] = (i[:, None] <= i[None, :])
    c[:, C_NTRI:C_NTRI + 128] = -1.0 * (i[:, None] > i[None, :])
    c[:, C_MI:C_MI + 128] = np.where(i[None, :] >= i[:, None], 0.0, NEG)
    c[:, C_MS:C_MS + 128] = np.where(i[None, :] > i[:, None], 0.0, NEG)
    c[:, C_MSL:C_MSL + 128] = np.where(i[None, :] < i[:, None], 0.0, NEG)
    q = np.arange(512)
    for kb in range(4):
        c[:, C_SBM + kb * 512:C_SBM + (kb + 1) * 512] = ((kb * 128 + i)[:, None] < q[None, :])
    return c


def host_params(inp, depth):
    p = np.zeros((depth, 128, NPRM), np.float32)
    for l in range(depth):
        p[l, :, 0:8] = np.asarray(inp["w_norm_mix"][l]).reshape(8, 128).T
        p[l, :, 8:16] = np.asarray(inp["w_norm_ffn"][l]).reshape(8, 128).T
        sc = np.asarray(inp["w_sconv"][l])
        for i in range(2):
            p[l, :, 16 + i * 3:19 + i * 3] = sc[:, i * 128:(i + 1) * 128].T
        gc = np.asarray(inp["w_gdn_conv"][l])
        for m in range(12):
            p[l, :, 22 + m * 4:26 + m * 4] = gc[:, m * 128:(m + 1) * 128].T
        p[l, :, 70] = np.asarray(inp["w_gdn_norm"][l])
        fc = np.asarray(inp["w_ffn_conv"][l])
        for m in range(44):
            p[l, :, 71 + m * 3:74 + m * 3] = fc[:, m * 128:(m + 1) * 128].T
        p[l, 0:4, 203] = 1.0
        p[l, 4:8, 203] = -1.0
        p[l, 0:4, 204] = np.asarray(inp["gdn_dt_bias"][l])
        p[l, 0:4, 205] = np.asarray(inp["gdn_a_log"][l])
    return p


_NC_CACHE = {}


def make_in_maps(inp, xs, depth):
    f = lambda a: np.ascontiguousarray(np.asarray(a, dtype=np.float32))
    common = dict(
        w_in=f(inp["w_mix_in"])[:depth], w_out=f(inp["w_mix_out"])[:depth], w_up=f(inp["w_ffn_up"])[:depth],
        w_dn=f(inp["w_ffn_down"])[:depth], prm=host_params(inp, depth),
        wfin=f(inp["w_norm_final"]).reshape(1, D), cst=host_consts())
    return [dict(common, x=f(xs[i])) for i in range(len(xs))]


def kernel(**inp):
    x = np.asarray(inp["x"], dtype=np.float32)
    B, L, _ = x.shape
    key = (L, DEPTH)
    if key not in _NC_CACHE:
        _NC_CACHE[key] = build(L)
    nc = _NC_CACHE[key]
    xs = [x[i % B] for i in range(8)]
    in_maps = make_in_maps(inp, xs, DEPTH)
    res = run_bass_kernel_spmd(nc, in_maps, core_ids=list(range(8)))
    return np.stack([res.results[i]["out"] for i in range(B)], axis=0).astype(np.float32)
```

```python
import numpy as np
from contextlib import ExitStack
import concourse.bass as bass
import concourse.mybir as mybir
from concourse.bass_utils import run_bass_kernel_spmd

F32 = mybir.dt.float32
BF16 = mybir.dt.bfloat16
F32R = mybir.dt.float32r
ALU = mybir.AluOpType
AF = mybir.ActivationFunctionType

D = 1024
DEPTH = 2
DFF = 2816
DIN = 3592
EPS = 1e-6
NEG = -60000.0
NPRM = 206
C_ID, C_UT, C_NTRI, C_MI, C_MS, C_MSL, C_SBM = 0, 128, 256, 384, 512, 640, 768
NCST = 768 + 2048

EPOCH = 20000


class Buf:
    __slots__ = ("t", "w", "r")

    def __init__(self, t):
        self.t = t
        self.w = None
        self.r = {}

    def __getitem__(self, k):
        return self.t[k]


class _Rec:
    def __getattr__(self, name):
        def f(*a, **kw):
            self.call = (name, a, kw)
            return self
        return f


class Sched:
    def __init__(self, nc, es):
        self.nc = nc
        self.es = es
        self.names = ["pe", "act", "dve", "pool", "sp"]
        self.prog = {e: [] for e in self.names}
        self.sem = {}
        self.val = {}
        self.cur = {}
        self.n = {e: 0 for e in self.names}
        self.seen = {e: {} for e in self.names}
        for e in self.names:
            self._new_epoch(e, 0)
        self.dq = {}
        self.dqi = {}
        for q in ("sp", "pool", "act"):
            keys = []
            for i in range(8):
                k = ("d", q, i)
                self.sem[k] = es.enter_context(nc.semaphore(f"d_{q}_{i}"))
                self.val[k] = 0
                keys.append(k)
            self.dq[q] = keys
            self.dqi[q] = 0
        self.nbuf = 0

    def _new_epoch(self, e, ep):
        k = ("e", e, ep)
        self.sem[k] = self.es.enter_context(self.nc.semaphore(f"s_{e}_{ep}"))
        self.val[k] = 0
        self.cur[e] = k

    def sb(self, shape, dt, name=None):
        self.nbuf += 1
        t = self.es.enter_context(self.nc.sbuf_tensor(f"{name or 'b'}_{self.nbuf}", list(shape), dt))
        return Buf(t)

    def _deps(self, reads, writes):
        deps = {}

        def add(ev):
            if ev is None:
                return
            k, v = ev
            if deps.get(k, 0) < v:
                deps[k] = v

        for b in reads:
            add(b.w)
        for b in writes:
            add(b.w)
            for k, v in b.r.items():
                add((k, v))
        return deps

    def _waits(self, e, deps):
        ws = []
        seen = self.seen[e]
        for k, v in deps.items():
            if seen.get(k, 0) < v:
                seen[k] = v
                ws.append((self.sem[k], v))
        return ws

    def _mark(self, ev, reads, writes):
        k, v = ev
        for b in reads:
            if b.r.get(k, 0) < v:
                b.r[k] = v
        for b in writes:
            b.w = ev
            b.r = {}

    def op(self, e, fn, reads=(), writes=()):
        ws = self._waits(e, self._deps(reads, writes))
        if self.val[self.cur[e]] >= EPOCH:
            self._new_epoch(e, self.cur[e][2] + 1)
        k = self.cur[e]
        self.val[k] += 1
        v = self.val[k]
        sem = self.sem[k]
        if e == "pe":
            self.seen[e][k] = v

        rec = _Rec()
        fn(rec)
        name, a, kw = rec.call

        def thunk(eng, ws=ws, name=name, a=a, kw=kw, sem=sem):
            for s, val in ws:
                eng.wait_ge(s, val)
            getattr(eng, name)(*a, **kw).then_inc(sem, 1)

        self.prog[e].append(thunk)
        self._mark((k, v), reads, writes)

    def dma(self, q, out, in_, reads=(), writes=()):
        deps = self._deps(reads, writes)
        i = self.dqi[q]
        self.dqi[q] = (i + 1) % len(self.dq[q])
        k = self.dq[q][i]
        if deps.get(k, 0) < self.val[k]:
            deps[k] = self.val[k]
        ws = self._waits(q, deps)
        self.val[k] += 16
        v = self.val[k]
        sem = self.sem[k]

        def thunk(eng, ws=ws, sem=sem, out=out, in_=in_):
            for s, val in ws:
                eng.wait_ge(s, val)
            eng.dma_start(out=out, in_=in_).then_inc(sem, 16)

        self.prog[q].append(thunk)
        self._mark((k, v), reads, writes)

    def barrier(self):
        allv = {k: v for k, v in self.val.items() if v > 0}
        for e in self.names:
            ws = self._waits(e, allv)
            if ws:
                def thunk(eng, ws=ws):
                    for s, val in ws:
                        eng.wait_ge(s, val)
                self.prog[e].append(thunk)

    def emit(self):
        nc = self.nc
        self.barrier()
        with nc.Block() as block:
            @block.tensor
            def _(eng):
                for th in self.prog["pe"]:
                    th(eng)

            @block.scalar
            def _(eng):
                for th in self.prog["act"]:
                    th(eng)

            @block.vector
            def _(eng):
                for th in self.prog["dve"]:
                    th(eng)

            @block.gpsimd
            def _(eng):
                for th in self.prog["pool"]:
                    th(eng)

            @block.sync
            def _(eng):
                for th in self.prog["sp"]:
                    th(eng)


def skew_emit(units):
    n = len(units)
    ns = max(len(u) for u in units) if units else 0
    for it in range(n + ns - 1):
        for st in range(ns):
            u = it - st
            if 0 <= u < n and st < len(units[u]) and units[u][st] is not None:
                units[u][st]()


def build(L, depth=DEPTH, phases="ABGC", dbg=(), ext_in=()):
    NT = L // 512
    nc = bass.Bass("TRN2", target_bir_lowering=False)
    es = ExitStack()
    dr = {}

    def dram(name, shape, dt, kind="Internal"):
        if name in dbg:
            kind = "ExternalOutput"
        if name in ext_in:
            kind = "ExternalInput"
        dr[name] = nc.dram_tensor(name, list(shape), dt, kind=kind)
        return dr[name]

    x_in = dram("x", [L, D], F32, "ExternalInput")
    w_in = dram("w_in", [depth, D, DIN], F32, "ExternalInput")
    w_out = dram("w_out", [depth, D, D], F32, "ExternalInput")
    w_up = dram("w_up", [depth, D, 2 * DFF], F32, "ExternalInput")
    w_dn = dram("w_dn", [depth, DFF, D], F32, "ExternalInput")
    prm = dram("prm", [depth, 128, NPRM], F32, "ExternalInput")
    wfin = dram("wfin", [1, D], F32, "ExternalInput")
    cst = dram("cst", [128, NCST], F32, "ExternalInput")
    out = dram("out", [L, D], F32, "ExternalOutput")
    xres = dram("xres", [L, D], F32)
    YT = dram("YT", [D, L], BF16)
    gq = dram("gq", [512, L], BF16)
    gk = dram("gk", [512, L], BF16)
    gz = dram("gz", [512, L], BF16)
    gktm = dram("gktm", [L, 512], BF16)
    gvtm = dram("gvtm", [L, 512], BF16)
    gab = dram("gab", [L, 12], F32)
    sq = dram("sq", [256, L], BF16)
    sk = dram("sk", [256, L], BF16)
    svtm = dram("svtm", [L, 256], BF16)
    if "dbg" in dbg:
        dram("dbg", [16, 128, 512], F32)

    with es:
        S = Sched(nc, es)
        cf = S.sb([128, C_SBM], F32, "cf")
        S.dma("sp", cf[:, :], cst[:, 0:C_SBM], writes=[cf])
        idb = S.sb([128, 128], BF16, "idb")
        S.op("dve", lambda e: e.tensor_copy(out=idb[:, :], in_=cf[:, C_ID:C_ID + 128]), [cf], [idb])
        onesb = S.sb([128, 128], BF16, "onesb")
        S.op("dve", lambda e: e.memset(onesb[:, :], 1.0), [], [onesb])
        onesf = S.sb([128, 128], F32, "onesf")
        S.op("dve", lambda e: e.memset(onesf[:, :], 1.0), [], [onesf])
        psum = [Buf(es.enter_context(nc.psum_tensor(f"ps{i}", [128, 512], F32))) for i in range(8)]
        prmall = S.sb([128, depth, NPRM], F32, "prm")
        for l in range(depth):
            S.dma("sp", prmall[:, l, :], prm[l, :, :], writes=[prmall])

        pctr = [0]

        def nextps(lo=0, hi=8):
            i = lo + pctr[0] % (hi - lo)
            pctr[0] += 1
            return psum[i]

        def rmsnorm_T(es2, xt, nb, wcol, l, hT, W, banks=(4, 8)):
            junk, ss, rstd, xn = W["junk"], W["ss"], W["rstd"], W["xn"]
            for b in range(nb):
                S.op("act", lambda e, b=b: e.activation(out=junk[:, :], in_=xt[:, b, :], func=AF.Square,
                                                         accum_out=ss[:, b:b + 1]), [xt], [junk, ss])
            S.op("act", lambda e: e.activation(out=rstd[:, 0:nb], in_=ss[:, 0:nb], func=AF.Ln,
                                               scale=1.0 / D, bias=W["eps"][:, 0:1]), [ss, W["eps"]], [rstd])
            S.op("act", lambda e: e.activation(out=rstd[:, 0:nb], in_=rstd[:, 0:nb], func=AF.Exp, scale=-0.5),
                 [rstd], [rstd])
            for b in range(nb):
                if b % 2 == 0:
                    S.op("dve", lambda e, b=b: e.tensor_scalar(out=xn[:, b, :], in0=xt[:, b, :], scalar1=rstd[:, b:b + 1],
                                                               scalar2=None, op0=ALU.mult), [xt, rstd], [xn])
                else:
                    S.op("act", lambda e, b=b: e.activation(out=xn[:, b, :], in_=xt[:, b, :], func=AF.Copy,
                                                            scale=rstd[:, b:b + 1]), [xt, rstd], [xn])
            n = nb * 128
            cpb = 1024 // n
            for c0 in range(0, 8, cpb):
                ps = nextps(*banks)
                pv = ps.t[:, :].bitcast(BF16)
                for ci in range(cpb):
                    c = c0 + ci
                    for b in range(nb):
                        S.op("pe", lambda e, c=c, b=b, ci=ci, pv=pv: e.transpose(
                            out=pv[:, ci * n + b * 128: ci * n + (b + 1) * 128],
                            in_=xn[:, b, c * 128:(c + 1) * 128], identity=idb[:, :]), [xn, idb], [ps])
                for ci in range(cpb):
                    c = c0 + ci
                    S.op("act", lambda e, c=c, ci=ci, pv=pv: e.activation(
                        out=hT[:, c, 0:n], in_=pv[:, ci * n:(ci + 1) * n], func=AF.Copy,
                        scale=prmall[:, l, wcol + c:wcol + c + 1]), [ps, prmall], [hT])

        def load_w_bf16(dst, src_ap, nchunk):
            for c in range(nchunk):
                S.dma("pool", dst[:, c, :], src_ap[c * 128:(c + 1) * 128, :], writes=[dst])

        for l in range(depth):
            xsrc = x_in if l == 0 else xres
            if "A" in phases:
                with ExitStack() as pa:
                    es_save = S.es
                    S.es = pa
                    phase_A(S, nc, L, l, xsrc, w_in, prmall, cf, idb, onesb, psum, nextps, rmsnorm_T,
                            load_w_bf16, dr)
                    S.barrier()
                    S.es = es_save
            if "B" in phases:
                with ExitStack() as pa:
                    es_save = S.es
                    S.es = pa
                    phase_SB(S, nc, L, l, cf, psum, nextps, dr)
                    S.barrier()
                    S.es = es_save
            if "G" in phases:
                with ExitStack() as pa:
                    es_save = S.es
                    S.es = pa
                    phase_GDN(S, nc, L, l, prmall, cf, idb, onesb, onesf, psum, nextps, dr)
                    S.barrier()
                    S.es = es_save
            if "C" in phases:
                with ExitStack() as pa:
                    es_save = S.es
                    S.es = pa
                    phase_C(S, nc, L, l, depth, xsrc, w_out, w_up, w_dn, wfin, prmall, idb, psum, nextps,
                            rmsnorm_T, load_w_bf16, dr)
                    S.barrier()
                    S.es = es_save
        S.emit()
    return nc


def phase_C(S, nc, L, l, depth, xsrc, w_out, w_up, w_dn, wfin, prmall, idb, psum, nextps, rmsnorm_T,
            load_w_bf16, dr):
    TT = 256
    nb = 2
    xres, YT, out = dr["xres"], dr["YT"], dr["out"]
    last = (l == depth - 1)
    Wo = S.sb([128, 8, D], BF16, "Wo")
    Wu = S.sb([128, 8, 2 * DFF], BF16, "Wu")
    Wd = S.sb([128, 22, D], BF16, "Wd")
    load_w_bf16(Wo, w_out[l], 8)
    load_w_bf16(Wu, w_up[l], 8)
    load_w_bf16(Wd, w_dn[l], 22)
    xt = S.sb([128, nb, D], F32, "xt")
    yT = S.sb([128, 8, TT], BF16, "yT")
    hT = S.sb([128, 8, TT], BF16, "hT")
    aT = S.sb([128, 22, TT], BF16, "aT")
    Wk = dict(junk=S.sb([128, D], BF16, "junk"), ss=S.sb([128, 4], F32, "ss"), rstd=S.sb([128, 4], F32, "rstd"),
              xn=S.sb([128, nb, D], BF16, "xn"), eps=S.sb([128, 1], F32, "eps"))
    S.op("dve", lambda e: e.memset(Wk["eps"][:, :], EPS), [], [Wk["eps"]])
    halo = S.sb([128, 44, 2], F32, "halo")
    S.op("pool", lambda e: e.memset(halo[:, :, :], 0.0), [], [halo])
    ub = [S.sb([128, TT + 2], F32, f"ub{i}") for i in range(4)]
    cv = [S.sb([128, TT], F32, f"cv{i}") for i in range(4)]
    tmp = [S.sb([128, TT], F32, f"tmp{i}") for i in range(4)]
    one = S.sb([128, 1], F32, "one")
    S.op("dve", lambda e: e.memset(one[:, :], 1.0), [], [one])
    wfb = None
    if last:
        wfb = S.sb([128, D], F32, "wfb")
        S.dma("sp", wfb[:, :], wfin[0:1, :].partition_broadcast(128), writes=[wfb])
    FC = 71
    k = 0
    def outproj_early(t):
        t0 = t * TT
        S.dma("sp", yT[:, :, :], YT[:, t0:t0 + TT].rearrange("(c p) n -> p c n", p=128), writes=[yT])
        for b in range(nb):
            for h in range(2):
                ps = psum[4 + b * 2 + h]
                for c in range(8):
                    S.op("pe", lambda e, ps=ps, c=c, b=b, h=h: e.matmul(
                        ps[:, :], lhsT=yT[:, c, b * 128:(b + 1) * 128], rhs=Wo[:, c, h * 512:(h + 1) * 512],
                        start=(c == 0), stop=(c == 7)), [yT, Wo], [ps])

    outproj_early(0)
    for t in range(L // TT):
        t0 = t * TT
        S.dma("sp", xt[:, :, :], xsrc[t0:t0 + TT, :].rearrange("(b p) d -> p b d", p=128), writes=[xt])
        for b in range(nb):
            for h in range(2):
                ps = psum[4 + b * 2 + h]
                S.op("dve", lambda e, ps=ps, b=b, h=h: e.tensor_tensor(
                    out=xt[:, b, h * 512:(h + 1) * 512], in0=ps[:, :], in1=xt[:, b, h * 512:(h + 1) * 512],
                    op=ALU.add), [ps, xt], [xt])
        rmsnorm_T(None, xt, nb, 8, l, hT, Wk, banks=(0, 4))
        units = []
        for j in range(22):
            ug, uv = ub[(2 * j) % 4], ub[(2 * j + 1) % 4]
            cg, cvv = cv[(2 * j) % 4], cv[(2 * j + 1) % 4]
            t1, t2 = tmp[(2 * j) % 4], tmp[(2 * j + 1) % 4]

            def s0(j=j, ug=ug, uv=uv):
                for m, u in ((j, ug), (j + 22, uv)):
                    ps = nextps(0, 4)
                    for c in range(8):
                        S.op("pe", lambda e, ps=ps, c=c, m=m: e.matmul(
                            ps[:, 0:TT], lhsT=Wu[:, c, m * 128:(m + 1) * 128], rhs=hT[:, c, :],
                            start=(c == 0), stop=(c == 7)), [Wu, hT], [ps])
                    S.op("act", lambda e, ps=ps, u=u: e.activation(out=u[:, 2:TT + 2], in_=ps[:, 0:TT], func=AF.Copy),
                         [ps], [u])
                    S.op("pool", lambda e, u=u, m=m: e.tensor_copy(out=u[:, 0:2], in_=halo[:, m, :]), [halo], [u])

            def s1(j=j, ug=ug, uv=uv, cg=cg, cvv=cvv):
                for m, u, c_ in ((j, ug, cg), (j + 22, uv, cvv)):
                    pc = FC + m * 3
                    S.op("dve", lambda e, u=u, c_=c_, pc=pc: e.tensor_scalar(
                        out=c_[:, :], in0=u[:, 0:TT], scalar1=prmall[:, l, pc:pc + 1], scalar2=None, op0=ALU.mult),
                        [u, prmall], [c_])
                    S.op("dve", lambda e, u=u, c_=c_, pc=pc: e.scalar_tensor_tensor(
                        out=c_[:, :], in0=u[:, 1:TT + 1], scalar=prmall[:, l, pc + 1:pc + 2], in1=c_[:, :],
                        op0=ALU.mult, op1=ALU.add), [u, prmall, c_], [c_])
                    S.op("dve", lambda e, u=u, c_=c_, pc=pc: e.scalar_tensor_tensor(
                        out=c_[:, :], in0=u[:, 2:TT + 2], scalar=prmall[:, l, pc + 2:pc + 3], in1=c_[:, :],
                        op0=ALU.mult, op1=ALU.add), [u, prmall, c_], [c_])
                    S.op("pool", lambda e, u=u, m=m: e.tensor_copy(out=halo[:, m, :], in_=u[:, TT:TT + 2]), [u], [halo])

            def s2(cg=cg, cvv=cvv, t1=t1, t2=t2):
                S.op("act", lambda e: e.activation(out=t1[:, :], in_=cg[:, :], func=AF.Exp, scale=-1.0), [cg], [t1])
                S.op("act", lambda e: e.activation(out=t1[:, :], in_=t1[:, :], func=AF.Ln, bias=one[:, 0:1]), [t1, one], [t1])
                S.op("act", lambda e: e.activation(out=t1[:, :], in_=t1[:, :], func=AF.Exp, scale=-1.0), [t1], [t1])
                S.op("pool", lambda e: e.tensor_tensor(out=t2[:, :], in0=cg[:, :], in1=cvv[:, :], op=ALU.mult),
                     [cg, cvv], [t2])

            def s3(j=j, t1=t1, t2=t2):
                S.op("pool", lambda e: e.tensor_tensor(out=aT[:, j, :], in0=t1[:, :], in1=t2[:, :], op=ALU.mult),
                     [t1, t2], [aT])

            units.append([s0, s1, s2, s3])
            if j == 8 and t + 1 < L // TT:
                units.append([lambda t=t: outproj_early(t + 1)])
        skew_emit(units)
        for b in range(nb):
            for h in range(2):
                ps = nextps(0, 4)
                for j in range(22):
                    S.op("pe", lambda e, ps=ps, j=j, b=b, h=h: e.matmul(
                        ps[:, :], lhsT=aT[:, j, b * 128:(b + 1) * 128], rhs=Wd[:, j, h * 512:(h + 1) * 512],
                        start=(j == 0), stop=(j == 21)), [aT, Wd], [ps])
                S.op("dve", lambda e, ps=ps, b=b, h=h: e.tensor_tensor(
                    out=xt[:, b, h * 512:(h + 1) * 512], in0=ps[:, :], in1=xt[:, b, h * 512:(h + 1) * 512],
                    op=ALU.add), [ps, xt], [xt])
        if not last:
            S.dma("sp", xres[t0:t0 + TT, :].rearrange("(b p) d -> p b d", p=128), xt[:, :, :], reads=[xt])
        else:
            junk, ss, rstd = Wk["junk"], Wk["ss"], Wk["rstd"]
            for b in range(nb):
                S.op("act", lambda e, b=b: e.activation(out=junk[:, :], in_=xt[:, b, :], func=AF.Square,
                                                         accum_out=ss[:, b:b + 1]), [xt], [junk, ss])
            S.op("act", lambda e: e.activation(out=rstd[:, 0:nb], in_=ss[:, 0:nb], func=AF.Ln,
                                               scale=1.0 / D, bias=Wk["eps"][:, 0:1]), [ss, Wk["eps"]], [rstd])
            S.op("act", lambda e: e.activation(out=rstd[:, 0:nb], in_=rstd[:, 0:nb], func=AF.Exp, scale=-0.5),
                 [rstd], [rstd])
            for b in range(nb):
                S.op("dve", lambda e, b=b: e.scalar_tensor_tensor(
                    out=xt[:, b, :], in0=xt[:, b, :], scalar=rstd[:, b:b + 1], in1=wfb[:, :],
                    op0=ALU.mult, op1=ALU.mult), [xt, rstd, wfb], [xt])
            S.dma("sp", out[t0:t0 + TT, :].rearrange("(b p) d -> p b d", p=128), xt[:, :, :], reads=[xt])


def phase_A(S, nc, L, l, xsrc, w_in, prmall, cf, idb, onesb, psum, nextps, rmsnorm_T, load_w_bf16, dr):
    TT = 512
    nb = 4
    YT, gq, gk, gz, gktm, gvtm, gab, sq, sk, svtm = (dr[n] for n in
                                                       ("YT", "gq", "gk", "gz", "gktm", "gvtm", "gab", "sq", "sk", "svtm"))
    Win = S.sb([128, 8, DIN], BF16, "Win")
    load_w_bf16(Win, w_in[l], 8)
    xt = S.sb([128, nb, D], F32, "xt")
    hT = S.sb([128, 8, TT], BF16, "hT")
    xn = S.sb([128, nb, D], BF16, "xn")
    ss, rstd = S.sb([128, 4], F32, "ss"), S.sb([128, 4], F32, "rstd")
    eps = S.sb([128, 1], F32, "eps")
    S.op("dve", lambda e: e.memset(eps[:, :], EPS), [], [eps])
    one = S.sb([128, 1], F32, "one")
    S.op("dve", lambda e: e.memset(one[:, :], 1.0), [], [one])
    lnq = S.sb([128, 1], F32, "lnq")
    S.op("dve", lambda e: e.memset(lnq[:, :], -0.5 * float(np.log(128.0))), [], [lnq])
    na = S.sb([128, 1], F32, "na")
    S.op("act", lambda e: e.activation(out=na[:, :], in_=prmall[:, l, 205:206], func=AF.Exp), [prmall], [na])
    S.op("dve", lambda e: e.tensor_scalar(out=na[:, :], in0=na[:, :], scalar1=-1.0, scalar2=None, op0=ALU.mult),
         [na], [na])
    rbuf = [S.sb([128, TT + 3], F32, f"rbuf{m}") for m in range(12)]
    pbuf = [S.sb([128, TT + 2], F32, f"pbuf{i}") for i in range(2)]
    for b_ in rbuf:
        S.op("pool", lambda e, b_=b_: e.memset(b_[:, 0:3], 0.0), [], [b_])
    for b_ in pbuf:
        S.op("pool", lambda e, b_=b_: e.memset(b_[:, 0:2], 0.0), [], [b_])

    def fam(n, dt, name, shape=(128, TT)):
        return [S.sb(list(shape), dt, f"{name}{i}") for i in range(n)]

    cvb, e1, sfl, rs = fam(4, F32, "cvb"), fam(3, F32, "e1"), fam(5, F32, "sfl"), fam(3, F32, "rs")
    evA, evB, evC = fam(5, F32, "evA"), fam(2, F32, "evB"), fam(2, F32, "evC")
    sqb, obf = fam(3, BF16, "sqb"), fam(9, BF16, "obf")
    tm = fam(3, BF16, "tm", (128, 4, 128))
    ab = fam(4, F32, "ab")
    gabt = S.sb([128, 4, 12], F32, "gabt")
    idf = cf
    nt = L // TT

    def xload(t):
        S.dma("sp", xt[:, :, :], xsrc[t * TT:(t + 1) * TT, :].rearrange("(b p) d -> p b d", p=128), writes=[xt])

    def mm_tile(c0, ncols):
        ps = nextps(0, 4)
        for c in range(8):
            S.op("pe", lambda e, ps=ps, c=c: e.matmul(ps[0:ncols, 0:TT], lhsT=Win[:, c, c0:c0 + ncols], rhs=hT[:, c, :],
                                                      start=(c == 0), stop=(c == 7)), [Win, hT], [ps])
        return ps

    def evac(ps, dst_ap, dst):
        S.op("act", lambda e: e.activation(out=dst_ap, in_=ps[:, 0:TT], func=AF.Copy), [ps], [dst])

    def sigmoid(src, e_):
        S.op("act", lambda e: e.activation(out=e_[:, :], in_=src[:, 0:TT], func=AF.Exp, scale=-1.0), [src], [e_])
        S.op("act", lambda e: e.activation(out=e_[:, :], in_=e_[:, :], func=AF.Ln, bias=one[:, 0:1]), [e_, one], [e_])
        S.op("act", lambda e: e.activation(out=e_[:, :], in_=e_[:, :], func=AF.Exp, scale=-1.0), [e_], [e_])

    def conv(src, cv_, pc, K):
        S.op("dve", lambda e: e.tensor_scalar(out=cv_[:, :], in0=src[:, 0:TT], scalar1=prmall[:, l, pc:pc + 1],
                                              scalar2=None, op0=ALU.mult), [src, prmall], [cv_])
        for kk in range(1, K):
            S.op("dve", lambda e, kk=kk: e.scalar_tensor_tensor(
                out=cv_[:, :], in0=src[:, kk:TT + kk], scalar=prmall[:, l, pc + kk:pc + kk + 1], in1=cv_[:, :],
                op0=ALU.mult, op1=ALU.add), [src, prmall, cv_], [cv_])
        S.op("pool", lambda e: e.tensor_copy(out=src[:, 0:K - 1], in_=src[:, TT:TT + K - 1]), [src], [src])

    def tr_pe(o, bank):
        pv = bank.t[:, :].bitcast(BF16)
        for b in range(4):
            S.op("pe", lambda e, b=b: e.transpose(out=pv[:, b * 128:(b + 1) * 128], in_=o[:, b * 128:(b + 1) * 128],
                                                  identity=idb[:, :]), [o, idb], [bank])

    def tr_ev(bank, t_, dst_ap):
        pv = bank.t[:, :].bitcast(BF16)
        S.op("act", lambda e: e.activation(out=t_[:, :, :], in_=pv[:, 0:512].rearrange("p (b f) -> p b f", b=4),
                                           func=AF.Copy), [bank], [t_])
        S.dma("sp", dst_ap, t_[:, :, :], reads=[t_])

    units = []

    def add(stages):
        units.append(stages + [None] * (10 - len(stages)))

    for t in range(nt):
        t0 = t * TT
        def prologue(t=t):
            for b in range(nb):
                S.op("act", lambda e, b=b: e.activation(out=xn[:, b, :], in_=xt[:, b, :], func=AF.Square,
                                                         accum_out=ss[:, b:b + 1]), [xt], [xn, ss])
            S.op("act", lambda e: e.activation(out=rstd[:, 0:nb], in_=ss[:, 0:nb], func=AF.Ln, scale=1.0 / D,
                                               bias=eps[:, 0:1]), [ss, eps], [rstd])
            S.op("act", lambda e: e.activation(out=rstd[:, 0:nb], in_=rstd[:, 0:nb], func=AF.Exp, scale=-0.5), [rstd], [rstd])
            for b in range(nb):
                if b % 2 == 0:
                    S.op("dve", lambda e, b=b: e.tensor_scalar(out=xn[:, b, :], in0=xt[:, b, :], scalar1=rstd[:, b:b + 1],
                                                               scalar2=None, op0=ALU.mult), [xt, rstd], [xn])
                else:
                    S.op("act", lambda e, b=b: e.activation(out=xn[:, b, :], in_=xt[:, b, :], func=AF.Copy,
                                                            scale=rstd[:, b:b + 1]), [xt, rstd], [xn])
            if t + 1 < nt:
                xload(t + 1)
            for c0 in range(0, 8, 2):
                ps = nextps(0, 4)
                pv = ps.t[:, :].bitcast(BF16)
                for ci in range(2):
                    c = c0 + ci
                    for b in range(nb):
                        S.op("pe", lambda e, c=c, b=b, ci=ci: e.transpose(
                            out=pv[:, ci * 512 + b * 128: ci * 512 + (b + 1) * 128],
                            in_=xn[:, b, c * 128:(c + 1) * 128], identity=idb[:, :]), [xn, idb], [ps])
                for ci in range(2):
                    c = c0 + ci
                    S.op("act", lambda e, c=c, ci=ci: e.activation(
                        out=hT[:, c, :], in_=pv[:, ci * 512:(ci + 1) * 512], func=AF.Copy,
                        scale=prmall[:, l, c:c + 1]), [ps, prmall], [hT])
        add([prologue])
        for i in range(2):
            u = len(units)
            a_, b_, c_, cv_, o = evA[u % 5], evB[i], evC[i], cvb[u % 4], obf[u % 9]
            pb = pbuf[i]

            def s0(i=i, a_=a_, b_=b_, c_=c_):
                evac(mm_tile(256 + i * 128, 128), a_[:, :], a_)
                evac(mm_tile(512 + i * 128, 128), b_[:, :], b_)
                evac(mm_tile(i * 128, 128), c_[:, :], c_)

            def s1(i=i, a_=a_, b_=b_, cv_=cv_, pb=pb):
                S.op("dve", lambda e: e.tensor_tensor(out=pb[:, 2:TT + 2], in0=a_[:, :], in1=b_[:, :], op=ALU.mult),
                     [a_, b_], [pb])
                conv(pb, cv_, 16 + i * 3, 3)

            def s2(i=i, c_=c_, cv_=cv_, o=o, t0=t0):
                S.op("dve", lambda e: e.tensor_tensor(out=o[:, :], in0=c_[:, :], in1=cv_[:, :], op=ALU.mult), [c_, cv_], [o])
                S.dma("sp", YT[i * 128:(i + 1) * 128, t0:t0 + TT], o[:, :], reads=[o])
            add([s0, s1, s2])
        for m in range(12):
            u = len(units)
            h = m % 4
            rb, cv_, e_, s_, q_, r_, o, t_ = rbuf[m], cvb[u % 4], e1[u % 3], sfl[u % 5], sqb[u % 3], rs[u % 3], obf[u % 9], tm[u % 3]
            pss, ptr = psum[4 + u % 2], psum[6 + u % 2]
            qk = m < 8

            def s0(m=m, rb=rb):
                evac(mm_tile(768 + m * 128, 128), rb[:, 3:TT + 3], rb)

            def s1(m=m, rb=rb, cv_=cv_):
                conv(rb, cv_, 22 + m * 4, 4)

            def s2(cv_=cv_, e_=e_):
                sigmoid(cv_, e_)

            def s3(cv_=cv_, e_=e_, dst=(s_ if qk else o)):
                S.op("dve", lambda e: e.tensor_tensor(out=dst[:, :], in0=cv_[:, :], in1=e_[:, :], op=ALU.mult),
                     [cv_, e_], [dst])
            st = [s0, s1, s2, s3]
            if qk:
                def s4(s_=s_, q_=q_):
                    S.op("act", lambda e: e.activation(out=q_[:, :], in_=s_[:, :], func=AF.Square), [s_], [q_])

                def s5(q_=q_, pss=pss):
                    S.op("pe", lambda e: e.matmul(pss[:, :], lhsT=onesb[:, :], rhs=q_[:, :], start=True, stop=True),
                         [onesb, q_], [pss])

                def s6(m=m, pss=pss, r_=r_):
                    S.op("act", lambda e: e.activation(out=r_[:, :], in_=pss[:, :], func=AF.Ln, bias=eps[:, 0:1]),
                         [pss, eps], [r_])
                    if m < 4:
                        S.op("act", lambda e: e.activation(out=r_[:, :], in_=r_[:, :], func=AF.Exp, scale=-0.5,
                                                           bias=lnq[:, 0:1]), [r_, lnq], [r_])
                    else:
                        S.op("act", lambda e: e.activation(out=r_[:, :], in_=r_[:, :], func=AF.Exp, scale=-0.5), [r_], [r_])

                def s7(m=m, h=h, s_=s_, r_=r_, o=o, t0=t0):
                    S.op("pool", lambda e: e.tensor_tensor(out=o[:, :], in0=s_[:, :], in1=r_[:, :], op=ALU.mult), [s_, r_], [o])
                    dst = gq if m < 4 else gk
                    S.dma("sp", dst[h * 128:(h + 1) * 128, t0:t0 + TT], o[:, :], reads=[o])
                st += [s4, s5, s6, s7]
            else:
                st += [None, None, None, None]
            if m >= 4:
                dstT = gktm if m < 8 else gvtm

                def s8(o=o, ptr=ptr):
                    tr_pe(o, ptr)

                def s9(h=h, ptr=ptr, t_=t_, dstT=dstT, t0=t0):
                    tr_ev(ptr, t_, dstT[t0:t0 + TT, h * 128:(h + 1) * 128].rearrange("(b p) f -> p b f", p=128))
                st += [s8, s9]
            add(st)
        for h in range(4):
            u = len(units)
            a_, e_, o = evA[u % 5], e1[u % 3], obf[u % 9]

            def s0(h=h, a_=a_):
                evac(mm_tile(2304 + h * 128, 128), a_[:, :], a_)

            def s2(a_=a_, e_=e_):
                sigmoid(a_, e_)

            def s3(h=h, a_=a_, e_=e_, o=o, t0=t0):
                S.op("dve", lambda e: e.tensor_tensor(out=o[:, :], in0=a_[:, :], in1=e_[:, :], op=ALU.mult), [a_, e_], [o])
                S.dma("sp", gz[h * 128:(h + 1) * 128, t0:t0 + TT], o[:, :], reads=[o])
            add([s0, None, s2, s3])
        u = len(units)
        e8, ln8, rc8, g8 = ab
        ptr = psum[6 + u % 2]

        def s0():
            ps = mm_tile(2816, 8)
            S.op("act", lambda e: e.activation(out=e8[0:8, :], in_=ps[0:8, :], func=AF.Exp,
                                               scale=prmall[0:8, l, 203:204], bias=prmall[0:8, l, 204:205]),
                 [ps, prmall], [e8])

        def s1():
            S.op("dve", lambda e: e.tensor_scalar(out=e8[0:8, :], in0=e8[0:8, :], scalar1=1.0, scalar2=None, op0=ALU.add),
                 [e8], [e8])

        def s2():
            S.op("act", lambda e: e.activation(out=ln8[0:8, :], in_=e8[0:8, :], func=AF.Ln), [e8], [ln8])

        def s3():
            S.op("dve", lambda e: e.reciprocal(out=rc8[0:8, :], in_=e8[0:8, :]), [e8], [rc8])
            S.op("dve", lambda e: e.tensor_scalar(out=g8[0:8, :], in0=ln8[0:8, :], scalar1=na[0:8, 0:1], scalar2=None,
                                                  op0=ALU.mult), [ln8, na], [g8])

        def s8(ptr=ptr):
            for b in range(4):
                S.op("pe", lambda e, b=b: e.transpose(out=ptr[:, b * 16:b * 16 + 8], in_=g8[0:8, b * 128:(b + 1) * 128],
                                                      identity=idf[0:8, C_ID:C_ID + 8]), [g8, idf], [ptr])
                S.op("pe", lambda e, b=b: e.transpose(out=ptr[:, b * 16 + 8:b * 16 + 16], in_=rc8[0:8, b * 128:(b + 1) * 128],
                                                      identity=idf[0:8, C_ID:C_ID + 8]), [rc8, idf], [ptr])

        def s9(ptr=ptr, t0=t0):
            pA = ptr.t[:, 0:64].rearrange("p (b f) -> p b f", b=4)
            S.op("dve", lambda e: e.tensor_copy(out=gabt[:, :, 0:4], in_=pA[:, :, 0:4]), [ptr], [gabt])
            S.op("dve", lambda e: e.tensor_copy(out=gabt[:, :, 4:8], in_=pA[:, :, 12:16]), [ptr], [gabt])
            S.op("dve", lambda e: e.tensor_copy(out=gabt[:, :, 8:12], in_=pA[:, :, 4:8]), [ptr], [gabt])
            S.dma("sp", gab[t0:t0 + TT, :].rearrange("(b p) f -> p b f", p=128), gabt[:, :, :], reads=[gabt])
        add([s0, s1, s2, s3, None, None, None, None, s8, s9])
        for i in range(2):
            for dst, c0 in ((sq, 2824), (sk, 3080)):
                u = len(units)
                o = obf[u % 9]

                def s0(i=i, dst=dst, c0=c0, o=o, t0=t0):
                    evac(mm_tile(c0 + i * 128, 128), o[:, :], o)
                    S.dma("sp", dst[i * 128:(i + 1) * 128, t0:t0 + TT], o[:, :], reads=[o])
                add([s0])
            u = len(units)
            o, t_, ptr = obf[u % 9], tm[u % 3], psum[6 + u % 2]

            def s0(i=i, o=o):
                evac(mm_tile(3336 + i * 128, 128), o[:, :], o)

            def s8(o=o, ptr=ptr):
                tr_pe(o, ptr)

            def s9(i=i, ptr=ptr, t_=t_, t0=t0):
                tr_ev(ptr, t_, svtm[t0:t0 + TT, i * 128:(i + 1) * 128].rearrange("(b p) f -> p b f", p=128))
            add([s0, None, None, None, None, None, None, None, s8, s9])
    xload(0)
    skew_emit(units)


def phase_SB(S, nc, L, l, cf, psum, nextps, dr):
    YT, sq, sk, svtm, cst = dr["YT"], dr["sq"], dr["sk"], dr["svtm"], dr["cst"]
    NB = L // 128
    NT = L // 512
    M01 = S.sb([128, 4, 512], BF16, "M01")
    S.dma("pool", M01[:, :, :], cst[:, C_SBM:C_SBM + 2048].rearrange("p (k q) -> p k q", k=4), writes=[M01])
    ntri = S.sb([128, 128], BF16, "ntri")
    S.op("dve", lambda e: e.tensor_copy(out=ntri[:, :], in_=cf[:, C_NTRI:C_NTRI + 128]), [cf], [ntri])
    nones = S.sb([128, 128], BF16, "nones")
    S.op("dve", lambda e: e.memset(nones[:, :], -1.0), [], [nones])
    one = S.sb([128, 1], F32, "one")
    S.op("dve", lambda e: e.memset(one[:, :], 1.0), [], [one])
    KT = [S.sb([128, L], BF16, f"KT{i}") for i in range(2)]
    QT = [S.sb([128, L], BF16, f"QTz{h}") for h in range(4)]
    Vp = [S.sb([128, NB, 128], BF16, f"Vp{i}") for i in range(2)]
    for i in range(2):
        S.dma("sp", KT[i][:, :], sk[i * 128:(i + 1) * 128, :], writes=[KT[i]])
        for hh in range(2):
            qz = QT[2 * i + hh]
            oth = (1 - hh) * 64
            S.op("pool" if hh else "dve", lambda e, qz=qz, oth=oth: e.memset(qz[oth:oth + 64, :], 0.0), [], [qz])
            S.dma("sp", qz[hh * 64:hh * 64 + 64, :], sq[i * 128 + hh * 64:i * 128 + hh * 64 + 64, :], writes=[qz])
        for n0 in range(0, NB, 8):
            S.dma("sp", Vp[i][:, n0:n0 + 8, :],
                  svtm[n0 * 128:(n0 + 8) * 128, i * 128:(i + 1) * 128].rearrange("(n p) f -> p n f", p=128), writes=[Vp[i]])
    ebuf = [S.sb([128, 512], F32, f"eb{i}") for i in range(3)]
    spb = [S.sb([128, 512], BF16, f"spb{i}") for i in range(4)]
    lbuf = [S.sb([128, 512], F32, f"lb{i}") for i in range(3)]
    abuf = [S.sb([128, 512], BF16, f"ab{i}") for i in range(4)]
    Sacc = [S.sb([128, 512], BF16, f"Sacc{i}") for i in range(3)]
    osb = [S.sb([128, 512], BF16, f"osb{i}") for i in range(2)]
    units = []
    on = 0
    for h in range(4):
        for T in range(NT):
            nkb = 4 * T + 4
            for idx, kb in enumerate(range(nkb - 1, -1, -1)):
                u = len(units)
                units.append(dict(u=u, h=h, i=h // 2, r0=(h % 2) * 64, T=T, idx=idx, kb=kb, diag=kb >= 4 * T, kbi=kb - 4 * T,
                                  on=on, e_=ebuf[u % 3], sp_=spb[u % 4], lb_=lbuf[u % 3], a_=abuf[u % 4],
                                  zps=psum[u % 3], tps=psum[3 + u % 3], ops_=psum[6 + on % 2], o_=osb[on % 2]))
            on += 1
    st = dict(sa=0)

    def S1(U):
        i, r0, T, kb, zps, e_, sp_ = U["i"], U["r0"], U["T"], U["kb"], U["zps"], U["e_"], U["sp_"]
        qz = QT[U["h"]]
        S.op("pe", lambda e: e.matmul(zps[:, :], lhsT=KT[i][:, kb * 128:(kb + 1) * 128],
                                      rhs=qz[:, T * 512:(T + 1) * 512], start=True, stop=True),
             [KT[i], qz], [zps])
        S.op("act", lambda e: e.activation(out=e_[:, :], in_=zps[:, :], func=AF.Exp, scale=0.125), [zps], [e_])
        S.op("act", lambda e: e.activation(out=sp_[:, :], in_=e_[:, :], func=AF.Ln, bias=1.0), [e_], [sp_])
        if U["diag"]:
            S.op("pool", lambda e: e.tensor_tensor(out=sp_[:, :], in0=sp_[:, :], in1=M01[:, U["kbi"], :], op=ALU.mult),
                 [sp_, M01], [sp_])

    def S2(U):
        idx, kb, zps, tps, sp_, lb_ = U["idx"], U["kb"], U["zps"], U["tps"], U["sp_"], U["lb_"]
        S.op("pe", lambda e: e.matmul(tps[:, :], lhsT=ntri[:, :], rhs=sp_[:, :], start=True, stop=(idx == 0)),
             [ntri, sp_], [tps])
        if idx == 0:
            st["sa"] = 0
        else:
            sc = Sacc[st["sa"] % 3]
            S.op("pe", lambda e: e.matmul(tps[:, :], lhsT=nones[:, :], rhs=sc[:, :], start=False, stop=True),
                 [nones, sc], [tps])
        if kb > 0:
            if idx == 0:
                sn = Sacc[st["sa"] % 3]
                S.op("pool", lambda e: e.tensor_copy(out=sn[:, :], in_=sp_[:, :]), [sp_], [sn])
            else:
                so, sn = Sacc[st["sa"] % 3], Sacc[(st["sa"] + 1) % 3]
                S.op("pool", lambda e: e.tensor_tensor(out=sn[:, :], in0=so[:, :], in1=sp_[:, :], op=ALU.add),
                     [so, sp_], [sn])
                st["sa"] += 1
        S.op("dve", lambda e: e.scalar_tensor_tensor(out=lb_[:, :], in0=zps[:, :], scalar=0.125, in1=sp_[:, :],
                                                     op0=ALU.mult, op1=ALU.subtract), [zps, sp_], [lb_])

    def S3(U):
        i, r0, h, T, idx, kb = U["i"], U["r0"], U["h"], U["T"], U["idx"], U["kb"]
        tps, lb_, a_, ops_, o_ = U["tps"], U["lb_"], U["a_"], U["ops_"], U["o_"]
        S.op("dve", lambda e: e.tensor_tensor(out=lb_[:, :], in0=tps[:, :], in1=lb_[:, :], op=ALU.add), [tps, lb_], [lb_])
        S.op("act", lambda e: e.activation(out=a_[:, :], in_=lb_[:, :], func=AF.Exp), [lb_], [a_])
        if U["diag"]:
            S.op("pool", lambda e: e.tensor_tensor(out=a_[:, :], in0=a_[:, :], in1=M01[:, U["kbi"], :], op=ALU.mult),
                 [a_, M01], [a_])

    def S4(U):
        i, r0, h, T, idx, kb = U["i"], U["r0"], U["h"], U["T"], U["idx"], U["kb"]
        a_, ops_, o_ = U["a_"], U["ops_"], U["o_"]
        S.op("pe", lambda e: e.matmul(ops_[:, :], lhsT=Vp[i][:, kb, :], rhs=a_[:, :], start=(idx == 0),
                                      stop=(kb == 0)), [Vp[i], a_], [ops_])
        if kb == 0:
            S.op("act", lambda e: e.activation(out=o_[r0:r0 + 64, :], in_=ops_[r0:r0 + 64, :], func=AF.Copy), [ops_], [o_])
            S.dma("sp", YT[768 + h * 64:768 + (h + 1) * 64, T * 512:(T + 1) * 512], o_[r0:r0 + 64, :], reads=[o_])

    skew_emit([[lambda U=U: S1(U), lambda U=U: S2(U), lambda U=U: S3(U), lambda U=U: S4(U)] for U in units])


def phase_GDN(S, nc, L, l, prmall, cf, idb, onesb, onesf, psum, nextps, dr):
    YT, gq, gk, gz, gktm, gvtm, gab = (dr[n] for n in ("YT", "gq", "gk", "gz", "gktm", "gvtm", "gab"))
    NG = L // 512
    NCH = L // 128
    UT = cf[:, C_UT:C_UT + 128]
    idf = cf[:, C_ID:C_ID + 128]
    MI, MS, MSL = cf[:, C_MI:C_MI + 128], cf[:, C_MS:C_MS + 128], cf[:, C_MSL:C_MSL + 128]
    idb4 = S.sb([128, 4, 128], F32, "idb4")
    for h in range(4):
        S.op("dve", lambda e, h=h: e.tensor_copy(out=idb4[:, h, :], in_=idf), [cf], [idb4])
    eps = S.sb([128, 1], F32, "eps")
    S.op("dve", lambda e: e.memset(eps[:, :], EPS), [], [eps])

    def B4(name, dt=BF16):
        return S.sb([128, 4, 128], dt, name)

    qTg = [S.sb([128, 4, 512], BF16, f"qTg{i}") for i in range(2)]
    kTg = [S.sb([128, 4, 512], BF16, f"kTg{i}") for i in range(2)]
    gzg = [S.sb([128, 4, 512], BF16, f"gzg{i}") for i in range(2)]
    ktmg = [S.sb([128, 4, 512], BF16, f"ktmg{i}") for i in range(2)]
    vtmg = [S.sb([128, 4, 512], BF16, f"vtmg{i}") for i in range(2)]
    gabg = [S.sb([128, 4, 12], F32, f"gabg{i}") for i in range(2)]
    ygo = [S.sb([128, 4, 512], BF16, f"ygo{i}") for i in range(2)]
    sets = []
    for si in range(2):
        W = {}
        for n in ("Gc", "nGc", "R2", "eG", "bEG", "dGl", "eGlmG", "eGl"):
            W[n] = S.sb([128, 4], F32, f"{n}{si}")
        for n in ("diag1", "diag2", "arg1", "arg2", "arg3", "E1", "E2", "E3", "EG", "u_sb", "rn", "y1"):
            W[n] = B4(f"{n}{si}", F32)
        for n in ("AT", "An", "P0", "P1", "Pt0", "Pt1", "M0", "M1", "Mt0", "Mt1", "vb", "kge", "IpA", "Rr", "Xr"):
            W[n] = B4(f"{n}{si}", F32R)
        for n in ("attnT", "kdec", "wT", "qg", "vnew", "sqo"):
            W[n] = B4(f"{n}{si}")
        sets.append(W)
    S32, Sbf = B4("S32", F32), B4("Sbf")
    S.op("pool", lambda e: e.memset(S32[:, :, :], 0.0), [], [S32])
    S.op("pool", lambda e: e.memset(Sbf[:, :, :], 0.0), [], [Sbf])
    wgn = prmall[:, l, 70:71]
    bctr = [0, 0]

    def f2(b):
        return b[:, :, :].rearrange("p h c -> p (h c)")

    def f2r(b):
        return b[:, :, :].bitcast(F32).rearrange("p h c -> p (h c)")

    def load_group(g):
        t0 = g * 512
        i = g % 2
        S.dma("sp", qTg[i][:, :, :], gq[:, t0:t0 + 512].rearrange("(h p) n -> p h n", p=128), writes=[qTg[i]])
        S.dma("sp", kTg[i][:, :, :], gk[:, t0:t0 + 512].rearrange("(h p) n -> p h n", p=128), writes=[kTg[i]])
        S.dma("sp", gzg[i][:, :, :], gz[:, t0:t0 + 512].rearrange("(h p) n -> p h n", p=128), writes=[gzg[i]])
        S.dma("sp", ktmg[i][:, :, :], gktm[t0:t0 + 512, :].rearrange("(n p) f -> p n f", p=128), writes=[ktmg[i]])
        S.dma("sp", vtmg[i][:, :, :], gvtm[t0:t0 + 512, :].rearrange("(n p) f -> p n f", p=128), writes=[vtmg[i]])
        S.dma("sp", gabg[i][:, :, :], gab[t0:t0 + 512, :].rearrange("(n p) f -> p n f", p=128), writes=[gabg[i]])

    def mm4(ps, lhs_fn, rhs_fn, reads):
        for h in range(4):
            S.op("pe", lambda e, h=h: e.matmul(ps[:, h * 128:(h + 1) * 128], lhsT=lhs_fn(h), rhs=rhs_fn(h), start=True,
                                               stop=True), reads, [ps])

    def chunk(n):
        g, ci = n // 4, n % 4
        gi, si = g % 2, n % 2
        W = sets[si]

        def nps():
            i = si * 4 + bctr[si] % 4
            bctr[si] += 1
            return psum[i]

        qT_, kT_, gz_, ktm_, vtm_, gab_, yo_ = qTg[gi], kTg[gi], gzg[gi], ktmg[gi], vtmg[gi], gabg[gi], ygo[gi]
        cs = slice(ci * 128, (ci + 1) * 128)
        graw, beta, lnb = gab_[:, ci, 0:4], gab_[:, ci, 4:8], gab_[:, ci, 8:12]
        Gc, nGc, R2, eG, bEG, dGl, eGlmG, eGl = (W[k] for k in ("Gc", "nGc", "R2", "eG", "bEG", "dGl", "eGlmG", "eGl"))
        diag1, diag2, arg1, arg2, arg3 = (W[k] for k in ("diag1", "diag2", "arg1", "arg2", "arg3"))
        E1, E2, E3, EG, u_sb, rn, y1 = (W[k] for k in ("E1", "E2", "E3", "EG", "u_sb", "rn", "y1"))
        AT, An, vb, kge, IpA, Rr, Xr = (W[k] for k in ("AT", "An", "vb", "kge", "IpA", "Rr", "Xr"))
        attnT, kdec, wT, qg, vnew, sqo = (W[k] for k in ("attnT", "kdec", "wT", "qg", "vnew", "sqo"))
        Pb, Ptb, Mb, Mtb = [W["P0"], W["P1"]], [W["Pt0"], W["Pt1"]], [W["M0"], W["M1"]], [W["Mt0"], W["Mt1"]]
        psG = nps()
        S.op("pe", lambda e: e.matmul(psG[:, 0:4], lhsT=UT, rhs=graw, start=True, stop=True), [cf, gab_], [psG])
        S.op("pe", lambda e: e.matmul(psG[:, 8:12], lhsT=onesf[:, :], rhs=graw, start=True, stop=True), [onesf, gab_], [psG])
        S.op("dve", lambda e: e.tensor_copy(out=Gc[:, :], in_=psG[:, 0:4]), [psG], [Gc])
        S.op("dve", lambda e: e.tensor_scalar(out=nGc[:, :], in0=psG[:, 0:4], scalar1=-1.0, scalar2=None, op0=ALU.mult),
             [psG], [nGc])
        S.op("dve", lambda e: e.tensor_tensor(out=R2[:, :], in0=psG[:, 0:4], in1=lnb, op=ALU.add), [psG, gab_], [R2])
        S.op("act", lambda e: e.activation(out=eG[:, :], in_=psG[:, 0:4], func=AF.Exp), [psG], [eG])
        S.op("dve", lambda e: e.tensor_tensor(out=bEG[:, :], in0=eG[:, :], in1=beta, op=ALU.mult), [eG, gab_], [bEG])
        S.op("dve", lambda e: e.tensor_tensor(out=dGl[:, :], in0=psG[:, 8:12], in1=Gc[:, :], op=ALU.subtract),
             [psG, Gc], [dGl])
        S.op("act", lambda e: e.activation(out=eGlmG[:, :], in_=dGl[:, :], func=AF.Exp), [dGl], [eGlmG])
        S.op("act", lambda e: e.activation(out=eGl[:, :], in_=psG[:, 8:12], func=AF.Exp), [psG], [eGl])
        for h in range(4):
            S.op("pool", lambda e, h=h: e.tensor_scalar(out=diag1[:, h, :], in0=idf, scalar1=Gc[:, h:h + 1], scalar2=1.0,
                                                        op0=ALU.mult, op1=ALU.mult), [cf, Gc], [diag1])
            S.op("pool", lambda e, h=h: e.tensor_scalar(out=diag2[:, h, :], in0=idf, scalar1=R2[:, h:h + 1], scalar2=1.0,
                                                        op0=ALU.mult, op1=ALU.mult), [cf, R2], [diag2])
        for h in range(4):
            hs = slice(h * 128, (h + 1) * 128)
            S.op("dve", lambda e, h=h, hs=hs: e.tensor_scalar(out=vb[:, h, :], in0=vtm_[:, ci, hs], scalar1=beta[:, h:h + 1],
                                                              scalar2=None, op0=ALU.mult), [vtm_, gab_], [vb])
            S.op("dve", lambda e, h=h, hs=hs: e.tensor_scalar(out=kge[:, h, :], in0=ktm_[:, ci, hs], scalar1=bEG[:, h:h + 1],
                                                              scalar2=None, op0=ALU.mult), [ktm_, bEG], [kge])
            S.op("pool", lambda e, h=h, hs=hs: e.tensor_scalar(
                out=kdec[:, h, :], in0=ktm_[:, ci, hs], scalar1=eGlmG[:, h:h + 1], scalar2=1.0, op0=ALU.mult,
                op1=ALU.mult), [ktm_, eGlmG], [kdec])
        yield
        psR1 = nps()
        S.op("pe", lambda e: e.matmul(psR1[:, :], lhsT=onesf[:, :], rhs=f2(diag1), start=True, stop=True),
             [onesf, diag1], [psR1])
        psR2 = nps()
        S.op("pe", lambda e: e.matmul(psR2[:, :], lhsT=onesf[:, :], rhs=f2(diag2), start=True, stop=True),
             [onesf, diag2], [psR2])
        for h in range(4):
            S.op("dve", lambda e, h=h: e.scalar_tensor_tensor(
                out=arg1[:, h, :], in0=psR1[:, h * 128:(h + 1) * 128], scalar=nGc[:, h:h + 1], in1=MI, op0=ALU.add,
                op1=ALU.add), [psR1, nGc, cf], [arg1])
            S.op("dve", lambda e, h=h: e.scalar_tensor_tensor(
                out=arg2[:, h, :], in0=psR2[:, h * 128:(h + 1) * 128], scalar=nGc[:, h:h + 1], in1=MS, op0=ALU.add,
                op1=ALU.add), [psR2, nGc, cf], [arg2])
            S.op("dve", lambda e, h=h: e.scalar_tensor_tensor(
                out=arg3[:, h, :], in0=psR1[:, h * 128:(h + 1) * 128], scalar=-1.0, in1=MSL, op0=ALU.mult,
                op1=ALU.add), [psR1, cf], [arg3])
        S.op("act", lambda e: e.activation(out=f2(E1), in_=f2(arg1), func=AF.Exp), [arg1], [E1])
        S.op("act", lambda e: e.activation(out=f2(E2), in_=f2(arg2), func=AF.Exp), [arg2], [E2])
        for h in range(4):
            S.op("act", lambda e, h=h: e.activation(out=E3[:, h, :], in_=arg3[:, h, :], func=AF.Exp, bias=R2[:, h:h + 1]),
                 [arg3, R2], [E3])
        S.op("act", lambda e: e.activation(out=f2(EG), in_=psR1[:, :], func=AF.Exp), [psR1], [EG])
        S.op("pool", lambda e: e.tensor_tensor(out=qg[:, :, :], in0=qT_[:, :, cs], in1=EG[:, :, :], op=ALU.mult),
             [qT_, EG], [qg])
        yield
        psKK = nps()
        mm4(psKK, lambda h: kT_[:, h, cs], lambda h: kT_[:, h, cs], [kT_])
        psQK = nps()
        mm4(psQK, lambda h: kT_[:, h, cs], lambda h: qT_[:, h, cs], [kT_, qT_])
        S.op("dve", lambda e: e.tensor_tensor(out=f2(AT), in0=psKK[:, :], in1=f2(E2), op=ALU.mult), [psKK, E2], [AT])
        S.op("dve", lambda e: e.tensor_tensor(out=f2(An), in0=psKK[:, :], in1=f2(E3), op=ALU.mult), [psKK, E3], [An])
        S.op("dve", lambda e: e.tensor_tensor(out=f2(attnT), in0=psQK[:, :], in1=f2(E1), op=ALU.mult), [psQK, E1], [attnT])
        P, Pt = AT, An
        M, Mt = Mb[0], Mtb[0]
        S.op("dve", lambda e: e.tensor_tensor(out=f2(M), in0=f2(idb4), in1=f2r(AT), op=ALU.subtract), [idb4, AT], [M])
        S.op("dve", lambda e: e.tensor_tensor(out=f2(Mt), in0=f2(idb4), in1=f2r(An), op=ALU.subtract), [idb4, An], [Mt])
        S.op("pool", lambda e: e.tensor_tensor(out=f2(IpA), in0=f2(idb4), in1=f2r(An), op=ALU.add), [idb4, An], [IpA])
        yield
        for k in range(1, 7):
            lastk = (k == 6)
            Pn, Ptn = Pb[k % 2], Ptb[k % 2]
            Mn, Mtn = Mb[k % 2], Mtb[k % 2]
            psP = nps()
            mm4(psP, lambda h: Pt[:, h, :], lambda h: P[:, h, :], [P, Pt])
            S.op("act", lambda e: e.activation(out=f2(Pn), in_=psP[:, :], func=AF.Copy), [psP], [Pn])
            if not lastk:
                psPt = nps()
                mm4(psPt, lambda h: P[:, h, :], lambda h: Pt[:, h, :], [P, Pt])
                S.op("act", lambda e: e.activation(out=f2(Ptn), in_=psPt[:, :], func=AF.Copy), [psPt], [Ptn])
            yield
            psM = nps()
            mm4(psM, lambda h: Mt[:, h, :], lambda h: Pn[:, h, :], [Mt, Pn])
            S.op("dve", lambda e: e.tensor_tensor(out=f2(Mn), in0=psM[:, :], in1=f2r(M), op=ALU.add), [psM, M], [Mn])
            psMt = nps()
            mm4(psMt, lambda h: Pn[:, h, :], lambda h: Mt[:, h, :], [Pn, Mt])
            S.op("dve", lambda e: e.tensor_tensor(out=f2(Mtn), in0=psMt[:, :], in1=f2r(Mt), op=ALU.add),
                 [psMt, Mt], [Mtn])
            P, Pt, M, Mt = Pn, Ptn, Mn, Mtn
            yield
        psZ = nps()
        mm4(psZ, lambda h: IpA[:, h, :], lambda h: M[:, h, :], [IpA, M])
        S.op("dve", lambda e: e.tensor_tensor(out=f2(Rr), in0=f2(idb4), in1=psZ[:, :], op=ALU.subtract), [idb4, psZ], [Rr])
        yield
        psC = nps()
        mm4(psC, lambda h: Mt[:, h, :], lambda h: Rr[:, h, :], [Mt, Rr])
        S.op("dve", lambda e: e.tensor_tensor(out=f2(Xr), in0=psC[:, :], in1=f2r(M), op=ALU.add), [psC, M], [Xr])
        yield
        TT = Xr
        psU = nps()
        mm4(psU, lambda h: TT[:, h, :], lambda h: vb[:, h, :], [TT, vb])
        S.op("act", lambda e: e.activation(out=f2(u_sb), in_=psU[:, :], func=AF.Copy), [psU], [u_sb])
        psW = nps()
        mm4(psW, lambda h: kge[:, h, :], lambda h: TT[:, h, :], [TT, kge])
        S.op("act", lambda e: e.activation(out=f2(wT), in_=psW[:, :], func=AF.Copy), [psW], [wT])
        yield
        psWS = nps()
        mm4(psWS, lambda h: wT[:, h, :], lambda h: Sbf[:, h, :], [wT, Sbf])
        S.op("dve", lambda e: e.tensor_tensor(out=f2(vnew), in0=f2(u_sb), in1=psWS[:, :], op=ALU.subtract),
             [u_sb, psWS], [vnew])
        yield
        psO = nps()
        for h in range(4):
            S.op("pe", lambda e, h=h: e.matmul(psO[:, h * 128:(h + 1) * 128], lhsT=Sbf[:, h, :], rhs=qg[:, h, :],
                                               start=True, stop=False), [Sbf, qg], [psO])
            S.op("pe", lambda e, h=h: e.matmul(psO[:, h * 128:(h + 1) * 128], lhsT=vnew[:, h, :], rhs=attnT[:, h, :],
                                               start=False, stop=True), [vnew, attnT], [psO])
        psS = nps()
        mm4(psS, lambda h: kdec[:, h, :], lambda h: vnew[:, h, :], [kdec, vnew])
        for h in range(4):
            S.op("dve", lambda e, h=h: e.scalar_tensor_tensor(
                out=S32[:, h, :], in0=S32[:, h, :], scalar=eGl[:, h:h + 1], in1=psS[:, h * 128:(h + 1) * 128],
                op0=ALU.mult, op1=ALU.add), [S32, eGl, psS], [S32])
        S.op("act", lambda e: e.activation(out=f2(Sbf), in_=f2(S32), func=AF.Copy), [S32], [Sbf])
        S.op("act", lambda e: e.activation(out=f2(sqo), in_=psO[:, :], func=AF.Square), [psO], [sqo])
        yield
        psN = nps()
        S.op("pe", lambda e: e.matmul(psN[:, :], lhsT=onesb[:, :], rhs=f2(sqo), start=True, stop=True), [onesb, sqo], [psN])
        S.op("act", lambda e: e.activation(out=f2(rn), in_=psN[:, :], func=AF.Ln, scale=1.0 / 128.0, bias=eps[:, 0:1]),
             [psN, eps], [rn])
        S.op("act", lambda e: e.activation(out=f2(rn), in_=f2(rn), func=AF.Exp, scale=-0.5), [rn], [rn])
        S.op("dve", lambda e: e.scalar_tensor_tensor(out=f2(y1), in0=psO[:, :], scalar=wgn, in1=f2(rn), op0=ALU.mult,
                                                     op1=ALU.mult), [psO, prmall, rn], [y1])
        S.op("pool", lambda e: e.tensor_tensor(out=yo_[:, :, cs], in0=y1[:, :, :], in1=gz_[:, :, cs], op=ALU.mult),
             [y1, gz_], [yo_])
        if ci == 3:
            t0 = g * 512
            S.dma("sp", YT[256:768, t0:t0 + 512].rearrange("(h p) n -> p h n", p=128), yo_[:, :, :], reads=[yo_])
            if g + 2 < NG:
                load_group(g + 2)
        yield

    load_group(0)
    if NG > 1:
        load_group(1)
    NGRP = 21
    A, a_cnt = chunk(0), 0
    for _ in range(NGRP // 2):
        next(A)
        a_cnt += 1
    for n in range(1, NCH + 1):
        B, b_cnt = (chunk(n) if n < NCH else None), 0
        while a_cnt < NGRP:
            if B is not None and b_cnt < NGRP:
                next(B)
                b_cnt += 1
            next(A)
            a_cnt += 1
        A, a_cnt = B, b_cnt


def host_consts():
    c = np.zeros((128, NCST), np.float32)
    i = np.arange(128)
    c[:, C_ID:C_ID + 128] = np.eye(128, dtype=np.float32)
    c[:, C_UT:C_UT + 128] = (i[:, None] <= i[None, :])
    c[:, C_NTRI:C_NTRI + 128] = -1.0 * (i[:, None] > i[None, :])
    c[:, C_MI:C_MI + 128] = np.where(i[None, :] >= i[:, None], 0.0, NEG)
    c[:, C_MS:C_MS + 128] = np.where(i[None, :] > i[:, None], 0.0, NEG)
    c[:, C_MSL:C_MSL + 128] = np.where(i[None, :] < i[:, None], 0.0, NEG)
    q = np.arange(512)
    for kb in range(4):
        c[:, C_SBM + kb * 512:C_SBM + (kb + 1) * 512] = ((kb * 128 + i)[:, None] < q[None, :])
    return c


def host_params(inp, depth):
    p = np.zeros((depth, 128, NPRM), np.float32)
    for l in range(depth):
        p[l, :, 0:8] = np.asarray(inp["w_norm_mix"][l]).reshape(8, 128).T
        p[l, :, 8:16] = np.asarray(inp["w_norm_ffn"][l]).reshape(8, 128).T
        sc = np.asarray(inp["w_sconv"][l])
        for i in range(2):
            p[l, :, 16 + i * 3:19 + i * 3] = sc[:, i * 128:(i + 1) * 128].T
        gc = np.asarray(inp["w_gdn_conv"][l])
        for m in range(12):
            p[l, :, 22 + m * 4:26 + m * 4] = gc[:, m * 128:(m + 1) * 128].T
        p[l, :, 70] = np.asarray(inp["w_gdn_norm"][l])
        fc = np.asarray(inp["w_ffn_conv"][l])
        for m in range(44):
            p[l, :, 71 + m * 3:74 + m * 3] = fc[:, m * 128:(m + 1) * 128].T
        p[l, 0:4, 203] = 1.0
        p[l, 4:8, 203] = -1.0
        p[l, 0:4, 204] = np.asarray(inp["gdn_dt_bias"][l])
        p[l, 0:4, 205] = np.asarray(inp["gdn_a_log"][l])
    return p


_NC_CACHE = {}


def make_in_maps(inp, xs, depth):
    f = lambda a: np.ascontiguousarray(np.asarray(a, dtype=np.float32))
    common = dict(
        w_in=f(inp["w_mix_in"])[:depth], w_out=f(inp["w_mix_out"])[:depth], w_up=f(inp["w_ffn_up"])[:depth],
        w_dn=f(inp["w_ffn_down"])[:depth], prm=host_params(inp, depth),
        wfin=f(inp["w_norm_final"]).reshape(1, D), cst=host_consts())
    return [dict(common, x=f(xs[i])) for i in range(len(xs))]


def kernel(**inp):
    x = np.asarray(inp["x"], dtype=np.float32)
    B, L, _ = x.shape
    key = (L, DEPTH)
    if key not in _NC_CACHE:
        _NC_CACHE[key] = build(L)
    nc = _NC_CACHE[key]
    xs = [x[i % B] for i in range(8)]
    in_maps = make_in_maps(inp, xs, DEPTH)
    res = run_bass_kernel_spmd(nc, in_maps, core_ids=list(range(8)))
    return np.stack([res.results[i]["out"] for i in range(B)], axis=0).astype(np.float32)
```

```python
import numpy as np
from contextlib import ExitStack
import concourse.bass as bass
import concourse.mybir as mybir
from concourse.bass_utils import run_bass_kernel_spmd

F32 = mybir.dt.float32
BF16 = mybir.dt.bfloat16
F32R = mybir.dt.float32r
ALU = mybir.AluOpType
AF = mybir.ActivationFunctionType

D = 1024
DEPTH = 2
DFF = 2816
DIN = 3592
EPS = 1e-6
NEG = -60000.0
NPRM = 206
C_ID, C_UT, C_NTRI, C_MI, C_MS, C_MSL, C_SBM = 0, 128, 256, 384, 512, 640, 768
NCST = 768 + 2048

EPOCH = 20000


class Buf:
    __slots__ = ("t", "w", "r")

    def __init__(self, t):
        self.t = t
        self.w = None
        self.r = {}

    def __getitem__(self, k):
        return self.t[k]


class _Rec:
    def __getattr__(self, name):
        def f(*a, **kw):
            self.call = (name, a, kw)
            return self
        return f


class Sched:
    def __init__(self, nc, es):
        self.nc = nc
        self.es = es
        self.names = ["pe", "act", "dve", "pool", "sp"]
        self.prog = {e: [] for e in self.names}
        self.sem = {}
        self.val = {}
        self.cur = {}
        self.n = {e: 0 for e in self.names}
        self.seen = {e: {} for e in self.names}
        for e in self.names:
            self._new_epoch(e, 0)
        self.dq = {}
        self.dqi = {}
        for q in ("sp", "pool", "act"):
            keys = []
            for i in range(8):
                k = ("d", q, i)
                self.sem[k] = es.enter_context(nc.semaphore(f"d_{q}_{i}"))
                self.val[k] = 0
                keys.append(k)
            self.dq[q] = keys
            self.dqi[q] = 0
        self.nbuf = 0

    def _new_epoch(self, e, ep):
        k = ("e", e, ep)
        self.sem[k] = self.es.enter_context(self.nc.semaphore(f"s_{e}_{ep}"))
        self.val[k] = 0
        self.cur[e] = k

    def sb(self, shape, dt, name=None):
        self.nbuf += 1
        t = self.es.enter_context(self.nc.sbuf_tensor(f"{name or 'b'}_{self.nbuf}", list(shape), dt))
        return Buf(t)

    def _deps(self, reads, writes):
        deps = {}

        def add(ev):
            if ev is None:
                return
            k, v = ev
            if deps.get(k, 0) < v:
                deps[k] = v

        for b in reads:
            add(b.w)
        for b in writes:
            add(b.w)
            for k, v in b.r.items():
                add((k, v))
        return deps

    def _waits(self, e, deps):
        ws = []
        seen = self.seen[e]
        for k, v in deps.items():
            if seen.get(k, 0) < v:
                seen[k] = v
                ws.append((self.sem[k], v))
        return ws

    def _mark(self, ev, reads, writes):
        k, v = ev
        for b in reads:
            if b.r.get(k, 0) < v:
                b.r[k] = v
        for b in writes:
            b.w = ev
            b.r = {}

    def op(self, e, fn, reads=(), writes=()):
        ws = self._waits(e, self._deps(reads, writes))
        if self.val[self.cur[e]] >= EPOCH:
            self._new_epoch(e, self.cur[e][2] + 1)
        k = self.cur[e]
        self.val[k] += 1
        v = self.val[k]
        sem = self.sem[k]
        if e == "pe":
            self.seen[e][k] = v

        rec = _Rec()
        fn(rec)
        name, a, kw = rec.call

        def thunk(eng, ws=ws, name=name, a=a, kw=kw, sem=sem):
            for s, val in ws:
                eng.wait_ge(s, val)
            getattr(eng, name)(*a, **kw).then_inc(sem, 1)

        self.prog[e].append(thunk)
        self._mark((k, v), reads, writes)

    def dma(self, q, out, in_, reads=(), writes=()):
        deps = self._deps(reads, writes)
        i = self.dqi[q]
        self.dqi[q] = (i + 1) % len(self.dq[q])
        k = self.dq[q][i]
        if deps.get(k, 0) < self.val[k]:
            deps[k] = self.val[k]
        ws = self._waits(q, deps)
        self.val[k] += 16
        v = self.val[k]
        sem = self.sem[k]

        def thunk(eng, ws=ws, sem=sem, out=out, in_=in_):
            for s, val in ws:
                eng.wait_ge(s, val)
            eng.dma_start(out=out, in_=in_).then_inc(sem, 16)

        self.prog[q].append(thunk)
        self._mark((k, v), reads, writes)

    def barrier(self):
        allv = {k: v for k, v in self.val.items() if v > 0}
        for e in self.names:
            ws = self._waits(e, allv)
            if ws:
                def thunk(eng, ws=ws):
                    for s, val in ws:
                        eng.wait_ge(s, val)
                self.prog[e].append(thunk)

    def emit(self):
        nc = self.nc
        self.barrier()
        with nc.Block() as block:
            @block.tensor
            def _(eng):
                for th in self.prog["pe"]:
                    th(eng)

            @block.scalar
            def _(eng):
                for th in self.prog["act"]:
                    th(eng)

            @block.vector
            def _(eng):
                for th in self.prog["dve"]:
                    th(eng)

            @block.gpsimd
            def _(eng):
                for th in self.prog["pool"]:
                    th(eng)

            @block.sync
            def _(eng):
                for th in self.prog["sp"]:
                    th(eng)


def skew_emit(units):
    n = len(units)
    ns = max(len(u) for u in units) if units else 0
    for it in range(n + ns - 1):
        for st in range(ns):
            u = it - st
            if 0 <= u < n and st < len(units[u]) and units[u][st] is not None:
                units[u][st]()


def build(L, depth=DEPTH, phases="ABGC", dbg=(), ext_in=()):
    NT = L // 512
    nc = bass.Bass("TRN2", target_bir_lowering=False)
    es = ExitStack()
    dr = {}

    def dram(name, shape, dt, kind="Internal"):
        if name in dbg:
            kind = "ExternalOutput"
        if name in ext_in:
            kind = "ExternalInput"
        dr[name] = nc.dram_tensor(name, list(shape), dt, kind=kind)
        return dr[name]

    x_in = dram("x", [L, D], F32, "ExternalInput")
    w_in = dram("w_in", [depth, D, DIN], F32, "ExternalInput")
    w_out = dram("w_out", [depth, D, D], F32, "ExternalInput")
    w_up = dram("w_up", [depth, D, 2 * DFF], F32, "ExternalInput")
    w_dn = dram("w_dn", [depth, DFF, D], F32, "ExternalInput")
    prm = dram("prm", [depth, 128, NPRM], F32, "ExternalInput")
    wfin = dram("wfin", [1, D], F32, "ExternalInput")
    cst = dram("cst", [128, NCST], F32, "ExternalInput")
    out = dram("out", [L, D], F32, "ExternalOutput")
    xres = dram("xres", [L, D], F32)
    YT = dram("YT", [D, L], BF16)
    gq = dram("gq", [512, L], BF16)
    gk = dram("gk", [512, L], BF16)
    gz = dram("gz", [512, L], BF16)
    gktm = dram("gktm", [L, 512], BF16)
    gvtm = dram("gvtm", [L, 512], BF16)
    gab = dram("gab", [L, 12], F32)
    sq = dram("sq", [256, L], BF16)
    sk = dram("sk", [256, L], BF16)
    svtm = dram("svtm", [L, 256], BF16)
    if "dbg" in dbg:
        dram("dbg", [16, 128, 512], F32)

    with es:
        S = Sched(nc, es)
        cf = S.sb([128, C_SBM], F32, "cf")
        S.dma("sp", cf[:, :], cst[:, 0:C_SBM], writes=[cf])
        idb = S.sb([128, 128], BF16, "idb")
        S.op("dve", lambda e: e.tensor_copy(out=idb[:, :], in_=cf[:, C_ID:C_ID + 128]), [cf], [idb])
        onesb = S.sb([128, 128], BF16, "onesb")
        S.op("dve", lambda e: e.memset(onesb[:, :], 1.0), [], [onesb])
        onesf = S.sb([128, 128], F32, "onesf")
        S.op("dve", lambda e: e.memset(onesf[:, :], 1.0), [], [onesf])
        psum = [Buf(es.enter_context(nc.psum_tensor(f"ps{i}", [128, 512], F32))) for i in range(8)]
        prmall = S.sb([128, depth, NPRM], F32, "prm")
        for l in range(depth):
            S.dma("sp", prmall[:, l, :], prm[l, :, :], writes=[prmall])

        pctr = [0]

        def nextps(lo=0, hi=8):
            i = lo + pctr[0] % (hi - lo)
            pctr[0] += 1
            return psum[i]

        def rmsnorm_T(es2, xt, nb, wcol, l, hT, W, banks=(4, 8)):
            junk, ss, rstd, xn = W["junk"], W["ss"], W["rstd"], W["xn"]
            for b in range(nb):
                S.op("act", lambda e, b=b: e.activation(out=junk[:, :], in_=xt[:, b, :], func=AF.Square,
                                                         accum_out=ss[:, b:b + 1]), [xt], [junk, ss])
            S.op("act", lambda e: e.activation(out=rstd[:, 0:nb], in_=ss[:, 0:nb], func=AF.Ln,
                                               scale=1.0 / D, bias=W["eps"][:, 0:1]), [ss, W["eps"]], [rstd])
            S.op("act", lambda e: e.activation(out=rstd[:, 0:nb], in_=rstd[:, 0:nb], func=AF.Exp, scale=-0.5),
                 [rstd], [rstd])
            for b in range(nb):
                if b % 2 == 0:
                    S.op("dve", lambda e, b=b: e.tensor_scalar(out=xn[:, b, :], in0=xt[:, b, :], scalar1=rstd[:, b:b + 1],
                                                               scalar2=None, op0=ALU.mult), [xt, rstd], [xn])
                else:
                    S.op("act", lambda e, b=b: e.activation(out=xn[:, b, :], in_=xt[:, b, :], func=AF.Copy,
                                                            scale=rstd[:, b:b + 1]), [xt, rstd], [xn])
            n = nb * 128
            cpb = 1024 // n
            for c0 in range(0, 8, cpb):
                ps = nextps(*banks)
                pv = ps.t[:, :].bitcast(BF16)
                for ci in range(cpb):
                    c = c0 + ci
                    for b in range(nb):
                        S.op("pe", lambda e, c=c, b=b, ci=ci, pv=pv: e.transpose(
                            out=pv[:, ci * n + b * 128: ci * n + (b + 1) * 128],
                            in_=xn[:, b, c * 128:(c + 1) * 128], identity=idb[:, :]), [xn, idb], [ps])
                for ci in range(cpb):
                    c = c0 + ci
                    S.op("act", lambda e, c=c, ci=ci, pv=pv: e.activation(
                        out=hT[:, c, 0:n], in_=pv[:, ci * n:(ci + 1) * n], func=AF.Copy,
                        scale=prmall[:, l, wcol + c:wcol + c + 1]), [ps, prmall], [hT])

        def load_w_bf16(dst, src_ap, nchunk):
            for c in range(nchunk):
                S.dma("pool", dst[:, c, :], src_ap[c * 128:(c + 1) * 128, :], writes=[dst])

        for l in range(depth):
            xsrc = x_in if l == 0 else xres
            if "A" in phases:
                with ExitStack() as pa:
                    es_save = S.es
                    S.es = pa
                    phase_A(S, nc, L, l, xsrc, w_in, prmall, cf, idb, onesb, psum, nextps, rmsnorm_T,
                            load_w_bf16, dr)
                    S.barrier()
                    S.es = es_save
            if "B" in phases:
                with ExitStack() as pa:
                    es_save = S.es
                    S.es = pa
                    phase_SB(S, nc, L, l, cf, psum, nextps, dr)
                    S.barrier()
                    S.es = es_save
            if "G" in phases:
                with ExitStack() as pa:
                    es_save = S.es
                    S.es = pa
                    phase_GDN(S, nc, L, l, prmall, cf, idb, onesb, onesf, psum, nextps, dr)
                    S.barrier()
                    S.es = es_save
            if "C" in phases:
                with ExitStack() as pa:
                    es_save = S.es
                    S.es = pa
                    phase_C(S, nc, L, l, depth, xsrc, w_out, w_up, w_dn, wfin, prmall, idb, psum, nextps,
                            rmsnorm_T, load_w_bf16, dr)
                    S.barrier()
                    S.es = es_save
        S.emit()
    return nc


def phase_C(S, nc, L, l, depth, xsrc, w_out, w_up, w_dn, wfin, prmall, idb, psum, nextps, rmsnorm_T,
            load_w_bf16, dr):
    TT = 256
    nb = 2
    xres, YT, out = dr["xres"], dr["YT"], dr["out"]
    last = (l == depth - 1)
    Wo = S.sb([128, 8, D], BF16, "Wo")
    Wu = S.sb([128, 8, 2 * DFF], BF16, "Wu")
    Wd = S.sb([128, 22, D], BF16, "Wd")
    load_w_bf16(Wo, w_out[l], 8)
    load_w_bf16(Wu, w_up[l], 8)
    load_w_bf16(Wd, w_dn[l], 22)
    xt = S.sb([128, nb, D], F32, "xt")
    yT = S.sb([128, 8, TT], BF16, "yT")
    hT = S.sb([128, 8, TT], BF16, "hT")
    aT = S.sb([128, 22, TT], BF16, "aT")
    Wk = dict(junk=S.sb([128, D], BF16, "junk"), ss=S.sb([128, 4], F32, "ss"), rstd=S.sb([128, 4], F32, "rstd"),
              xn=S.sb([128, nb, D], BF16, "xn"), eps=S.sb([128, 1], F32, "eps"))
    S.op("dve", lambda e: e.memset(Wk["eps"][:, :], EPS), [], [Wk["eps"]])
    halo = S.sb([128, 44, 2], F32, "halo")
    S.op("pool", lambda e: e.memset(halo[:, :, :], 0.0), [], [halo])
    ub = [S.sb([128, TT + 2], F32, f"ub{i}") for i in range(4)]
    cv = [S.sb([128, TT], F32, f"cv{i}") for i in range(4)]
    tmp = [S.sb([128, TT], F32, f"tmp{i}") for i in range(4)]
    one = S.sb([128, 1], F32, "one")
    S.op("dve", lambda e: e.memset(one[:, :], 1.0), [], [one])
    wfb = None
    if last:
        wfb = S.sb([128, D], F32, "wfb")
        S.dma("sp", wfb[:, :], wfin[0:1, :].partition_broadcast(128), writes=[wfb])
    FC = 71
    k = 0
    def outproj_early(t):
        t0 = t * TT
        S.dma("sp", yT[:, :, :], YT[:, t0:t0 + TT].rearrange("(c p) n -> p c n", p=128), writes=[yT])
        for b in range(nb):
            for h in range(2):
                ps = psum[4 + b * 2 + h]
                for c in range(8):
                    S.op("pe", lambda e, ps=ps, c=c, b=b, h=h: e.matmul(
                        ps[:, :], lhsT=yT[:, c, b * 128:(b + 1) * 128], rhs=Wo[:, c, h * 512:(h + 1) * 512],
                        start=(c == 0), stop=(c == 7)), [yT, Wo], [ps])

    outproj_early(0)
    for t in range(L // TT):
        t0 = t * TT
        S.dma("sp", xt[:, :, :], xsrc[t0:t0 + TT, :].rearrange("(b p) d -> p b d", p=128), writes=[xt])
        for b in range(nb):
            for h in range(2):
                ps = psum[4 + b * 2 + h]
                S.op("dve", lambda e, ps=ps, b=b, h=h: e.tensor_tensor(
                    out=xt[:, b, h * 512:(h + 1) * 512], in0=ps[:, :], in1=xt[:, b, h * 512:(h + 1) * 512],
                    op=ALU.add), [ps, xt], [xt])
        rmsnorm_T(None, xt, nb, 8, l, hT, Wk, banks=(0, 4))
        units = []
        for j in range(22):
            ug, uv = ub[(2 * j) % 4], ub[(2 * j + 1) % 4]
            cg, cvv = cv[(2 * j) % 4], cv[(2 * j + 1) % 4]
            t1, t2 = tmp[(2 * j) % 4], tmp[(2 * j + 1) % 4]

            def s0(j=j, ug=ug, uv=uv):
                for m, u in ((j, ug), (j + 22, uv)):
                    ps = nextps(0, 4)
                    for c in range(8):
                        S.op("pe", lambda e, ps=ps, c=c, m=m: e.matmul(
                            ps[:, 0:TT], lhsT=Wu[:, c, m * 128:(m + 1) * 128], rhs=hT[:, c, :],
                            start=(c == 0), stop=(c == 7)), [Wu, hT], [ps])
                    S.op("act", lambda e, ps=ps, u=u: e.activation(out=u[:, 2:TT + 2], in_=ps[:, 0:TT], func=AF.Copy),
                         [ps], [u])
                    S.op("pool", lambda e, u=u, m=m: e.tensor_copy(out=u[:, 0:2], in_=halo[:, m, :]), [halo], [u])

            def s1(j=j, ug=ug, uv=uv, cg=cg, cvv=cvv):
                for m, u, c_ in ((j, ug, cg), (j + 22, uv, cvv)):
                    pc = FC + m * 3
                    S.op("dve", lambda e, u=u, c_=c_, pc=pc: e.tensor_scalar(
                        out=c_[:, :], in0=u[:, 0:TT], scalar1=prmall[:, l, pc:pc + 1], scalar2=None, op0=ALU.mult),
                        [u, prmall], [c_])
                    S.op("dve", lambda e, u=u, c_=c_, pc=pc: e.scalar_tensor_tensor(
                        out=c_[:, :], in0=u[:, 1:TT + 1], scalar=prmall[:, l, pc + 1:pc + 2], in1=c_[:, :],
                        op0=ALU.mult, op1=ALU.add), [u, prmall, c_], [c_])
                    S.op("dve", lambda e, u=u, c_=c_, pc=pc: e.scalar_tensor_tensor(
                        out=c_[:, :], in0=u[:, 2:TT + 2], scalar=prmall[:, l, pc + 2:pc + 3], in1=c_[:, :],
                        op0=ALU.mult, op1=ALU.add), [u, prmall, c_], [c_])
                    S.op("pool", lambda e, u=u, m=m: e.tensor_copy(out=halo[:, m, :], in_=u[:, TT:TT + 2]), [u], [halo])

            def s2(cg=cg, cvv=cvv, t1=t1, t2=t2):
                S.op("act", lambda e: e.activation(out=t1[:, :], in_=cg[:, :], func=AF.Exp, scale=-1.0), [cg], [t1])
                S.op("act", lambda e: e.activation(out=t1[:, :], in_=t1[:, :], func=AF.Ln, bias=1.0), [t1], [t1])
                S.op("act", lambda e: e.activation(out=t1[:, :], in_=t1[:, :], func=AF.Exp, scale=-1.0), [t1], [t1])
                S.op("pool", lambda e: e.tensor_tensor(out=t2[:, :], in0=cg[:, :], in1=cvv[:, :], op=ALU.mult),
                     [cg, cvv], [t2])

            def s3(j=j, t1=t1, t2=t2):
                S.op("pool", lambda e: e.tensor_tensor(out=aT[:, j, :], in0=t1[:, :], in1=t2[:, :], op=ALU.mult),
                     [t1, t2], [aT])

            units.append([s0, s1, s2, s3])
            if j == 8 and t + 1 < L // TT:
                units.append([lambda t=t: outproj_early(t + 1)])
        skew_emit(units)
        for b in range(nb):
            for h in range(2):
                ps = nextps(0, 4)
                for j in range(22):
                    S.op("pe", lambda e, ps=ps, j=j, b=b, h=h: e.matmul(
                        ps[:, :], lhsT=aT[:, j, b * 128:(b + 1) * 128], rhs=Wd[:, j, h * 512:(h + 1) * 512],
                        start=(j == 0), stop=(j == 21)), [aT, Wd], [ps])
                S.op("dve", lambda e, ps=ps, b=b, h=h: e.tensor_tensor(
                    out=xt[:, b, h * 512:(h + 1) * 512], in0=ps[:, :], in1=xt[:, b, h * 512:(h + 1) * 512],
                    op=ALU.add), [ps, xt], [xt])
        if not last:
            S.dma("sp", xres[t0:t0 + TT, :].rearrange("(b p) d -> p b d", p=128), xt[:, :, :], reads=[xt])
        else:
            junk, ss, rstd = Wk["junk"], Wk["ss"], Wk["rstd"]
            for b in range(nb):
                S.op("act", lambda e, b=b: e.activation(out=junk[:, :], in_=xt[:, b, :], func=AF.Square,
                                                         accum_out=ss[:, b:b + 1]), [xt], [junk, ss])
            S.op("act", lambda e: e.activation(out=rstd[:, 0:nb], in_=ss[:, 0:nb], func=AF.Ln,
                                               scale=1.0 / D, bias=Wk["eps"][:, 0:1]), [ss, Wk["eps"]], [rstd])
            S.op("act", lambda e: e.activation(out=rstd[:, 0:nb], in_=rstd[:, 0:nb], func=AF.Exp, scale=-0.5),
                 [rstd], [rstd])
            for b in range(nb):
                S.op("dve", lambda e, b=b: e.scalar_tensor_tensor(
                    out=xt[:, b, :], in0=xt[:, b, :], scalar=rstd[:, b:b + 1], in1=wfb[:, :],
                    op0=ALU.mult, op1=ALU.mult), [xt, rstd, wfb], [xt])
            S.dma("sp", out[t0:t0 + TT, :].rearrange("(b p) d -> p b d", p=128), xt[:, :, :], reads=[xt])


def phase_A(S, nc, L, l, xsrc, w_in, prmall, cf, idb, onesb, psum, nextps, rmsnorm_T, load_w_bf16, dr):
    TT = 512
    nb = 4
    YT, gq, gk, gz, gktm, gvtm, gab, sq, sk, svtm = (dr[n] for n in
                                                       ("YT", "gq", "gk", "gz", "gktm", "gvtm", "gab", "sq", "sk", "svtm"))
    Win = S.sb([128, 8, DIN], BF16, "Win")
    load_w_bf16(Win, w_in[l], 8)
    xt = S.sb([128, nb, D], F32, "xt")
    hT = S.sb([128, 8, TT], BF16, "hT")
    xn = S.sb([128, nb, D], BF16, "xn")
    ss, rstd = S.sb([128, 4], F32, "ss"), S.sb([128, 4], F32, "rstd")
    eps = S.sb([128, 1], F32, "eps")
    S.op("dve", lambda e: e.memset(eps[:, :], EPS), [], [eps])
    one = S.sb([128, 1], F32, "one")
    S.op("dve", lambda e: e.memset(one[:, :], 1.0), [], [one])
    lnq = S.sb([128, 1], F32, "lnq")
    S.op("dve", lambda e: e.memset(lnq[:, :], -0.5 * float(np.log(128.0))), [], [lnq])
    na = S.sb([128, 1], F32, "na")
    S.op("act", lambda e: e.activation(out=na[:, :], in_=prmall[:, l, 205:206], func=AF.Exp), [prmall], [na])
    S.op("dve", lambda e: e.tensor_scalar(out=na[:, :], in0=na[:, :], scalar1=-1.0, scalar2=None, op0=ALU.mult),
         [na], [na])
    rbuf = [S.sb([128, TT + 3], F32, f"rbuf{m}") for m in range(12)]
    pbuf = [S.sb([128, TT + 2], F32, f"pbuf{i}") for i in range(2)]
    for b_ in rbuf:
        S.op("pool", lambda e, b_=b_: e.memset(b_[:, 0:3], 0.0), [], [b_])
    for b_ in pbuf:
        S.op("pool", lambda e, b_=b_: e.memset(b_[:, 0:2], 0.0), [], [b_])

    def fam(n, dt, name, shape=(128, TT)):
        return [S.sb(list(shape), dt, f"{name}{i}") for i in range(n)]

    cvb, e1, sfl, rs = fam(4, F32, "cvb"), fam(3, F32, "e1"), fam(5, F32, "sfl"), fam(3, F32, "rs")
    evA, evB, evC = fam(5, F32, "evA"), fam(2, F32, "evB"), fam(2, F32, "evC")
    sqb, obf = fam(3, BF16, "sqb"), fam(9, BF16, "obf")
    tm = fam(3, BF16, "tm", (128, 4, 128))
    ab = fam(4, F32, "ab")
    gabt = S.sb([128, 4, 12], F32, "gabt")
    idf = cf
    nt = L // TT

    def xload(t):
        S.dma("sp", xt[:, :, :], xsrc[t * TT:(t + 1) * TT, :].rearrange("(b p) d -> p b d", p=128), writes=[xt])

    def mm_tile(c0, ncols):
        ps = nextps(0, 4)
        for c in range(8):
            S.op("pe", lambda e, ps=ps, c=c: e.matmul(ps[0:ncols, 0:TT], lhsT=Win[:, c, c0:c0 + ncols], rhs=hT[:, c, :],
                                                      start=(c == 0), stop=(c == 7)), [Win, hT], [ps])
        return ps

    def evac(ps, dst_ap, dst):
        S.op("act", lambda e: e.activation(out=dst_ap, in_=ps[:, 0:TT], func=AF.Copy), [ps], [dst])

    def sigmoid(src, e_):
        S.op("act", lambda e: e.activation(out=e_[:, :], in_=src[:, 0:TT], func=AF.Exp, scale=-1.0), [src], [e_])
        S.op("act", lambda e: e.activation(out=e_[:, :], in_=e_[:, :], func=AF.Ln, bias=1.0), [e_], [e_])
        S.op("act", lambda e: e.activation(out=e_[:, :], in_=e_[:, :], func=AF.Exp, scale=-1.0), [e_], [e_])

    def conv(src, cv_, pc, K):
        S.op("dve", lambda e: e.tensor_scalar(out=cv_[:, :], in0=src[:, 0:TT], scalar1=prmall[:, l, pc:pc + 1],
                                              scalar2=None, op0=ALU.mult), [src, prmall], [cv_])
        for kk in range(1, K):
            S.op("dve", lambda e, kk=kk: e.scalar_tensor_tensor(
                out=cv_[:, :], in0=src[:, kk:TT + kk], scalar=prmall[:, l, pc + kk:pc + kk + 1], in1=cv_[:, :],
                op0=ALU.mult, op1=ALU.add), [src, prmall, cv_], [cv_])
        S.op("pool", lambda e: e.tensor_copy(out=src[:, 0:K - 1], in_=src[:, TT:TT + K - 1]), [src], [src])

    def tr_pe(o, bank):
        pv = bank.t[:, :].bitcast(BF16)
        for b in range(4):
            S.op("pe", lambda e, b=b: e.transpose(out=pv[:, b * 128:(b + 1) * 128], in_=o[:, b * 128:(b + 1) * 128],
                                                  identity=idb[:, :]), [o, idb], [bank])

    def tr_ev(bank, t_, dst_ap):
        pv = bank.t[:, :].bitcast(BF16)
        S.op("act", lambda e: e.activation(out=t_[:, :, :], in_=pv[:, 0:512].rearrange("p (b f) -> p b f", b=4),
                                           func=AF.Copy), [bank], [t_])
        S.dma("sp", dst_ap, t_[:, :, :], reads=[t_])

    units = []

    def add(stages):
        units.append(stages + [None] * (10 - len(stages)))

    for t in range(nt):
        t0 = t * TT
        def prologue(t=t):
            for b in range(nb):
                S.op("act", lambda e, b=b: e.activation(out=xn[:, b, :], in_=xt[:, b, :], func=AF.Square,
                                                         accum_out=ss[:, b:b + 1]), [xt], [xn, ss])
            S.op("act", lambda e: e.activation(out=rstd[:, 0:nb], in_=ss[:, 0:nb], func=AF.Ln, scale=1.0 / D,
                                               bias=eps[:, 0:1]), [ss, eps], [rstd])
            S.op("act", lambda e: e.activation(out=rstd[:, 0:nb], in_=rstd[:, 0:nb], func=AF.Exp, scale=-0.5), [rstd], [rstd])
            for b in range(nb):
                if b % 2 == 0:
                    S.op("dve", lambda e, b=b: e.tensor_scalar(out=xn[:, b, :], in0=xt[:, b, :], scalar1=rstd[:, b:b + 1],
                                                               scalar2=None, op0=ALU.mult), [xt, rstd], [xn])
                else:
                    S.op("act", lambda e, b=b: e.activation(out=xn[:, b, :], in_=xt[:, b, :], func=AF.Copy,
                                                            scale=rstd[:, b:b + 1]), [xt, rstd], [xn])
            if t + 1 < nt:
                xload(t + 1)
            for c0 in range(0, 8, 2):
                ps = nextps(0, 4)
                pv = ps.t[:, :].bitcast(BF16)
                for ci in range(2):
                    c = c0 + ci
                    for b in range(nb):
                        S.op("pe", lambda e, c=c, b=b, ci=ci: e.transpose(
                            out=pv[:, ci * 512 + b * 128: ci * 512 + (b + 1) * 128],
                            in_=xn[:, b, c * 128:(c + 1) * 128], identity=idb[:, :]), [xn, idb], [ps])
                for ci in range(2):
                    c = c0 + ci
                    S.op("act", lambda e, c=c, ci=ci: e.activation(
                        out=hT[:, c, :], in_=pv[:, ci * 512:(ci + 1) * 512], func=AF.Copy,
                        scale=prmall[:, l, c:c + 1]), [ps, prmall], [hT])
        add([prologue])
        for i in range(2):
            u = len(units)
            a_, b_, c_, cv_, o = evA[u % 5], evB[i], evC[i], cvb[u % 4], obf[u % 9]
            pb = pbuf[i]

            def s0(i=i, a_=a_, b_=b_, c_=c_):
                evac(mm_tile(256 + i * 128, 128), a_[:, :], a_)
                evac(mm_tile(512 + i * 128, 128), b_[:, :], b_)
                evac(mm_tile(i * 128, 128), c_[:, :], c_)

            def s1(i=i, a_=a_, b_=b_, cv_=cv_, pb=pb):
                S.op("dve", lambda e: e.tensor_tensor(out=pb[:, 2:TT + 2], in0=a_[:, :], in1=b_[:, :], op=ALU.mult),
                     [a_, b_], [pb])
                conv(pb, cv_, 16 + i * 3, 3)

            def s2(i=i, c_=c_, cv_=cv_, o=o, t0=t0):
                S.op("dve", lambda e: e.tensor_tensor(out=o[:, :], in0=c_[:, :], in1=cv_[:, :], op=ALU.mult), [c_, cv_], [o])
                S.dma("sp", YT[i * 128:(i + 1) * 128, t0:t0 + TT], o[:, :], reads=[o])
            add([s0, s1, s2])
        for m in range(12):
            u = len(units)
            h = m % 4
            rb, cv_, e_, s_, q_, r_, o, t_ = rbuf[m], cvb[u % 4], e1[u % 3], sfl[u % 5], sqb[u % 3], rs[u % 3], obf[u % 9], tm[u % 3]
            pss, ptr = psum[4 + u % 2], psum[6 + u % 2]
            qk = m < 8

            def s0(m=m, rb=rb):
                evac(mm_tile(768 + m * 128, 128), rb[:, 3:TT + 3], rb)

            def s1(m=m, rb=rb, cv_=cv_):
                conv(rb, cv_, 22 + m * 4, 4)

            def s2(cv_=cv_, e_=e_):
                sigmoid(cv_, e_)

            def s3(cv_=cv_, e_=e_, dst=(s_ if qk else o)):
                S.op("dve", lambda e: e.tensor_tensor(out=dst[:, :], in0=cv_[:, :], in1=e_[:, :], op=ALU.mult),
                     [cv_, e_], [dst])
            st = [s0, s1, s2, s3]
            if qk:
                def s4(s_=s_, q_=q_):
                    S.op("act", lambda e: e.activation(out=q_[:, :], in_=s_[:, :], func=AF.Square), [s_], [q_])

                def s5(q_=q_, pss=pss):
                    S.op("pe", lambda e: e.matmul(pss[:, :], lhsT=onesb[:, :], rhs=q_[:, :], start=True, stop=True),
                         [onesb, q_], [pss])

                def s6(m=m, pss=pss, r_=r_):
                    S.op("act", lambda e: e.activation(out=r_[:, :], in_=pss[:, :], func=AF.Ln, bias=EPS),
                         [pss], [r_])
                    if m < 4:
                        S.op("act", lambda e: e.activation(out=r_[:, :], in_=r_[:, :], func=AF.Exp, scale=-0.5,
                                                           bias=lnq[:, 0:1]), [r_, lnq], [r_])
                    else:
                        S.op("act", lambda e: e.activation(out=r_[:, :], in_=r_[:, :], func=AF.Exp, scale=-0.5), [r_], [r_])

                def s7(m=m, h=h, s_=s_, r_=r_, o=o, t0=t0):
                    S.op("pool", lambda e: e.tensor_tensor(out=o[:, :], in0=s_[:, :], in1=r_[:, :], op=ALU.mult), [s_, r_], [o])
                    dst = gq if m < 4 else gk
                    S.dma("sp", dst[h * 128:(h + 1) * 128, t0:t0 + TT], o[:, :], reads=[o])
                st += [s4, s5, s6, s7]
            else:
                st += [None, None, None, None]
            if m >= 4:
                dstT = gktm if m < 8 else gvtm

                def s8(o=o, ptr=ptr):
                    tr_pe(o, ptr)

                def s9(h=h, ptr=ptr, t_=t_, dstT=dstT, t0=t0):
                    tr_ev(ptr, t_, dstT[t0:t0 + TT, h * 128:(h + 1) * 128].rearrange("(b p) f -> p b f", p=128))
                st += [s8, s9]
            add(st)
        for h in range(4):
            u = len(units)
            a_, e_, o = evA[u % 5], e1[u % 3], obf[u % 9]

            def s0(h=h, a_=a_):
                evac(mm_tile(2304 + h * 128, 128), a_[:, :], a_)

            def s2(a_=a_, e_=e_):
                sigmoid(a_, e_)

            def s3(h=h, a_=a_, e_=e_, o=o, t0=t0):
                S.op("dve", lambda e: e.tensor_tensor(out=o[:, :], in0=a_[:, :], in1=e_[:, :], op=ALU.mult), [a_, e_], [o])
                S.dma("sp", gz[h * 128:(h + 1) * 128, t0:t0 + TT], o[:, :], reads=[o])
            add([s0, None, s2, s3])
        u = len(units)
        e8, ln8, rc8, g8 = ab
        ptr = psum[6 + u % 2]

        def s0():
            ps = mm_tile(2816, 8)
            S.op("act", lambda e: e.activation(out=e8[0:8, :], in_=ps[0:8, :], func=AF.Exp,
                                               scale=prmall[0:8, l, 203:204], bias=prmall[0:8, l, 204:205]),
                 [ps, prmall], [e8])

        def s1():
            S.op("dve", lambda e: e.tensor_scalar(out=e8[0:8, :], in0=e8[0:8, :], scalar1=1.0, scalar2=None, op0=ALU.add),
                 [e8], [e8])

        def s2():
            S.op("act", lambda e: e.activation(out=ln8[0:8, :], in_=e8[0:8, :], func=AF.Ln), [e8], [ln8])

        def s3():
            S.op("dve", lambda e: e.reciprocal(out=rc8[0:8, :], in_=e8[0:8, :]), [e8], [rc8])
            S.op("dve", lambda e: e.tensor_scalar(out=g8[0:8, :], in0=ln8[0:8, :], scalar1=na[0:8, 0:1], scalar2=None,
                                                  op0=ALU.mult), [ln8, na], [g8])

        def s8(ptr=ptr):
            for b in range(4):
                S.op("pe", lambda e, b=b: e.transpose(out=ptr[:, b * 16:b * 16 + 8], in_=g8[0:8, b * 128:(b + 1) * 128],
                                                      identity=idf[0:8, C_ID:C_ID + 8]), [g8, idf], [ptr])
                S.op("pe", lambda e, b=b: e.transpose(out=ptr[:, b * 16 + 8:b * 16 + 16], in_=rc8[0:8, b * 128:(b + 1) * 128],
                                                      identity=idf[0:8, C_ID:C_ID + 8]), [rc8, idf], [ptr])

        def s9(ptr=ptr, t0=t0):
            pA = ptr.t[:, 0:64].rearrange("p (b f) -> p b f", b=4)
            S.op("dve", lambda e: e.tensor_copy(out=gabt[:, :, 0:4], in_=pA[:, :, 0:4]), [ptr], [gabt])
            S.op("dve", lambda e: e.tensor_copy(out=gabt[:, :, 4:8], in_=pA[:, :, 12:16]), [ptr], [gabt])
            S.op("dve", lambda e: e.tensor_copy(out=gabt[:, :, 8:12], in_=pA[:, :, 4:8]), [ptr], [gabt])
            S.dma("sp", gab[t0:t0 + TT, :].rearrange("(b p) f -> p b f", p=128), gabt[:, :, :], reads=[gabt])
        add([s0, s1, s2, s3, None, None, None, None, s8, s9])
        for i in range(2):
            for dst, c0 in ((sq, 2824), (sk, 3080)):
                u = len(units)
                o = obf[u % 9]

                def s0(i=i, dst=dst, c0=c0, o=o, t0=t0):
                    evac(mm_tile(c0 + i * 128, 128), o[:, :], o)
                    S.dma("sp", dst[i * 128:(i + 1) * 128, t0:t0 + TT], o[:, :], reads=[o])
                add([s0])
            u = len(units)
            o, t_, ptr = obf[u % 9], tm[u % 3], psum[6 + u % 2]

            def s0(i=i, o=o):
                evac(mm_tile(3336 + i * 128, 128), o[:, :], o)

            def s8(o=o, ptr=ptr):
                tr_pe(o, ptr)

            def s9(i=i, ptr=ptr, t_=t_, t0=t0):
                tr_ev(ptr, t_, svtm[t0:t0 + TT, i * 128:(i + 1) * 128].rearrange("(b p) f -> p b f", p=128))
            add([s0, None, None, None, None, None, None, None, s8, s9])
    xload(0)
    skew_emit(units)


def phase_SB(S, nc, L, l, cf, psum, nextps, dr):
    YT, sq, sk, svtm, cst = dr["YT"], dr["sq"], dr["sk"], dr["svtm"], dr["cst"]
    NB = L // 128
    NT = L // 512
    M01 = S.sb([128, 4, 512], BF16, "M01")
    S.dma("pool", M01[:, :, :], cst[:, C_SBM:C_SBM + 2048].rearrange("p (k q) -> p k q", k=4), writes=[M01])
    ntri = S.sb([128, 128], BF16, "ntri")
    S.op("dve", lambda e: e.tensor_copy(out=ntri[:, :], in_=cf[:, C_NTRI:C_NTRI + 128]), [cf], [ntri])
    nones = S.sb([128, 128], BF16, "nones")
    S.op("dve", lambda e: e.memset(nones[:, :], -1.0), [], [nones])
    one = S.sb([128, 1], F32, "one")
    S.op("dve", lambda e: e.memset(one[:, :], 1.0), [], [one])
    KT = [S.sb([128, L], BF16, f"KT{i}") for i in range(2)]
    QT = [S.sb([128, L], BF16, f"QTz{h}") for h in range(4)]
    Vp = [S.sb([128, NB, 128], BF16, f"Vp{i}") for i in range(2)]
    for i in range(2):
        S.dma("sp", KT[i][:, :], sk[i * 128:(i + 1) * 128, :], writes=[KT[i]])
        for hh in range(2):
            qz = QT[2 * i + hh]
            oth = (1 - hh) * 64
            S.op("pool" if hh else "dve", lambda e, qz=qz, oth=oth: e.memset(qz[oth:oth + 64, :], 0.0), [], [qz])
            S.dma("sp", qz[hh * 64:hh * 64 + 64, :], sq[i * 128 + hh * 64:i * 128 + hh * 64 + 64, :], writes=[qz])
        for n0 in range(0, NB, 8):
            S.dma("sp", Vp[i][:, n0:n0 + 8, :],
                  svtm[n0 * 128:(n0 + 8) * 128, i * 128:(i + 1) * 128].rearrange("(n p) f -> p n f", p=128), writes=[Vp[i]])
    ebuf = [S.sb([128, 512], F32, f"eb{i}") for i in range(3)]
    spb = [S.sb([128, 512], BF16, f"spb{i}") for i in range(4)]
    lbuf = [S.sb([128, 512], F32, f"lb{i}") for i in range(3)]
    abuf = [S.sb([128, 512], BF16, f"ab{i}") for i in range(4)]
    Sacc = [S.sb([128, 512], BF16, f"Sacc{i}") for i in range(3)]
    osb = [S.sb([128, 512], BF16, f"osb{i}") for i in range(2)]
    units = []
    on = 0
    for h in range(4):
        for T in range(NT):
            nkb = 4 * T + 4
            for idx, kb in enumerate(range(nkb - 1, -1, -1)):
                u = len(units)
                units.append(dict(u=u, h=h, i=h // 2, r0=(h % 2) * 64, T=T, idx=idx, kb=kb, diag=kb >= 4 * T, kbi=kb - 4 * T,
                                  on=on, e_=ebuf[u % 3], sp_=spb[u % 4], lb_=lbuf[u % 3], a_=abuf[u % 4],
                                  zps=psum[u % 3], tps=psum[3 + u % 3], ops_=psum[6 + on % 2], o_=osb[on % 2]))
            on += 1
    st = dict(sa=0)

    def S1(U):
        i, r0, T, kb, zps, e_, sp_ = U["i"], U["r0"], U["T"], U["kb"], U["zps"], U["e_"], U["sp_"]
        qz = QT[U["h"]]
        S.op("pe", lambda e: e.matmul(zps[:, :], lhsT=KT[i][:, kb * 128:(kb + 1) * 128],
                                      rhs=qz[:, T * 512:(T + 1) * 512], start=True, stop=True),
             [KT[i], qz], [zps])
        S.op("act", lambda e: e.activation(out=e_[:, :], in_=zps[:, :], func=AF.Exp, scale=0.125), [zps], [e_])
        S.op("act", lambda e: e.activation(out=sp_[:, :], in_=e_[:, :], func=AF.Ln, bias=1.0), [e_], [sp_])
        if U["diag"]:
            S.op("pool", lambda e: e.tensor_tensor(out=sp_[:, :], in0=sp_[:, :], in1=M01[:, U["kbi"], :], op=ALU.mult),
                 [sp_, M01], [sp_])

    def S2(U):
        idx, kb, zps, tps, sp_, lb_ = U["idx"], U["kb"], U["zps"], U["tps"], U["sp_"], U["lb_"]
        S.op("pe", lambda e: e.matmul(tps[:, :], lhsT=ntri[:, :], rhs=sp_[:, :], start=True, stop=(idx == 0)),
             [ntri, sp_], [tps])
        if idx == 0:
            st["sa"] = 0
        else:
            sc = Sacc[st["sa"] % 3]
            S.op("pe", lambda e: e.matmul(tps[:, :], lhsT=nones[:, :], rhs=sc[:, :], start=False, stop=True),
                 [nones, sc], [tps])
        if kb > 0:
            if idx == 0:
                sn = Sacc[st["sa"] % 3]
                S.op("pool", lambda e: e.tensor_copy(out=sn[:, :], in_=sp_[:, :]), [sp_], [sn])
            else:
                so, sn = Sacc[st["sa"] % 3], Sacc[(st["sa"] + 1) % 3]
                S.op("pool", lambda e: e.tensor_tensor(out=sn[:, :], in0=so[:, :], in1=sp_[:, :], op=ALU.add),
                     [so, sp_], [sn])
                st["sa"] += 1
        S.op("dve", lambda e: e.scalar_tensor_tensor(out=lb_[:, :], in0=zps[:, :], scalar=0.125, in1=sp_[:, :],
                                                     op0=ALU.mult, op1=ALU.subtract), [zps, sp_], [lb_])

    def S3(U):
        i, r0, h, T, idx, kb = U["i"], U["r0"], U["h"], U["T"], U["idx"], U["kb"]
        tps, lb_, a_, ops_, o_ = U["tps"], U["lb_"], U["a_"], U["ops_"], U["o_"]
        S.op("dve", lambda e: e.tensor_tensor(out=lb_[:, :], in0=tps[:, :], in1=lb_[:, :], op=ALU.add), [tps, lb_], [lb_])
        S.op("act", lambda e: e.activation(out=a_[:, :], in_=lb_[:, :], func=AF.Exp), [lb_], [a_])
        if U["diag"]:
            S.op("pool", lambda e: e.tensor_tensor(out=a_[:, :], in0=a_[:, :], in1=M01[:, U["kbi"], :], op=ALU.mult),
                 [a_, M01], [a_])

    def S4(U):
        i, r0, h, T, idx, kb = U["i"], U["r0"], U["h"], U["T"], U["idx"], U["kb"]
        a_, ops_, o_ = U["a_"], U["ops_"], U["o_"]
        S.op("pe", lambda e: e.matmul(ops_[:, :], lhsT=Vp[i][:, kb, :], rhs=a_[:, :], start=(idx == 0),
                                      stop=(kb == 0)), [Vp[i], a_], [ops_])
        if kb == 0:
            S.op("act", lambda e: e.activation(out=o_[r0:r0 + 64, :], in_=ops_[r0:r0 + 64, :], func=AF.Copy), [ops_], [o_])
            S.dma("sp", YT[768 + h * 64:768 + (h + 1) * 64, T * 512:(T + 1) * 512], o_[r0:r0 + 64, :], reads=[o_])

    skew_emit([[lambda U=U: S1(U), lambda U=U: S2(U), lambda U=U: S3(U), lambda U=U: S4(U)] for U in units])


def phase_GDN(S, nc, L, l, prmall, cf, idb, onesb, onesf, psum, nextps, dr):
    YT, gq, gk, gz, gktm, gvtm, gab = (dr[n] for n in ("YT", "gq", "gk", "gz", "gktm", "gvtm", "gab"))
    NG = L // 512
    NCH = L // 128
    UT = cf[:, C_UT:C_UT + 128]
    idf = cf[:, C_ID:C_ID + 128]
    MI, MS, MSL = cf[:, C_MI:C_MI + 128], cf[:, C_MS:C_MS + 128], cf[:, C_MSL:C_MSL + 128]
    idb4 = S.sb([128, 4, 128], F32, "idb4")
    for h in range(4):
        S.op("dve", lambda e, h=h: e.tensor_copy(out=idb4[:, h, :], in_=idf), [cf], [idb4])
    eps = S.sb([128, 1], F32, "eps")
    S.op("dve", lambda e: e.memset(eps[:, :], EPS), [], [eps])

    def B4(name, dt=BF16):
        return S.sb([128, 4, 128], dt, name)

    qTg = [S.sb([128, 4, 512], BF16, f"qTg{i}") for i in range(2)]
    kTg = [S.sb([128, 4, 512], BF16, f"kTg{i}") for i in range(2)]
    gzg = [S.sb([128, 4, 512], BF16, f"gzg{i}") for i in range(2)]
    ktmg = [S.sb([128, 4, 512], BF16, f"ktmg{i}") for i in range(2)]
    vtmg = [S.sb([128, 4, 512], BF16, f"vtmg{i}") for i in range(2)]
    gabg = [S.sb([128, 4, 12], F32, f"gabg{i}") for i in range(2)]
    ygo = [S.sb([128, 4, 512], BF16, f"ygo{i}") for i in range(2)]
    sets = []
    for si in range(2):
        W = {}
        for n in ("Gc", "nGc", "R2", "eG", "bEG", "dGl", "eGlmG", "eGl"):
            W[n] = S.sb([128, 4], F32, f"{n}{si}")
        for n in ("diag1", "diag2", "arg1", "arg2", "arg3", "E1", "E2", "E3", "EG", "u_sb", "rn", "y1"):
            W[n] = B4(f"{n}{si}", F32)
        for n in ("AT", "An", "P0", "P1", "Pt0", "Pt1", "M0", "M1", "Mt0", "Mt1", "vb", "kge", "IpA", "Rr", "Xr"):
            W[n] = B4(f"{n}{si}", F32R)
        for n in ("attnT", "kdec", "wT", "qg", "vnew", "sqo"):
            W[n] = B4(f"{n}{si}")
        sets.append(W)
    S32, Sbf = B4("S32", F32), B4("Sbf")
    S.op("pool", lambda e: e.memset(S32[:, :, :], 0.0), [], [S32])
    S.op("pool", lambda e: e.memset(Sbf[:, :, :], 0.0), [], [Sbf])
    wgn = prmall[:, l, 70:71]
    bctr = [0, 0]

    def f2(b):
        return b[:, :, :].rearrange("p h c -> p (h c)")

    def f2r(b):
        return b[:, :, :].bitcast(F32).rearrange("p h c -> p (h c)")

    def load_group(g):
        t0 = g * 512
        i = g % 2
        S.dma("sp", qTg[i][:, :, :], gq[:, t0:t0 + 512].rearrange("(h p) n -> p h n", p=128), writes=[qTg[i]])
        S.dma("sp", kTg[i][:, :, :], gk[:, t0:t0 + 512].rearrange("(h p) n -> p h n", p=128), writes=[kTg[i]])
        S.dma("sp", gzg[i][:, :, :], gz[:, t0:t0 + 512].rearrange("(h p) n -> p h n", p=128), writes=[gzg[i]])
        S.dma("sp", ktmg[i][:, :, :], gktm[t0:t0 + 512, :].rearrange("(n p) f -> p n f", p=128), writes=[ktmg[i]])
        S.dma("sp", vtmg[i][:, :, :], gvtm[t0:t0 + 512, :].rearrange("(n p) f -> p n f", p=128), writes=[vtmg[i]])
        S.dma("sp", gabg[i][:, :, :], gab[t0:t0 + 512, :].rearrange("(n p) f -> p n f", p=128), writes=[gabg[i]])

    def mm4(ps, lhs_fn, rhs_fn, reads):
        for h in range(4):
            S.op("pe", lambda e, h=h: e.matmul(ps[:, h * 128:(h + 1) * 128], lhsT=lhs_fn(h), rhs=rhs_fn(h), start=True,
                                               stop=True), reads, [ps])

    def chunk(n):
        g, ci = n // 4, n % 4
        gi, si = g % 2, n % 2
        W = sets[si]

        def nps():
            i = si * 4 + bctr[si] % 4
            bctr[si] += 1
            return psum[i]

        qT_, kT_, gz_, ktm_, vtm_, gab_, yo_ = qTg[gi], kTg[gi], gzg[gi], ktmg[gi], vtmg[gi], gabg[gi], ygo[gi]
        cs = slice(ci * 128, (ci + 1) * 128)
        graw, beta, lnb = gab_[:, ci, 0:4], gab_[:, ci, 4:8], gab_[:, ci, 8:12]
        Gc, nGc, R2, eG, bEG, dGl, eGlmG, eGl = (W[k] for k in ("Gc", "nGc", "R2", "eG", "bEG", "dGl", "eGlmG", "eGl"))
        diag1, diag2, arg1, arg2, arg3 = (W[k] for k in ("diag1", "diag2", "arg1", "arg2", "arg3"))
        E1, E2, E3, EG, u_sb, rn, y1 = (W[k] for k in ("E1", "E2", "E3", "EG", "u_sb", "rn", "y1"))
        AT, An, vb, kge, IpA, Rr, Xr = (W[k] for k in ("AT", "An", "vb", "kge", "IpA", "Rr", "Xr"))
        attnT, kdec, wT, qg, vnew, sqo = (W[k] for k in ("attnT", "kdec", "wT", "qg", "vnew", "sqo"))
        Pb, Ptb, Mb, Mtb = [W["P0"], W["P1"]], [W["Pt0"], W["Pt1"]], [W["M0"], W["M1"]], [W["Mt0"], W["Mt1"]]
        psG = nps()
        S.op("pe", lambda e: e.matmul(psG[:, 0:4], lhsT=UT, rhs=graw, start=True, stop=True), [cf, gab_], [psG])
        S.op("pe", lambda e: e.matmul(psG[:, 8:12], lhsT=onesf[:, :], rhs=graw, start=True, stop=True), [onesf, gab_], [psG])
        S.op("dve", lambda e: e.tensor_copy(out=Gc[:, :], in_=psG[:, 0:4]), [psG], [Gc])
        S.op("dve", lambda e: e.tensor_scalar(out=nGc[:, :], in0=psG[:, 0:4], scalar1=-1.0, scalar2=None, op0=ALU.mult),
             [psG], [nGc])
        S.op("dve", lambda e: e.tensor_tensor(out=R2[:, :], in0=psG[:, 0:4], in1=lnb, op=ALU.add), [psG, gab_], [R2])
        S.op("act", lambda e: e.activation(out=eG[:, :], in_=psG[:, 0:4], func=AF.Exp), [psG], [eG])
        S.op("dve", lambda e: e.tensor_tensor(out=bEG[:, :], in0=eG[:, :], in1=beta, op=ALU.mult), [eG, gab_], [bEG])
        S.op("dve", lambda e: e.tensor_tensor(out=dGl[:, :], in0=psG[:, 8:12], in1=Gc[:, :], op=ALU.subtract),
             [psG, Gc], [dGl])
        S.op("act", lambda e: e.activation(out=eGlmG[:, :], in_=dGl[:, :], func=AF.Exp), [dGl], [eGlmG])
        S.op("act", lambda e: e.activation(out=eGl[:, :], in_=psG[:, 8:12], func=AF.Exp), [psG], [eGl])
        for h in range(4):
            S.op("pool", lambda e, h=h: e.tensor_scalar(out=diag1[:, h, :], in0=idf, scalar1=Gc[:, h:h + 1], scalar2=1.0,
                                                        op0=ALU.mult, op1=ALU.mult), [cf, Gc], [diag1])
            S.op("pool", lambda e, h=h: e.tensor_scalar(out=diag2[:, h, :], in0=idf, scalar1=R2[:, h:h + 1], scalar2=1.0,
                                                        op0=ALU.mult, op1=ALU.mult), [cf, R2], [diag2])
        for h in range(4):
            hs = slice(h * 128, (h + 1) * 128)
            S.op("dve", lambda e, h=h, hs=hs: e.tensor_scalar(out=vb[:, h, :], in0=vtm_[:, ci, hs], scalar1=beta[:, h:h + 1],
                                                              scalar2=None, op0=ALU.mult), [vtm_, gab_], [vb])
            S.op("dve", lambda e, h=h, hs=hs: e.tensor_scalar(out=kge[:, h, :], in0=ktm_[:, ci, hs], scalar1=bEG[:, h:h + 1],
                                                              scalar2=None, op0=ALU.mult), [ktm_, bEG], [kge])
            S.op("pool", lambda e, h=h, hs=hs: e.tensor_scalar(
                out=kdec[:, h, :], in0=ktm_[:, ci, hs], scalar1=eGlmG[:, h:h + 1], scalar2=1.0, op0=ALU.mult,
                op1=ALU.mult), [ktm_, eGlmG], [kdec])
        yield
        psR1 = nps()
        S.op("pe", lambda e: e.matmul(psR1[:, :], lhsT=onesf[:, :], rhs=f2(diag1), start=True, stop=True),
             [onesf, diag1], [psR1])
        psR2 = nps()
        S.op("pe", lambda e: e.matmul(psR2[:, :], lhsT=onesf[:, :], rhs=f2(diag2), start=True, stop=True),
             [onesf, diag2], [psR2])
        for h in range(4):
            S.op("dve", lambda e, h=h: e.scalar_tensor_tensor(
                out=arg1[:, h, :], in0=psR1[:, h * 128:(h + 1) * 128], scalar=nGc[:, h:h + 1], in1=MI, op0=ALU.add,
                op1=ALU.add), [psR1, nGc, cf], [arg1])
            S.op("dve", lambda e, h=h: e.scalar_tensor_tensor(
                out=arg2[:, h, :], in0=psR2[:, h * 128:(h + 1) * 128], scalar=nGc[:, h:h + 1], in1=MS, op0=ALU.add,
                op1=ALU.add), [psR2, nGc, cf], [arg2])
            S.op("dve", lambda e, h=h: e.scalar_tensor_tensor(
                out=arg3[:, h, :], in0=psR1[:, h * 128:(h + 1) * 128], scalar=-1.0, in1=MSL, op0=ALU.mult,
                op1=ALU.add), [psR1, cf], [arg3])
        S.op("act", lambda e: e.activation(out=f2(E1), in_=f2(arg1), func=AF.Exp), [arg1], [E1])
        S.op("act", lambda e: e.activation(out=f2(E2), in_=f2(arg2), func=AF.Exp), [arg2], [E2])
        for h in range(4):
            S.op("act", lambda e, h=h: e.activation(out=E3[:, h, :], in_=arg3[:, h, :], func=AF.Exp, bias=R2[:, h:h + 1]),
                 [arg3, R2], [E3])
        S.op("act", lambda e: e.activation(out=f2(EG), in_=psR1[:, :], func=AF.Exp), [psR1], [EG])
        S.op("pool", lambda e: e.tensor_tensor(out=qg[:, :, :], in0=qT_[:, :, cs], in1=EG[:, :, :], op=ALU.mult),
             [qT_, EG], [qg])
        yield
        psKK = nps()
        mm4(psKK, lambda h: kT_[:, h, cs], lambda h: kT_[:, h, cs], [kT_])
        psQK = nps()
        mm4(psQK, lambda h: kT_[:, h, cs], lambda h: qT_[:, h, cs], [kT_, qT_])
        S.op("dve", lambda e: e.tensor_tensor(out=f2(AT), in0=psKK[:, :], in1=f2(E2), op=ALU.mult), [psKK, E2], [AT])
        S.op("dve", lambda e: e.tensor_tensor(out=f2(An), in0=psKK[:, :], in1=f2(E3), op=ALU.mult), [psKK, E3], [An])
        S.op("dve", lambda e: e.tensor_tensor(out=f2(attnT), in0=psQK[:, :], in1=f2(E1), op=ALU.mult), [psQK, E1], [attnT])
        P, Pt = AT, An
        M, Mt = Mb[0], Mtb[0]
        S.op("dve", lambda e: e.tensor_tensor(out=f2(M), in0=f2(idb4), in1=f2r(AT), op=ALU.subtract), [idb4, AT], [M])
        S.op("dve", lambda e: e.tensor_tensor(out=f2(Mt), in0=f2(idb4), in1=f2r(An), op=ALU.subtract), [idb4, An], [Mt])
        S.op("pool", lambda e: e.tensor_tensor(out=f2(IpA), in0=f2(idb4), in1=f2r(An), op=ALU.add), [idb4, An], [IpA])
        yield
        for k in range(1, 7):
            lastk = (k == 6)
            Pn, Ptn = Pb[k % 2], Ptb[k % 2]
            Mn, Mtn = Mb[k % 2], Mtb[k % 2]
            psP = nps()
            mm4(psP, lambda h: Pt[:, h, :], lambda h: P[:, h, :], [P, Pt])
            S.op("act", lambda e: e.activation(out=f2(Pn), in_=psP[:, :], func=AF.Copy), [psP], [Pn])
            if not lastk:
                psPt = nps()
                mm4(psPt, lambda h: P[:, h, :], lambda h: Pt[:, h, :], [P, Pt])
                S.op("act", lambda e: e.activation(out=f2(Ptn), in_=psPt[:, :], func=AF.Copy), [psPt], [Ptn])
            yield
            psM = nps()
            mm4(psM, lambda h: Mt[:, h, :], lambda h: Pn[:, h, :], [Mt, Pn])
            S.op("dve", lambda e: e.tensor_tensor(out=f2(Mn), in0=psM[:, :], in1=f2r(M), op=ALU.add), [psM, M], [Mn])
            psMt = nps()
            mm4(psMt, lambda h: Pn[:, h, :], lambda h: Mt[:, h, :], [Pn, Mt])
            S.op("dve", lambda e: e.tensor_tensor(out=f2(Mtn), in0=psMt[:, :], in1=f2r(Mt), op=ALU.add),
                 [psMt, Mt], [Mtn])
            P, Pt, M, Mt = Pn, Ptn, Mn, Mtn
            yield
        psZ = nps()
        mm4(psZ, lambda h: IpA[:, h, :], lambda h: M[:, h, :], [IpA, M])
        S.op("dve", lambda e: e.tensor_tensor(out=f2(Rr), in0=f2(idb4), in1=psZ[:, :], op=ALU.subtract), [idb4, psZ], [Rr])
        yield
        psC = nps()
        mm4(psC, lambda h: Mt[:, h, :], lambda h: Rr[:, h, :], [Mt, Rr])
        S.op("dve", lambda e: e.tensor_tensor(out=f2(Xr), in0=psC[:, :], in1=f2r(M), op=ALU.add), [psC, M], [Xr])
        yield
        TT = Xr
        psU = nps()
        mm4(psU, lambda h: TT[:, h, :], lambda h: vb[:, h, :], [TT, vb])
        S.op("act", lambda e: e.activation(out=f2(u_sb), in_=psU[:, :], func=AF.Copy), [psU], [u_sb])
        psW = nps()
        mm4(psW, lambda h: kge[:, h, :], lambda h: TT[:, h, :], [TT, kge])
        S.op("act", lambda e: e.activation(out=f2(wT), in_=psW[:, :], func=AF.Copy), [psW], [wT])
        yield
        psWS = nps()
        mm4(psWS, lambda h: wT[:, h, :], lambda h: Sbf[:, h, :], [wT, Sbf])
        S.op("dve", lambda e: e.tensor_tensor(out=f2(vnew), in0=f2(u_sb), in1=psWS[:, :], op=ALU.subtract),
             [u_sb, psWS], [vnew])
        yield
        psO = nps()
        for h in range(4):
            S.op("pe", lambda e, h=h: e.matmul(psO[:, h * 128:(h + 1) * 128], lhsT=Sbf[:, h, :], rhs=qg[:, h, :],
                                               start=True, stop=False), [Sbf, qg], [psO])
            S.op("pe", lambda e, h=h: e.matmul(psO[:, h * 128:(h + 1) * 128], lhsT=vnew[:, h, :], rhs=attnT[:, h, :],
                                               start=False, stop=True), [vnew, attnT], [psO])
        psS = nps()
        mm4(psS, lambda h: kdec[:, h, :], lambda h: vnew[:, h, :], [kdec, vnew])
        for h in range(4):
            S.op("dve", lambda e, h=h: e.scalar_tensor_tensor(
                out=S32[:, h, :], in0=S32[:, h, :], scalar=eGl[:, h:h + 1], in1=psS[:, h * 128:(h + 1) * 128],
                op0=ALU.mult, op1=ALU.add), [S32, eGl, psS], [S32])
        S.op("act", lambda e: e.activation(out=f2(Sbf), in_=f2(S32), func=AF.Copy), [S32], [Sbf])
        S.op("act", lambda e: e.activation(out=f2(sqo), in_=psO[:, :], func=AF.Square), [psO], [sqo])
        yield
        psN = nps()
        S.op("pe", lambda e: e.matmul(psN[:, :], lhsT=onesb[:, :], rhs=f2(sqo), start=True, stop=True), [onesb, sqo], [psN])
        S.op("act", lambda e: e.activation(out=f2(rn), in_=psN[:, :], func=AF.Ln, scale=1.0 / 128.0, bias=EPS),
             [psN], [rn])
        S.op("act", lambda e: e.activation(out=f2(rn), in_=f2(rn), func=AF.Exp, scale=-0.5), [rn], [rn])
        S.op("dve", lambda e: e.scalar_tensor_tensor(out=f2(y1), in0=psO[:, :], scalar=wgn, in1=f2(rn), op0=ALU.mult,
                                                     op1=ALU.mult), [psO, prmall, rn], [y1])
        S.op("pool", lambda e: e.tensor_tensor(out=yo_[:, :, cs], in0=y1[:, :, :], in1=gz_[:, :, cs], op=ALU.mult),
             [y1, gz_], [yo_])
        if ci == 3:
            t0 = g * 512
            S.dma("sp", YT[256:768, t0:t0 + 512].rearrange("(h p) n -> p h n", p=128), yo_[:, :, :], reads=[yo_])
            if g + 2 < NG:
                load_group(g + 2)
        yield

    load_group(0)
    if NG > 1:
        load_group(1)
    NGRP = 21
    A, a_cnt = chunk(0), 0
    for _ in range(NGRP // 2):
        next(A)
        a_cnt += 1
    for n in range(1, NCH + 1):
        B, b_cnt = (chunk(n) if n < NCH else None), 0
        while a_cnt < NGRP:
            if B is not None and b_cnt < NGRP:
                next(B)
                b_cnt += 1
            next(A)
            a_cnt += 1
        A, a_cnt = B, b_cnt


def host_consts():
    c = np.zeros((128, NCST), np.float32)
    i = np.arange(128)
    c[:, C_ID:C_ID + 128] = np.eye(128, dtype=np.float32)
    c[:, C_UT:C_UT + 128] = (i[:, None] <= i[None, :])
    c[:, C_NTRI:C_NTRI + 128] = -1.0 * (i[:, None] > i[None, :])
    c[:, C_MI:C_MI + 128] = np.where(i[None, :] >= i[:, None], 0.0, NEG)
    c[:, C_MS:C_MS + 128] = np.where(i[None, :] > i[:, None], 0.0, NEG)
    c[:, C_MSL:C_MSL + 128] = np.where(i[None, :] < i[:, None], 0.0, NEG)
    q = np.arange(512)
    for kb in range(4):
        c[:, C_SBM + kb * 512:C_SBM + (kb + 1) * 512] = ((kb * 128 + i)[:, None] < q[None, :])
    return c


def host_params(inp, depth):
    p = np.zeros((depth, 128, NPRM), np.float32)
    for l in range(depth):
        p[l, :, 0:8] = np.asarray(inp["w_norm_mix"][l]).reshape(8, 128).T
        p[l, :, 8:16] = np.asarray(inp["w_norm_ffn"][l]).reshape(8, 128).T
        sc = np.asarray(inp["w_sconv"][l])
        for i in range(2):
            p[l, :, 16 + i * 3:19 + i * 3] = sc[:, i * 128:(i + 1) * 128].T
        gc = np.asarray(inp["w_gdn_conv"][l])
        for m in range(12):
            p[l, :, 22 + m * 4:26 + m * 4] = gc[:, m * 128:(m + 1) * 128].T
        p[l, :, 70] = np.asarray(inp["w_gdn_norm"][l])
        fc = np.asarray(inp["w_ffn_conv"][l])
        for m in range(44):
            p[l, :, 71 + m * 3:74 + m * 3] = fc[:, m * 128:(m + 1) * 128].T
        p[l, 0:4, 203] = 1.0
        p[l, 4:8, 203] = -1.0
        p[l, 0:4, 204] = np.asarray(inp["gdn_dt_bias"][l])
        p[l, 0:4, 205] = np.asarray(inp["gdn_a_log"][l])
    return p


_NC_CACHE = {}


def make_in_maps(inp, xs, depth):
    f = lambda a: np.ascontiguousarray(np.asarray(a, dtype=np.float32))
    common = dict(
        w_in=f(inp["w_mix_in"])[:depth], w_out=f(inp["w_mix_out"])[:depth], w_up=f(inp["w_ffn_up"])[:depth],
        w_dn=f(inp["w_ffn_down"])[:depth], prm=host_params(inp, depth),
        wfin=f(inp["w_norm_final"]).reshape(1, D), cst=host_consts())
    return [dict(common, x=f(xs[i])) for i in range(len(xs))]


def kernel(**inp):
    x = np.asarray(inp["x"], dtype=np.float32)
    B, L, _ = x.shape
    key = (L, DEPTH)
    if key not in _NC_CACHE:
        _NC_CACHE[key] = build(L)
    nc = _NC_CACHE[key]
    xs = [x[i % B] for i in range(8)]
    in_maps = make_in_maps(inp, xs, DEPTH)
    res = run_bass_kernel_spmd(nc, in_maps, core_ids=list(range(8)))
    return np.stack([res.results[i]["out"] for i in range(B)], axis=0).astype(np.float32)
```
